# Optimizing a Trainium2 kernel written in Bass

```python
import math
import jax, jax.numpy as jnp
from jax import lax
import numpy as np

D_MODEL = 2048
BATCH = 4
SEQ = 2048
DEPTH = 1
DEC_BATCH = 128
DEC_SEQ = 1
PAST_LEN = 16384
PAGE_SIZE = 128

MIX_WIDTH = D_MODEL
RET_HEADS = 4
RET_HEAD_DIM = (MIX_WIDTH // 2) // RET_HEADS
RET_WIDTH = RET_HEADS * RET_HEAD_DIM
MLSTM_HEADS = 4
MLSTM_HEAD_DIM = (MIX_WIDTH // 2) // MLSTM_HEADS
MLSTM_WIDTH = MLSTM_HEADS * MLSTM_HEAD_DIM
D_FF = 4 * D_MODEL
PLE_DIM = 256
CHUNK = 128
ROPE_BASE = 10000.0
LN_EPS = 1e-5
GN_EPS = 1e-6
DEEPNORM_ALPHA = (2 * DEPTH) ** 0.25
DEEPNORM_BETA = (8 * DEPTH) ** -0.25
IN_COLS = 4 * RET_WIDTH + 4 * MLSTM_WIDTH + 2 * MLSTM_HEADS
SPLITS = tuple(int(s) for s in np.cumsum([RET_WIDTH] * 4 + [MLSTM_WIDTH] * 4 + [MLSTM_HEADS]))

kernel_name = "hymba_retention_mlstm_deepnorm_step"

F32 = jnp.float32


def layer_norm(x, g, b):
    xf = x.astype(F32)
    mu = jnp.mean(xf, axis=-1, keepdims=True)
    var = jnp.mean(jnp.square(xf - mu), axis=-1, keepdims=True)
    y = (xf - mu) * lax.rsqrt(var + LN_EPS) * g.astype(F32) + b.astype(F32)
    return y.astype(x.dtype)


def head_norm(h, w):
    B, T, H, Dh = h.shape
    mu = jnp.mean(h, axis=-1, keepdims=True)
    var = jnp.mean(jnp.square(h - mu), axis=-1, keepdims=True)
    y = (h - mu) * lax.rsqrt(var + GN_EPS)
    return y.reshape(B, T, H * Dh) * w.astype(F32)


def rotary(x, pos):
    half = x.shape[-1] // 2
    inv = ROPE_BASE ** (-jnp.arange(half, dtype=F32) * 2.0 / x.shape[-1])
    ang = pos[:, None] * inv[None, :]
    cos = jnp.cos(ang)[None, :, None, :]
    sin = jnp.sin(ang)[None, :, None, :]
    x1, x2 = x[..., :half], x[..., half:]
    return jnp.concatenate([x1 * cos - x2 * sin, x1 * sin + x2 * cos], axis=-1)


def to_chunks(a, L):
    B, H, T = a.shape[:3]
    a = a.reshape((B, H, T // L, L) + a.shape[3:])
    return jnp.moveaxis(a, 2, 0)


def from_chunks(a):
    a = jnp.moveaxis(a, 0, 2)
    B, H, NC, L = a.shape[:4]
    return a.reshape((B, H, NC * L) + a.shape[4:])


def retention(q, k, v, S0):
    T = q.shape[2]
    L = math.gcd(T, CHUNK)
    lg = jnp.log(1.0 - 2.0 ** (-5.0 - jnp.arange(RET_HEADS, dtype=F32)))
    t = jnp.arange(L, dtype=F32)
    causal = t[:, None] >= t[None, :]
    intra = jnp.exp(jnp.where(causal, (t[:, None] - t[None, :]) * lg[:, None, None], -jnp.inf))
    q_decay = jnp.exp(lg[:, None] * (t + 1.0))[:, :, None]
    k_decay = jnp.exp(lg[:, None] * (L - 1.0 - t))[:, :, None]
    state_decay = jnp.exp(lg * L)[:, None, None]

    def step(S, qkv):
        qc, kc, vc = qkv
        sc = jnp.einsum('bhtd,bhsd->bhts', qc, kc) * intra
        o = jnp.einsum('bhts,bhsv->bhtv', sc, vc) + jnp.einsum('bhtd,bhdv->bhtv', qc, S) * q_decay
        S = S * state_decay + jnp.einsum('bhsd,bhsv->bhdv', kc * k_decay, vc)
        return S, o

    S, o = lax.scan(step, S0, (to_chunks(q, L), to_chunks(k, L), to_chunks(v, L)))
    return from_chunks(o), S


def mlstm(q, k, v, ig, lf, C0, n0, m0):
    T = q.shape[2]
    L = math.gcd(T, CHUNK)
    t = jnp.arange(L)
    causal = t[:, None] >= t[None, :]

    def step(carry, inp):
        C, n, m = carry
        qc, kc, vc, ic, fc = inp
        b = jnp.cumsum(fc, axis=-1)
        dlog = jnp.where(causal, b[..., :, None] - b[..., None, :] + ic[..., None, :], -jnp.inf)
        inter = b + m[..., None]
        mt = jnp.maximum(inter, jnp.max(dlog, axis=-1))
        dw = jnp.exp(dlog - mt[..., None])
        iw = jnp.exp(inter - mt)
        sc = jnp.einsum('bhtd,bhsd->bhts', qc, kc) * dw
        num = jnp.einsum('bhts,bhsv->bhtv', sc, vc) + iw[..., None] * jnp.einsum('bhtd,bhdv->bhtv', qc, C)
        den = jnp.sum(sc, axis=-1) + iw * jnp.einsum('bhtd,bhd->bht', qc, n)
        h = num / jnp.maximum(jnp.abs(den), jnp.exp(-mt))[..., None]
        m_new = mt[..., -1]
        sw = jnp.exp(b[..., -1:] - b + ic - m_new[..., None])
        sd = jnp.exp(b[..., -1] + m - m_new)
        C = sd[..., None, None] * C + jnp.einsum('bhs,bhsd,bhsv->bhdv', sw, kc, vc)
        n = sd[..., None] * n + jnp.einsum('bhs,bhsd->bhd', sw, kc)
        return (C, n, m_new), h

    (C, n, m), h = lax.scan(step, (C0, n0, m0),
                            (to_chunks(q, L), to_chunks(k, L), to_chunks(v, L),
                             to_chunks(ig, L), to_chunks(lf, L)))
    return from_chunks(h), C, n, m


def hybrid_layer(x, p, S_ret, C, n, m, pos0, w_in, b_gate, ret_gn_w, mlstm_gn_w, w_out,
                 ln1_g, ln1_b, w_ff1, w_ff2, w_pe, w_pe_gate, ln2_g, ln2_b):
    B, T, _ = x.shape
    proj = jnp.einsum('btd,de->bte', x, w_in).astype(F32)
    rq, rk, rv, rg, mq, mk, mv, mo, mi, mf = jnp.split(proj, SPLITS, axis=-1)
    pos = pos0 + jnp.arange(T, dtype=F32)

    rq = rotary(rq.reshape(B, T, RET_HEADS, RET_HEAD_DIM), pos)
    rk = rotary(rk.reshape(B, T, RET_HEADS, RET_HEAD_DIM), pos) * (RET_HEAD_DIM ** -0.5)
    rv = rv.reshape(B, T, RET_HEADS, RET_HEAD_DIM)
    ret_o, S_new = retention(rq.transpose(0, 2, 1, 3), rk.transpose(0, 2, 1, 3),
                             rv.transpose(0, 2, 1, 3), S_ret.astype(F32))
    ret_o = head_norm(ret_o.transpose(0, 2, 1, 3), ret_gn_w) * jax.nn.silu(rg)

    bg = b_gate.astype(F32)
    mq = mq.reshape(B, T, MLSTM_HEADS, MLSTM_HEAD_DIM).transpose(0, 2, 1, 3)
    mk = (mk * (MLSTM_HEAD_DIM ** -0.5)).reshape(B, T, MLSTM_HEADS, MLSTM_HEAD_DIM).transpose(0, 2, 1, 3)
    mv = mv.reshape(B, T, MLSTM_HEADS, MLSTM_HEAD_DIM).transpose(0, 2, 1, 3)
    ig = (mi + bg[:MLSTM_HEADS]).transpose(0, 2, 1)
    lf = jax.nn.log_sigmoid(mf + bg[MLSTM_HEADS:]).transpose(0, 2, 1)
    h, C_new, n_new, m_new = mlstm(mq, mk, mv, ig, lf, C.astype(F32), n.astype(F32), m.astype(F32))
    m_o = head_norm(h.transpose(0, 2, 1, 3), mlstm_gn_w) * jax.nn.sigmoid(mo)

    mix = jnp.einsum('bte,ed->btd', jnp.concatenate([ret_o, m_o], axis=-1).astype(x.dtype), w_out)
    x1 = layer_norm(DEEPNORM_ALPHA * x + mix, ln1_g, ln1_b)

    ff = jnp.einsum('btf,fd->btd', jnp.square(jax.nn.relu(jnp.einsum('btd,df->btf', x1, w_ff1))), w_ff2)
    pe = jnp.einsum('btp,pd->btd', p, w_pe) * jax.nn.sigmoid(jnp.einsum('btd,de->bte', x1, w_pe_gate))
    x2 = layer_norm(DEEPNORM_ALPHA * x1 + ff + pe, ln2_g, ln2_b)
    return x2, S_new, C_new, n_new, m_new


def setup_inputs(seed: int = 0) -> dict:
    key = jax.random.key(seed)
    ks = jax.random.split(key, 24)
    nrm = jax.random.normal
    H, Dh = MLSTM_HEADS, MLSTM_HEAD_DIM
    f_bias = jnp.broadcast_to(jnp.linspace(3.0, 6.0, H, dtype=F32), (DEPTH, H)) + 0.1 * nrm(ks[20], (DEPTH, H))
    i_bias = 0.1 * nrm(ks[21], (DEPTH, H))
    return {
        'x_prompt': nrm(ks[0], (BATCH, SEQ, D_MODEL), F32),
        'x_sample': nrm(ks[1], (DEC_BATCH, DEC_SEQ, D_MODEL), F32),
        'state_ret': 0.5 * nrm(ks[2], (DEPTH, DEC_BATCH, RET_HEADS, RET_HEAD_DIM, RET_HEAD_DIM), F32),
        'state_mlstm_C': 0.5 * nrm(ks[3], (DEPTH, DEC_BATCH, H, Dh, Dh), F32),
        'state_mlstm_n': 0.5 * nrm(ks[4], (DEPTH, DEC_BATCH, H, Dh), F32),
        'state_mlstm_m': jax.random.uniform(ks[5], (DEPTH, DEC_BATCH, H), F32, 0.0, 3.0),
        'p_prompt': nrm(ks[6], (DEPTH, BATCH, SEQ, PLE_DIM), F32),
        'p_sample': nrm(ks[7], (DEPTH, DEC_BATCH, DEC_SEQ, PLE_DIM), F32),
        'w_in': nrm(ks[8], (DEPTH, D_MODEL, IN_COLS), F32) * D_MODEL ** -0.5,
        'b_gate': jnp.concatenate([i_bias, f_bias], axis=-1),
        'ret_gn_w': 1.0 + 0.02 * nrm(ks[9], (DEPTH, RET_WIDTH), F32),
        'mlstm_gn_w': 1.0 + 0.02 * nrm(ks[10], (DEPTH, MLSTM_WIDTH), F32),
        'w_out': nrm(ks[11], (DEPTH, MIX_WIDTH, D_MODEL), F32) * MIX_WIDTH ** -0.5 * DEEPNORM_BETA,
        'ln1_g': 1.0 + 0.02 * nrm(ks[12], (DEPTH, D_MODEL), F32),
        'ln1_b': 0.02 * nrm(ks[13], (DEPTH, D_MODEL), F32),
        'w_ff1': nrm(ks[14], (DEPTH, D_MODEL, D_FF), F32) * D_MODEL ** -0.5,
        'w_ff2': nrm(ks[15], (DEPTH, D_FF, D_MODEL), F32) * D_FF ** -0.5 * DEEPNORM_BETA,
        'w_pe': nrm(ks[16], (DEPTH, PLE_DIM, D_MODEL), F32) * PLE_DIM ** -0.5 * DEEPNORM_BETA,
        'w_pe_gate': nrm(ks[17], (DEPTH, D_MODEL, D_MODEL), F32) * D_MODEL ** -0.5,
        'ln2_g': 1.0 + 0.02 * nrm(ks[18], (DEPTH, D_MODEL), F32),
        'ln2_b': 0.02 * nrm(ks[19], (DEPTH, D_MODEL), F32),
    }


def reference(x_prompt, x_sample, state_ret, state_mlstm_C, state_mlstm_n, state_mlstm_m,
              p_prompt, p_sample, w_in, b_gate, ret_gn_w, mlstm_gn_w, w_out, ln1_g, ln1_b,
              w_ff1, w_ff2, w_pe, w_pe_gate, ln2_g, ln2_b):
    Bp = x_prompt.shape[0]
    yp, ys = x_prompt, x_sample
    rS_p, C_p, n_p, m_p = [], [], [], []
    rS_s, C_s, n_s, m_s = [], [], [], []
    for i in range(DEPTH):
        w = (w_in[i], b_gate[i], ret_gn_w[i], mlstm_gn_w[i], w_out[i], ln1_g[i], ln1_b[i],
             w_ff1[i], w_ff2[i], w_pe[i], w_pe_gate[i], ln2_g[i], ln2_b[i])
        S0 = jnp.zeros((Bp, RET_HEADS, RET_HEAD_DIM, RET_HEAD_DIM), F32)
        C0 = jnp.zeros((Bp, MLSTM_HEADS, MLSTM_HEAD_DIM, MLSTM_HEAD_DIM), F32)
        n0 = jnp.zeros((Bp, MLSTM_HEADS, MLSTM_HEAD_DIM), F32)
        m0 = jnp.zeros((Bp, MLSTM_HEADS), F32)
        yp, a, b, c, d = hybrid_layer(yp, p_prompt[i], S0, C0, n0, m0, 0, *w)
        rS_p.append(a); C_p.append(b); n_p.append(c); m_p.append(d)
        ys, a, b, c, d = hybrid_layer(ys, p_sample[i], state_ret[i], state_mlstm_C[i],
                                      state_mlstm_n[i], state_mlstm_m[i], PAST_LEN, *w)
        rS_s.append(a); C_s.append(b); n_s.append(c); m_s.append(d)
    return (yp, ys, jnp.stack(rS_p), jnp.stack(C_p), jnp.stack(n_p), jnp.stack(m_p),
            jnp.stack(rS_s), jnp.stack(C_s), jnp.stack(n_s), jnp.stack(m_s))
```

```python
import numpy as np
import concourse.bass as bass
import concourse.mybir as mybir
from concourse.bass_utils import run_bass_kernel_spmd

F32 = mybir.dt.float32
BF16 = mybir.dt.bfloat16
AF = mybir.ActivationFunctionType
ALU = mybir.AluOpType
AX = mybir.AxisListType

ENGS = ("pe", "act", "dve", "pool", "sp")
NCORES = 8
D = 2048
TOK = 1040
ALPHA = 2.0 ** 0.25
GN_EPS = 1e-6
LN_EPS = 1e-5
G = [1.0 - 2.0 ** (-5.0 - h) for h in range(4)]


class Prog:
    DEF_COST = dict(pe=0.3, act=0.45, dve=0.45, pool=0.8, sp=0.3)

    def __init__(self, nc):
        self.nc = nc
        self.ops = []
        self.last_w = {}
        self.readers = {}
        self.dma_sem_count = {}
        self.bar = None
        self.bar_start = 0

    def _add(self, eng, fn, r, w, dma_sem=None, cost=None):
        i = len(self.ops)
        deps = set()
        if self.bar is not None:
            deps.add(self.bar)
        for k in list(r) + list(w):
            if k in self.last_w:
                deps.add(self.last_w[k])
        for k in w:
            for x in self.readers.get(k, ()):
                deps.add(x)
        deps.discard(i)
        for k in w:
            self.last_w[k] = i
            self.readers[k] = []
        for k in r:
            self.readers.setdefault(k, []).append(i)
        odeps = set(deps)
        if eng == "pe":
            deps = {d for d in deps if self.ops[d]["eng"] != "pe" or self.ops[d]["dma_sem"] is not None}
        if cost is None:
            cost = 2.5 if dma_sem is not None else self.DEF_COST[eng]
        op = dict(id=i, eng=eng, fn=fn, deps=deps, odeps=odeps, dma_sem=dma_sem, has_dep=False, has_any=False, cost=cost)
        if dma_sem is not None:
            c = self.dma_sem_count.get(dma_sem, 0) + 1
            self.dma_sem_count[dma_sem] = c
            op["target"] = 16 * c
        for d in deps:
            self.ops[d]["has_dep"] = True
        for d in odeps:
            self.ops[d]["has_any"] = True
        self.ops.append(op)
        return i

    def op(self, eng, fn, r=(), w=(), cost=None):
        return self._add(eng, fn, r, w, cost=cost)

    def dma(self, eng, out, in_, r=(), w=(), sem=None, cost=None):
        return self._add(eng, lambda e: e.dma_start(out=out, in_=in_), r, w, dma_sem=sem, cost=cost)

    def barrier(self, fn):
        i = len(self.ops)
        deps = {o["id"] for o in self.ops[self.bar_start:] if not o["has_any"]}
        if self.bar is not None:
            deps.add(self.bar)
        op = dict(id=i, eng="dve", fn=fn, deps=deps, odeps=set(deps), dma_sem=None, has_dep=True, has_any=True, cost=0.2)
        for d in deps:
            self.ops[d]["has_dep"] = True
            self.ops[d]["has_any"] = True
        self.ops.append(op)
        self.bar = i
        self.bar_start = i
        self.last_w = {}
        self.readers = {}
        return i

    def schedule(self):
        import heapq
        ops = self.ops
        n = len(ops)
        succ = [[] for _ in range(n)]
        indeg = [0] * n
        for o in ops:
            indeg[o["id"]] = len(o["odeps"])
            for d in o["odeps"]:
                succ[d].append(o["id"])
        ready = {e: [] for e in ENGS}
        rtime = [0.0] * n
        finish = [0.0] * n
        free = {e: 0.0 for e in ENGS}
        order = {e: [] for e in ENGS}
        for o in ops:
            if indeg[o["id"]] == 0:
                heapq.heappush(ready[o["eng"]], (0.0, o["id"]))
        done = 0
        while done < n:
            best = None
            for e in ENGS:
                h = ready[e]
                if not h:
                    continue
                t_free = free[e]
                cand = None
                avail = [x for x in h if x[0] <= t_free]
                if avail:
                    cid = min(x[1] for x in avail)
                    cand = (t_free, cid)
                else:
                    rt, cid = min(h)
                    cand = (rt, cid)
                if best is None or cand < best[0]:
                    best = (cand, e)
            (start, i), e = best
            h = ready[e]
            for k, x in enumerate(h):
                if x[1] == i:
                    h[k] = h[-1]
                    h.pop()
                    break
            heapq.heapify(h)
            o = ops[i]
            if o["dma_sem"] is not None:
                free[e] = start + 0.35
                finish[i] = start + o["cost"]
            else:
                free[e] = start + o["cost"]
                finish[i] = free[e]
            order[e].append(i)
            done += 1
            for sidx in succ[i]:
                so = ops[sidx]
                lat = 0.2 if (so["eng"] == e and o["dma_sem"] is None) else 1.2
                rtime[sidx] = max(rtime[sidx], finish[i] + lat)
                indeg[sidx] -= 1
                if indeg[sidx] == 0:
                    heapq.heappush(ready[so["eng"]], (rtime[sidx], sidx))
        self.sim_time = max(finish)
        return order

    def emit(self, reorder=True):
        nc = self.nc
        esem = {e: nc.alloc_semaphore("s_" + e) for e in ENGS}
        dsem = {k: nc.alloc_semaphore("d_%d" % j) for j, k in enumerate(self.dma_sem_count)}
        ops = self.ops
        if reorder:
            order = self.schedule()
        else:
            order = {e: [o["id"] for o in ops if o["eng"] == e] for e in ENGS}
        cnt = {e: 0 for e in ENGS}
        for e in ENGS:
            for i in order[e]:
                op = ops[i]
                if op["dma_sem"] is None and op["has_dep"]:
                    cnt[e] += 1
                    op["ms"] = cnt[e]
        stats = dict(waits=0, ops=len(ops))

        def run(eng_name, e):
            seen = {}
            for i in order[eng_name]:
                op = ops[i]
                need = {}
                for d in op["deps"]:
                    p = ops[d]
                    if p["dma_sem"] is not None:
                        key, val = ("d", p["dma_sem"]), p["target"]
                    else:
                        key, val = ("e", p["eng"]), p["ms"]
                    if val > need.get(key, 0):
                        need[key] = val
                for key, val in need.items():
                    if seen.get(key, 0) >= val:
                        continue
                    sm = dsem[key[1]] if key[0] == "d" else esem[key[1]]
                    e.wait_ge(sm, val)
                    stats["waits"] += 1
                    seen[key] = val
                ins = op["fn"](e)
                if op["dma_sem"] is not None:
                    ins.then_inc(dsem[op["dma_sem"]], 16)
                elif op["has_dep"]:
                    ins.then_inc(esem[op["eng"]], 1)
            if eng_name == "sp":
                for k, c in self.dma_sem_count.items():
                    if seen.get(("d", k), 0) < 16 * c:
                        e.wait_ge(dsem[k], 16 * c)

        with nc.Block() as block:
            @block.tensor
            def _(e):
                run("pe", e)

            @block.scalar
            def _(e):
                run("act", e)

            @block.vector
            def _(e):
                run("dve", e)

            @block.gpsimd
            def _(e):
                run("pool", e)

            @block.sync
            def _(e):
                run("sp", e)
        self.stats = stats


class Arena:
    def __init__(self, nc, nbytes):
        self.t = nc.alloc_sbuf_tensor("arena", [128, nbytes // 2], BF16)
        self.off = 0
        self.cap = nbytes
        self.peak = 0

    def alloc(self, shape, dtype):
        n = 1
        for s in shape[1:]:
            n *= s
        esz = 4 if dtype == F32 else 2
        nb = (n * esz + 31) // 32 * 32
        o = self.off
        self.off += nb
        self.peak = max(self.peak, self.off)
        assert self.off <= self.cap, ("SBUF arena overflow", self.off, self.cap)
        v = self.t[0:shape[0], o // 2:o // 2 + n * esz // 2]
        if dtype == F32:
            v = v.bitcast(F32)
        if len(shape) == 3:
            v = v.rearrange("p (a b) -> p a b", a=shape[1])
        elif len(shape) == 4:
            v = v.rearrange("p (a b c) -> p a b c", a=shape[1], b=shape[2])
        return v


def build_program(dbg=False):
    nc = bass.Bass("TRN2", target_bir_lowering=False)

    def din(name, shape):
        return nc.dram_tensor(name, list(shape), F32, kind="ExternalInput").ap()

    def dout(name, shape):
        return nc.dram_tensor(name, list(shape), F32, kind="ExternalOutput").ap()

    x_own = din("x_own", [1024, D]); x_pre = din("x_pre", [1024, D]); x_smp = din("x_smp", [16, D])
    p_own = din("p_own", [1024, 256]); p_smp = din("p_smp", [16, 256])
    st_ret = din("st_ret", [16, 4, 256, 256]); st_C = din("st_C", [16, 4, 256, 256])
    st_n = din("st_n", [16, 1024]); st_m = din("st_m", [16, 4])
    w_in = din("w_in", [D, 8200]); w_out = din("w_out", [D, D]); w_ff1 = din("w_ff1", [D, 8192])
    w_ff2 = din("w_ff2", [8192, D]); w_pe = din("w_pe", [256, D]); w_pg = din("w_pe_gate", [D, D])
    ln_d = din("ln_all", [4, D])
    bg_d = din("b_gate", [1, 8])
    gnw_d = din("gnw", [128, 16])
    c_sq = din("c_sq", [128, 6, 128])
    c_rmask = din("c_rmask", [128, 4, 128])
    c_small = din("c_small", [128, 48])
    c_id16 = din("c_id16", [128, 256])
    cs_own_d = din("cs_own", [128, 2, TOK]); cs_pre_d = din("cs_pre", [128, 2, 1024])

    y_own = dout("y_own", [1024, D]); y_smp = dout("y_smp", [16, D])
    o_retS_p = dout("retS_p", [4, 256, 256]); o_C_p = dout("C_p", [4, 256, 256])
    o_n_p = dout("n_p", [4, 256]); o_m_p = dout("m_p", [1, 4])
    o_retS_s = dout("retS_s", [16, 4, 256, 256]); o_C_s = dout("C_s", [16, 4, 256, 256])
    o_n_s = dout("n_s", [16, 1024]); o_m_s = dout("m_s", [16, 4])

    P = Prog(nc)
    AR = Arena(nc, 206 * 1024)
    PS = [nc.alloc_psum_tensor("ps%d" % i, [128, 512], F32) for i in range(5)]
    PSX = nc.alloc_psum_tensor("psx", [128, 512], F32)
    PB = [nc.alloc_psum_tensor("pb%d" % i, [128, 1024], BF16) for i in range(2)]
    st = dict(ps=0, pb=0, ev=0, wr=0)

    def bank():
        i = st["ps"]; st["ps"] = (i + 1) % 5
        return PS[i], "PS%d" % i

    def bbank():
        i = st["pb"]; st["pb"] = (i + 1) % 2
        return PB[i], "PB%d" % i

    def ev_eng():
        st["ev"] ^= 1
        return "dve" if st["ev"] else "act"

    def copy_op(eng, out, in_, r, w, scale=None):
        if eng == "act":
            if scale is None:
                P.op("act", lambda e: e.copy(out, in_), r=r, w=w)
            else:
                P.op("act", lambda e: e.mul(out, in_, scale), r=r, w=w)
        else:
            if scale is None:
                P.op(eng, lambda e: e.tensor_copy(out, in_), r=r, w=w)
            else:
                P.op(eng, lambda e: e.tensor_scalar(out, in_, scale, None, ALU.mult), r=r, w=w)

    SQ = AR.alloc([128, 6, 128], F32)
    IDF, TRI, NEGM, SEL127, ONES = (SQ[:, i, :] for i in range(5))
    IDB = AR.alloc([128, 128], BF16)
    RMASK = AR.alloc([128, 4, 128], F32)
    SM = AR.alloc([128, 48], F32)
    QD, QD2, KDEC = SM[:, 0:4], SM[:, 4:8], SM[:, 8:12]
    KPRE = SM[:, 12:44].rearrange("p (c h) -> p c h", c=8)
    FLAG = SM[:, 44:45]
    EPSG = SM[:, 45:46]
    EPSL = SM[:, 46:47]
    ID16 = AR.alloc([128, 16, 16], BF16)
    BG = AR.alloc([128, 8], F32)
    GNW = AR.alloc([128, 16], F32)
    NSM = AR.alloc([16, 1024], F32)
    NNEW = AR.alloc([16, 1024], F32)
    MS0 = AR.alloc([16, 4], F32)
    BARS = AR.alloc([128, 8], F32)

    cl = lambda out, in_, q="sp": P.dma(q, out, in_, w=["C"], sem=("C" if q == "sp" else "Cp"))
    cl(SQ, c_sq); cl(RMASK, c_rmask); cl(SM, c_small); cl(GNW, gnw_d)
    cl(BG, bass.AP(bg_d.tensor, 0, [[0, 128], [1, 8]]))
    cl(NSM, st_n); cl(MS0, st_m)
    cl(ID16.rearrange("p a b -> p (a b)"), c_id16, "pool")
    cl(IDB, c_sq[:, 0, :], "pool")
    P.barrier(lambda e: e.memset(BARS[:, 0:1], 0.0))

    mixt_off = AR.off
    MIXT = AR.alloc([128, 16, TOK], BF16)
    markA = AR.off
    XT = AR.alloc([128, 16, TOK], BF16)
    WR = [AR.alloc([128, 16, 256], BF16) for _ in range(4)]
    ovl = AR.off
    XIN = [AR.alloc([128, D], BF16) for _ in range(2)]
    AR.off = ovl
    ROT = [AR.alloc([128, 512], F32) for _ in range(4)]
    CS = AR.alloc([128, 2, TOK], F32)
    AR.off = max(AR.off, ovl + 2 * D * 4)
    QT = AR.alloc([128, 2, TOK], BF16)
    KT = AR.alloc([128, 2, TOK], BF16)
    KTOK = AR.alloc([128, 8, 256], BF16)
    VEXT = AR.alloc([128, 9, 260], BF16)
    GATE = AR.alloc([128, 9, 256], BF16)
    SINIT = AR.alloc([128, 8, 2, 260], F32)
    SF = [AR.alloc([128, 2, 260], F32) for _ in range(2)]
    SBF = [AR.alloc([128, 2, 260], BF16) for _ in range(2)]
    NSLOT = 3
    SST = [AR.alloc([128, 2, 256], F32) for _ in range(NSLOT)]
    SNB = [AR.alloc([128, 2, 256], BF16) for _ in range(NSLOT)]
    WG = AR.alloc([128, 16, 8], BF16)
    GP = AR.alloc([128, 9, 8], F32)
    IG = AR.alloc([128, 9, 4], F32)
    SP = AR.alloc([128, 9, 4], F32)
    CSA = AR.alloc([128, 2, 8, 4], F32)
    AA = AR.alloc([128, 8, 4], F32)
    LST = AR.alloc([128, 2, 8, 4], F32)
    MCH = AR.alloc([128, 9, 4], F32)
    MML = AR.alloc([128, 8, 4], F32)
    ECH = AR.alloc([128, 9, 4], F32)
    T32 = AR.alloc([128, 8, 4], F32)
    SWT = AR.alloc([128, 8, 4], F32)
    MM = AR.alloc([128, 8, 4], F32)
    NEGMM = AR.alloc([128, 8, 4], F32)
    IW = AR.alloc([128, 8, 4], F32)
    EMT = AR.alloc([128, 8, 4], F32)
    SD = AR.alloc([128, 8, 4], F32)
    MINIT = AR.alloc([128, 4], F32)
    TMPA = AR.alloc([128, 4, 128], F32)
    NSET = 3
    SETS = []
    for i in range(NSET):
        SETS.append(dict(id=i, DW=AR.alloc([128, 128], F32), SCM=AR.alloc([128, 128], BF16), SCMT=AR.alloc([128, 128], BF16),
                         INTRA=AR.alloc([128, 260], F32), NUMER=AR.alloc([128, 256], F32), YF=AR.alloc([128, 256], F32),
                         Y2=AR.alloc([128, 256], BF16), BNS=AR.alloc([128, 6], F32), BNA=AR.alloc([128, 2], F32), SC1=AR.alloc([128, 8], F32)))
    _save = AR.off
    AR.off = ovl
    for i in range(NSET, NSET + 3):
        SETS.append(dict(id=i, DW=AR.alloc([128, 128], F32), SCM=AR.alloc([128, 128], BF16), SCMT=AR.alloc([128, 128], BF16),
                         INTRA=AR.alloc([128, 260], F32), NUMER=AR.alloc([128, 256], F32), YF=AR.alloc([128, 256], F32),
                         Y2=AR.alloc([128, 256], BF16), BNS=AR.alloc([128, 6], F32), BNA=AR.alloc([128, 2], F32), SC1=AR.alloc([128, 8], F32)))
    assert AR.off <= ovl + 4 * 2048 + 2 * TOK * 4, (AR.off - ovl)
    AR.off = _save
    ALIAS = ["R0", "R1", "R2", "R3", "CS"]
    QM = AR.alloc([128, 2, 16, 16], BF16)
    KMB = [AR.alloc([16, 256], BF16) for _ in range(NSLOT)]
    KS = AR.alloc([16, 256], F32)
    KSD = AR.alloc([16, 256], F32)
    QS = AR.alloc([16, 256], F32)
    SMP = AR.alloc([16, 32], F32)
    DG = AR.alloc([16, 16, 4], F32)
    IWB = AR.alloc([128, 16, 4], F32)
    TQ = AR.alloc([16, 256], F32)

    w_in_v = w_in.rearrange("(dt p) c -> p dt c", p=128)

    def load_w(c0, ncols=256):
        s = st["wr"]; st["wr"] = (s + 1) % 4
        P.dma("pool", WR[s][:, :, 0:ncols], w_in_v[:, :, c0:c0 + ncols], w=["WR%d" % s], sem="WR%d" % s, cost=9.0)
        return WR[s], "WR%d" % s

    def xt_keys(t0, n):
        tiles = range(t0 // 128, (t0 + n - 1) // 128 + 1)
        return ["XT%d_%d" % (tt, g) for tt in tiles for g in range(4)]

    def build_xt(src, ntiles, smp_src=None):
        jobs = [(src[tt * 128:(tt + 1) * 128, :], 128, tt) for tt in range(ntiles)]
        if smp_src is not None:
            jobs.append((smp_src, 16, 8))
        for i, (sap, np_, tt) in enumerate(jobs):
            xin = XIN[i % 2]; xk = "XIN%d" % (i % 2)
            P.dma("pool", xin[0:np_, :], sap, w=[xk], sem=xk, cost=5.5)
            for g in range(4):
                ps, pk = bank()
                psb = ps[:].bitcast(BF16)

                def tr(e, psb=psb, xin=xin, g=g, np_=np_):
                    for q in range(4):
                        dt = g * 4 + q
                        ins = e.transpose(psb[:, q * 128:q * 128 + np_], xin[0:np_, dt * 128:(dt + 1) * 128], IDB[0:np_, 0:np_])
                    return ins
                P.op("pe", tr, r=[xk], w=[pk])
                out = XT[:, g * 4:(g + 1) * 4, tt * 128:tt * 128 + np_]
                inn = psb[:, 0:512].rearrange("p (q t) -> p q t", q=4)[:, :, 0:np_]
                copy_op(ev_eng(), out, inn, r=[pk], w=["XT%d_%d" % (tt, g)])

    def proj_fm(W, wk, ct, t0, n):
        ps, pk = bank()

        def mm(e):
            for dt in range(16):
                ins = e.matmul(ps[:, 0:n], W[:, dt, ct * 128:(ct + 1) * 128], XT[:, dt, t0:t0 + n],
                               start=(dt == 0), stop=(dt == 15))
            return ins
        P.op("pe", mm, r=[wk] + xt_keys(t0, n), w=[pk], cost=0.3 + 16 * n / 2400.0)
        return ps, pk

    def rotary(W, wk, DST, dkey, t0, n):
        a, ak = proj_fm(W, wk, 0, t0, n)
        b, bk = proj_fm(W, wk, 1, t0, n)
        cos, sin = CS[:, 0, t0:t0 + n], CS[:, 1, t0:t0 + n]
        r0, r1, r2, r3 = (ROT[i][:, 0:n] for i in range(4))
        P.op("dve", lambda e: e.tensor_tensor(r0, a[:, 0:n], cos, ALU.mult), r=[ak, "CS"], w=["R0"])
        P.op("dve", lambda e: e.tensor_tensor(r1, b[:, 0:n], sin, ALU.mult), r=[bk, "CS"], w=["R1"])
        P.op("dve", lambda e: e.tensor_tensor(r2, a[:, 0:n], sin, ALU.mult), r=[ak, "CS"], w=["R2"])
        P.op("dve", lambda e: e.tensor_tensor(r3, b[:, 0:n], cos, ALU.mult), r=[bk, "CS"], w=["R3"])
        P.op("pool", lambda e: e.tensor_tensor(DST[:, 0, t0:t0 + n], r0, r1, ALU.subtract), r=["R0", "R1"], w=[dkey + "0_%d" % t0])
        P.op("pool", lambda e: e.tensor_tensor(DST[:, 1, t0:t0 + n], r2, r3, ALU.add), r=["R2", "R3"], w=[dkey + "1_%d" % t0])

    def plain_fm(W, wk, DST, dkey, t0, n, scale):
        for ct in range(2):
            a, ak = proj_fm(W, wk, ct, t0, n)
            copy_op(ev_eng(), DST[:, ct, t0:t0 + n], a[:, 0:n], r=[ak], w=[dkey + "%d_%d" % (ct, t0)], scale=scale)

    def blk_keys(dkey, t0, n):
        out = []
        for b0 in (0, 512, 1024):
            bn = 512 if b0 < 1024 else 16
            if t0 < b0 + bn and t0 + n > b0:
                out += [dkey + "0_%d" % b0, dkey + "1_%d" % b0]
        return out

    def proj_tm(W, wk, tt, np_, ncols=256):
        ps, pk = bank()

        def mm(e):
            for dt in range(16):
                ins = e.matmul(ps[0:np_, 0:ncols], XT[:, dt, tt * 128:tt * 128 + np_], W[:, dt, 0:ncols],
                               start=(dt == 0), stop=(dt == 15))
            return ins
        P.op("pe", mm, r=[wk] + xt_keys(tt * 128, np_), w=[pk], cost=0.3 + 16 * ncols / 2400.0)
        return ps, pk

    def make_ktok(c, scale_ap):
        pb, pk = bbank()

        def tr(e):
            for j in range(2):
                ins = e.transpose(pb[:, j * 128:(j + 1) * 128], KT[:, j, c * 128:(c + 1) * 128], IDB)
            return ins
        P.op("pe", tr, r=blk_keys("KT", c * 128, 128), w=[pk])
        copy_op(ev_eng(), KTOK[:, c, :], pb[:, 0:256], r=[pk, "SWT", "C"], w=["KTOK%d" % c], scale=scale_ap)

    def gates(ntiles_full, with_sample):
        ps, pk = bank()
        tiles = [(tt, 128) for tt in range(ntiles_full)] + ([(8, 16)] if with_sample else [])

        def mm(e):
            for tt, np_ in tiles:
                for dt in range(16):
                    ins = e.matmul(ps[0:np_, tt * 8:(tt + 1) * 8], XT[:, dt, tt * 128:tt * 128 + np_], WG[:, dt, :],
                                   start=(dt == 0), stop=(dt == 15))
            return ins
        P.op("pe", mm, r=["WG"] + xt_keys(0, TOK if with_sample else 1024), w=[pk])
        nt = len(tiles)
        for tt, np_ in tiles:
            pass
        full = ps[:, 0:8 * 8].rearrange("p (t g) -> p t g", t=8)
        P.op("dve", lambda e: e.tensor_tensor(IG[:, 0:8, :], full[:, :, 0:4], BG[:, 0:4].unsqueeze(1).to_broadcast([128, 8, 4]), ALU.add), r=[pk], w=["IG"])
        P.op("dve", lambda e: e.tensor_tensor(SP[:, 0:8, :], full[:, :, 4:8], BG[:, 4:8].unsqueeze(1).to_broadcast([128, 8, 4]), ALU.add), r=[pk], w=["SP"])
        if with_sample:
            P.op("dve", lambda e: e.tensor_tensor(IG[0:16, 8, :], ps[0:16, 64:68], BG[0:16, 0:4], ALU.add), r=[pk], w=["IGs"])
            P.op("dve", lambda e: e.tensor_tensor(SP[0:16, 8, :], ps[0:16, 68:72], BG[0:16, 4:8], ALU.add), r=[pk], w=["SPs"])
            P.op("act", lambda e: e.activation(SP[0:16, 8, :], SP[0:16, 8, :], AF.Exp, scale=-1.0), r=["SPs"], w=["SPs"])
            P.op("act", lambda e: e.activation(SP[0:16, 8, :], SP[0:16, 8, :], AF.Ln, bias=1.0), r=["SPs"], w=["SPs"])
        P.op("act", lambda e: e.activation(SP[:, 0:8, :], SP[:, 0:8, :], AF.Exp, scale=-1.0), r=["SP"], w=["SP"])
        P.op("act", lambda e: e.activation(SP[:, 0:8, :], SP[:, 0:8, :], AF.Ln, bias=1.0), r=["SP"], w=["SP"])
        ps2, pk2 = bank()
        P.op("pe", lambda e: e.matmul(ps2[:, 0:32], TRI, SP[:, 0:8, :].rearrange("p c h -> p (c h)"), start=True, stop=True), r=["SP", "C"], w=[pk2])
        cs3 = ps2[:, 0:32].rearrange("p (c h) -> p c h", c=8)
        P.op("dve", lambda e: e.tensor_copy(CSA[:, 0, :, :], cs3), r=[pk2], w=["CSA0"])
        P.op("dve", lambda e: e.tensor_tensor(AA[:], IG[:, 0:8, :], CSA[:, 0, :, :], ALU.add), r=["IG", "CSA0"], w=["AA"])
        for c in range(8):
            psa, pka = bank()

            def mm2(e, psa=psa, c=c):
                for h in range(4):
                    ins = e.matmul(psa[:, h * 128:(h + 1) * 128], AA[:, c, h:h + 1].to_broadcast([128, 128]), IDF, start=True, stop=True)
                return ins
            P.op("pe", mm2, r=["AA", "C"], w=[pka])
            P.op("dve", lambda e, psa=psa: e.tensor_tensor(TMPA[:], psa[:].rearrange("p (h s) -> p h s", h=4),
                                                           NEGM.unsqueeze(1).to_broadcast([128, 4, 128]), ALU.add), r=[pka, "C"], w=["TMPA"])
            P.op("dve", lambda e, c=c: e.tensor_reduce(CSA[:, 1, c, :], TMPA[:], AX.X, ALU.max), r=["TMPA"], w=["CSA1_%d" % c])
        ps3, pk3 = bank()
        P.op("pe", lambda e: e.matmul(ps3[:, 0:64], SEL127, CSA[:].rearrange("p a c h -> p (a c h)"), start=True, stop=True),
             r=["CSA0", "C"] + ["CSA1_%d" % c for c in range(8)], w=[pk3])
        P.op("dve", lambda e: e.tensor_copy(LST[:].rearrange("p a c h -> p (a c h)"), ps3[:, 0:64]), r=[pk3], w=["LST"])

    def m_chain():
        for c in range(8):
            P.op("dve", lambda e, c=c: e.tensor_tensor(MML[:, c, :], MCH[:, c, :], LST[:, 1, c, :], ALU.max), r=["MCH", "LST"], w=["MML"])
            P.op("dve", lambda e, c=c: e.tensor_tensor(MCH[:, c + 1, :], MML[:, c, :], LST[:, 0, c, :], ALU.subtract), r=["MML", "LST"], w=["MCH"])

    P.dma("pool", WG[:], w_in_v[:, :, 8192:8200], w=["WG"], sem="WG")
    build_xt(x_pre, 8)
    P.dma("sp", CS[:, :, 0:1024], cs_pre_d, w=["CS", "XIN0", "XIN1"], sem="CS")
    P.op("pool", lambda e: e.memset(VEXT[:, :, 256:257], 1.0), w=["VONE"])
    gates(8, False)
    P.op("dve", lambda e: e.memset(MCH[:, 0, :], 0.0), w=["MCH"])
    m_chain()
    P.op("dve", lambda e: e.tensor_tensor(T32[:], MCH[:, 0:8, :], MML[:], ALU.subtract), r=["MCH", "MML"], w=["T32"])
    P.op("dve", lambda e: e.memset(ECH[:, 7, :], 0.0), w=["ECH"])
    for c in range(6, -1, -1):
        P.op("dve", lambda e, c=c: e.tensor_tensor(ECH[:, c, :], ECH[:, c + 1, :], T32[:, c + 1, :], ALU.add), r=["ECH", "T32"], w=["ECH"])
    P.op("dve", lambda e: e.tensor_tensor(T32[:], ECH[:, 0:8, :], MML[:], ALU.subtract), r=["ECH", "MML", "T32"], w=["T32"])
    P.op("dve", lambda e: e.tensor_tensor(T32[:], T32[:], AA[:], ALU.add), r=["T32", "AA"], w=["T32"])
    P.op("act", lambda e: e.activation(SWT[:], T32[:], AF.Exp), r=["T32"], w=["SWT"])
    P.op("dve", lambda e: e.tensor_scalar(MINIT[:], MCH[:, 8, :], FLAG, None, ALU.mult), r=["MCH", "C"], w=["MINIT"])

    for hh in range(8):
        ret = hh < 4
        h = hh % 4
        if hh == 0:
            nxtP = (load_w(1024), load_w(2048))
        (Wk, wkk), (Wv, wvk) = nxtP
        for t0 in (0, 512):
            if ret:
                rotary(Wk, wkk, KT, "KT", t0, 512)
            else:
                plain_fm(Wk, wkk, KT, "KT", t0, 512, 0.0625)
        for tt in range(8):
            ps, pk = proj_tm(Wv, wvk, tt, 128)
            copy_op(ev_eng(), VEXT[:, tt, 0:256], ps[:, 0:256], r=[pk], w=["VEXT%d" % tt])
        if hh < 7:
            r2, h2 = (hh + 1) < 4, (hh + 1) % 4
            nxtP = (load_w((1024 if r2 else 5120) + 256 * h2), load_w((2048 if r2 else 6144) + 256 * h2))
        for c in range(8):
            make_ktok(c, KPRE[:, c, h:h + 1] if ret else SWT[:, c, h:h + 1])
        for j in range(2):
            ps, pk = bank()

            def acc(e, ps=ps, j=j):
                for c in range(8):
                    ins = e.matmul(ps[:, 0:257], KTOK[:, c, j * 128:(j + 1) * 128], VEXT[:, c, 0:257], start=(c == 0), stop=(c == 7))
                return ins
            P.op("pe", acc, r=["KTOK%d" % c for c in range(8)] + ["VEXT%d" % c for c in range(8)] + ["VONE"], w=[pk])
            P.op("dve", lambda e, ps=ps, j=j, hh=hh: e.tensor_scalar(SINIT[:, hh, j, 0:257], ps[:, 0:257], FLAG, None, ALU.mult),
                 r=[pk, "C"], w=["SINIT%d_%d" % (hh, j)])

    P.barrier(lambda e: e.memset(BARS[:, 0:1], 0.0))

    build_xt(x_own, 8, x_smp)
    P.dma("sp", CS[:], cs_own_d, w=["CS", "XIN0", "XIN1"], sem="CS")
    gates(8, True)
    P.op("dve", lambda e: e.tensor_copy(MCH[:, 0, :], MINIT[:]), r=["MINIT"], w=["MCH"])
    m_chain()
    P.dma("sp", o_m_p, MCH[0:1, 8, :], r=["MCH"], sem="st_mp")
    P.op("dve", lambda e: e.tensor_tensor(MM[:], MCH[:, 0:8, :], CSA[:, 1, :, :], ALU.max), r=["MCH"] + ["CSA1_%d" % c for c in range(8)], w=["MM"])
    P.op("dve", lambda e: e.tensor_scalar(NEGMM[:], MM[:], -1.0, None, ALU.mult), r=["MM"], w=["NEGMM"])
    P.op("dve", lambda e: e.tensor_tensor(T32[:], MCH[:, 0:8, :], MM[:], ALU.subtract), r=["MCH", "MM"], w=["T32"])
    P.op("act", lambda e: e.activation(IW[:], T32[:], AF.Exp), r=["T32"], w=["IW"])
    P.op("dve", lambda e: e.tensor_tensor(T32[:], CSA[:, 0, :, :], MM[:], ALU.subtract), r=["CSA0", "MM", "IW"], w=["T32"])
    P.op("act", lambda e: e.activation(EMT[:], T32[:], AF.Exp), r=["T32"], w=["EMT"])
    P.op("dve", lambda e: e.tensor_tensor(T32[:], AA[:], MML[:], ALU.subtract), r=["AA", "MML", "EMT"], w=["T32"])
    P.op("act", lambda e: e.activation(SWT[:], T32[:], AF.Exp), r=["T32"], w=["SWT"])
    P.op("dve", lambda e: e.tensor_tensor(T32[:], MCH[:, 0:8, :], MML[:], ALU.subtract), r=["MCH", "MML", "SWT"], w=["T32"])
    P.op("act", lambda e: e.activation(SD[:], T32[:], AF.Exp), r=["T32"], w=["SD"])
    s_mt, s_dw, s_iw, s_emt, s_den, s_r, s_t = (SMP[:, i * 4:(i + 1) * 4] for i in range(7))
    P.op("dve", lambda e: e.tensor_tensor(s_t, MS0[:], SP[0:16, 8, :], ALU.subtract), r=["C", "SPs"], w=["s_t"])
    P.op("dve", lambda e: e.tensor_tensor(s_mt, s_t, IG[0:16, 8, :], ALU.max), r=["s_t", "IGs"], w=["s_mt"])
    P.op("dve", lambda e: e.tensor_tensor(s_t, s_t, s_mt, ALU.subtract), r=["s_t", "s_mt"], w=["s_t"])
    P.op("act", lambda e: e.activation(s_iw, s_t, AF.Exp), r=["s_t"], w=["s_iw"])
    P.op("dve", lambda e: e.tensor_tensor(s_t, IG[0:16, 8, :], s_mt, ALU.subtract), r=["s_iw", "IGs", "s_mt"], w=["s_t"])
    P.op("act", lambda e: e.activation(s_dw, s_t, AF.Exp), r=["s_t"], w=["s_dw"])
    P.op("act", lambda e: e.activation(s_emt, s_mt, AF.Exp, scale=-1.0), r=["s_mt"], w=["s_emt"])
    P.dma("sp", o_m_s, s_mt, r=["s_mt"], sem="st_ms")
    P.op("dve", lambda e: e.tensor_tensor(DG[:], s_iw.unsqueeze(1).to_broadcast([16, 16, 4]),
                                          IDF[0:16, 0:16].unsqueeze(2).to_broadcast([16, 16, 4]), ALU.mult), r=["s_iw", "C"], w=["DG"])
    psw, pkw = bank()
    P.op("pe", lambda e: e.matmul(psw[:, 0:64], ONES[0:16, :], DG[:].rearrange("p b h -> p (b h)"), start=True, stop=True), r=["DG", "C"], w=[pkw])
    P.op("dve", lambda e: e.tensor_copy(IWB[:].rearrange("p b h -> p (b h)"), psw[:, 0:64]), r=[pkw], w=["IWB"])

    def norm_gate_out(S, src, skey, np_, fac, fac2, fkeys, gate, gkey, et0, t0):
        i = S["id"]
        BNS, BNA, YF, Y2 = S["BNS"], S["BNA"], S["YF"], S["Y2"]
        kb, ka, ks2, ky, ky2 = "BNS%d" % i, "BNA%d" % i, "s2_%d" % i, "YF%d" % i, "Y2_%d" % i
        P.op("dve", lambda e: e.bn_stats(BNS[0:np_, :], src), r=[skey], w=[kb])
        P.op("dve", lambda e: e.bn_aggr(BNA[0:np_, :], BNS[0:np_, :]), r=[kb], w=[ka])
        s2 = S["SC1"][0:np_, 0:1]
        if fac is None:
            P.op("act", lambda e: e.activation(s2, BNA[0:np_, 1:2], AF.Ln, bias=EPSG[0:np_, :]), r=[ka], w=[ks2])
            P.op("act", lambda e: e.activation(s2, s2, AF.Exp, scale=-0.5), r=[ks2], w=[ks2])
        else:
            P.op("act", lambda e: e.activation(s2, BNA[0:np_, 1:2], AF.Ln, bias=EPSG[0:np_, :], scale=fac2), r=[ka] + fkeys, w=[ks2])
            P.op("act", lambda e: e.activation(s2, s2, AF.Exp, scale=-0.5), r=[ks2], w=[ks2])
            P.op("dve", lambda e: e.tensor_scalar(s2, s2, fac, None, ALU.mult), r=[ks2] + fkeys, w=[ks2])
        P.op("dve", lambda e: e.tensor_scalar(YF[0:np_, :], src, BNA[0:np_, 0:1], s2, ALU.subtract, ALU.mult), r=[skey, ka, ks2], w=[ky])
        P.op("pool", lambda e: e.tensor_tensor(Y2[0:np_, :], YF[0:np_, :], gate, ALU.mult), r=[ky, gkey], w=[ky2])
        pb, pk = bbank()

        def tr(e):
            for j in range(2):
                ins = e.transpose(pb[:, j * 128:j * 128 + np_], Y2[0:np_, j * 128:(j + 1) * 128], IDB[0:np_, 0:np_])
            return ins
        P.op("pe", tr, r=[ky2], w=[pk])
        for j in range(2):
            P.op("act", lambda e, j=j: e.mul(MIXT[:, et0 + j, t0:t0 + np_], pb[:, j * 128:j * 128 + np_], GNW[:, et0 + j:et0 + j + 1]),
                 r=[pk, "C"], w=["MIXT%d_%d" % (et0 + j, t0)])

    smp_ctr = [0]

    for hh in range(8):
        ret = hh < 4
        h = hh % 4
        base = 0 if ret else 4096
        if hh == 0:
            nxtO = [load_w(q * 1024) for q in range(4)]
        (Wq, wqk), (Wk, wkk), (Wv, wvk), (Wg, wgk) = nxtO
        for (W, wk, DST, dk, sc) in ((Wq, wqk, QT, "QT", None), (Wk, wkk, KT, "KT", 0.0625)):
            for t0, n in ((0, 512), (512, 512), (1024, 16)):
                if ret:
                    rotary(W, wk, DST, dk, t0, n)
                else:
                    plain_fm(W, wk, DST, dk, t0, n, sc)
        for tt in range(9):
            np_ = 128 if tt < 8 else 16
            ps, pk = proj_tm(Wv, wvk, tt, np_)
            copy_op("dve", VEXT[0:np_, tt, 0:256], ps[0:np_, 0:256], r=[pk], w=["VEXT%d" % tt])
            ps, pk = proj_tm(Wg, wgk, tt, np_)
            gfn = AF.Silu if ret else AF.Sigmoid
            P.op("act", lambda e, ps=ps, tt=tt, np_=np_, gfn=gfn: e.activation(GATE[0:np_, tt, :], ps[0:np_, 0:256], gfn),
                 r=[pk], w=["GATE%d" % tt])
        if hh < 7:
            b2 = 0 if (hh + 1) < 4 else 4096
            nxtO = [load_w(b2 + q * 1024 + 256 * ((hh + 1) % 4)) for q in range(4)]
        for c in range(8):
            make_ktok(c, KDEC[:, h:h + 1] if ret else SWT[:, c, h:h + 1])
        for j in range(2):
            P.op("pool", lambda e, j=j, hh=hh: e.tensor_copy(SF[0][:, j, 0:257], SINIT[:, hh, j, 0:257]), r=["SINIT%d_%d" % (hh, j)], w=["SF0_%d" % j])
            P.op("act", lambda e, j=j: e.copy(SBF[0][:, j, 0:257], SF[0][:, j, 0:257]), r=["SF0_%d" % j], w=["SBF0_%d" % j])
        skeys = blk_keys("QT", 1024, 16)
        P.op("dve", lambda e: e.tensor_tensor(QM[:], QT[:, :, 1024:1040].unsqueeze(3).to_broadcast([128, 2, 16, 16]),
                                              ID16[:].unsqueeze(1).to_broadcast([128, 2, 16, 16]), ALU.mult), r=skeys + ["C"], w=["QM"])
        pb, pkb = bbank()

        def trs(e, pb=pb):
            for j in range(2):
                e.transpose(pb[0:16, j * 128:(j + 1) * 128], KT[:, j, 1024:1040], IDB)
            for j in range(2):
                ins = e.transpose(pb[0:16, 256 + j * 128:256 + (j + 1) * 128], QT[:, j, 1024:1040], IDB)
            return ins
        P.op("pe", trs, r=skeys + blk_keys("KT", 1024, 16), w=[pkb])
        if ret:
            P.op("dve", lambda e, pb=pb: e.tensor_scalar(KSD[:], pb[0:16, 0:256], 0.0625, None, ALU.mult), r=[pkb], w=["KSD"])
        else:
            P.op("dve", lambda e, pb=pb, h=h: e.tensor_scalar(KSD[:], pb[0:16, 0:256], s_dw[:, h:h + 1], None, ALU.mult), r=[pkb, "s_dw"], w=["KSD"])
            P.op("dve", lambda e, pb=pb: e.tensor_copy(QS[:], pb[0:16, 256:512]), r=[pkb], w=["QS"])
            P.op("dve", lambda e, h=h: e.scalar_tensor_tensor(NNEW[:, h * 256:(h + 1) * 256], NSM[:, h * 256:(h + 1) * 256], s_iw[:, h:h + 1], KSD[:], ALU.mult, ALU.add),
                 r=["C", "s_iw", "KSD"], w=["NNEW%d" % h])
            P.op("dve", lambda e, h=h: e.tensor_tensor(TQ[:], QS[:], NNEW[:, h * 256:(h + 1) * 256], ALU.mult), r=["QS", "NNEW%d" % h], w=["TQ"])
            P.op("dve", lambda e, h=h: e.tensor_reduce(s_den[:, h:h + 1], TQ[:], AX.X, ALU.add), r=["TQ"], w=["s_den%d" % h])
            P.op("dve", lambda e, h=h: e.tensor_scalar(s_r[:, h:h + 1], s_den[:, h:h + 1], -1.0, None, ALU.mult), r=["s_den%d" % h], w=["s_r%d" % h])
            P.op("dve", lambda e, h=h: e.tensor_tensor(s_den[:, h:h + 1], s_den[:, h:h + 1], s_r[:, h:h + 1], ALU.max), r=["s_den%d" % h, "s_r%d" % h], w=["s_den%d" % h])
            P.op("dve", lambda e, h=h: e.tensor_tensor(s_den[:, h:h + 1], s_den[:, h:h + 1], s_emt[:, h:h + 1], ALU.max), r=["s_den%d" % h, "s_emt"], w=["s_den%d" % h])
            P.op("dve", lambda e, h=h: e.reciprocal(s_r[:, h:h + 1], s_den[:, h:h + 1]), r=["s_den%d" % h], w=["s_r%d" % h])
            P.op("dve", lambda e, h=h: e.tensor_tensor(s_t[:, h:h + 1], s_r[:, h:h + 1], s_r[:, h:h + 1], ALU.mult), r=["s_r%d" % h], w=["s_t%d" % h])
        pos_, pkos = PSX, "PSX"
        src_state = st_ret if ret else st_C
        dst_state = o_retS_s if ret else o_C_s

        def sample_token(b):
            sl = smp_ctr[0] % NSLOT
            smp_ctr[0] += 1
            P.dma("sp", SST[sl][:], src_state[b, h].rearrange("(j p) v -> p j v", p=128), w=["SST%d" % sl], sem="SST%d" % sl)
            P.op("dve", lambda e: e.tensor_scalar(KMB[sl][:], KSD[:], IDF[0:16, b:b + 1], None, ALU.mult), r=["KSD", "C"], w=["KMB%d" % sl])
            pst, pks = bank()

            def ou_mm(e):
                for j in range(2):
                    ins = e.matmul(pst[:, j * 256:(j + 1) * 256], KMB[sl][:, j * 128:(j + 1) * 128], VEXT[0:16, 8, 0:256], start=True, stop=True)
                return ins
            P.op("pe", ou_mm, r=["KMB%d" % sl, "VEXT8"], w=[pks])
            scal = G[h] if ret else IWB[:, b, h:h + 1]
            flat = SST[sl][:].rearrange("p j v -> p (j v)")
            P.op("dve", lambda e: e.scalar_tensor_tensor(flat, flat, scal, pst[:, 0:512], ALU.mult, ALU.add),
                 r=[pks, "SST%d" % sl, "IWB"], w=["SST%d" % sl])
            P.op("act", lambda e: e.copy(SNB[sl][:], SST[sl][:]), r=["SST%d" % sl], w=["SNB%d" % sl])
            P.dma("sp", dst_state[b, h].rearrange("(j p) v -> p j v", p=128), SST[sl][:], r=["SST%d" % sl], sem="st_SST%d" % sl)

            def os_mm(e):
                for j in range(2):
                    ins = e.matmul(pos_[0:16, 0:256], QM[:, j, b, :], SNB[sl][:, j, :], start=(b == 0 and j == 0), stop=(b == 15 and j == 1))
                return ins
            P.op("pe", os_mm, r=["QM", "SNB%d" % sl], w=[pkos])

        for c in range(8):
            S = SETS[c % (NSET if ret else NSET + 3)]
            sid = S["id"]
            cur, nxt = c % 2, (c + 1) % 2
            cb = slice(c * 128, (c + 1) * 128)
            qk = blk_keys("QT", c * 128, 128); kk = blk_keys("KT", c * 128, 128)
            pst, pks = bank()

            def st_mm(e, pst=pst, c=c):
                for j in range(2):
                    ins = e.matmul(pst[:, j * 256:j * 256 + 256], KTOK[:, c, j * 128:(j + 1) * 128], VEXT[:, c, 0:256], start=True, stop=True)
                return ins
            P.op("pe", st_mm, r=["KTOK%d" % c, "VEXT%d" % c], w=[pks])
            if not ret:
                pn, pkn = bank()

                def n_mm(e, pn=pn, c=c):
                    for j in range(2):
                        ins = e.matmul(pn[:, j:j + 1], KTOK[:, c, j * 128:(j + 1) * 128], VEXT[:, c, 256:257], start=True, stop=True)
                    return ins
                P.op("pe", n_mm, r=["KTOK%d" % c, "VONE"], w=[pkn])
            for j in range(2):
                scal = G[h] ** 128 if ret else SD[:, c, h:h + 1]
                P.op("dve", lambda e, pst=pst, j=j, scal=scal, cur=cur, nxt=nxt: e.scalar_tensor_tensor(SF[nxt][:, j, 0:256], SF[cur][:, j, 0:256], scal, pst[:, j * 256:(j + 1) * 256], ALU.mult, ALU.add),
                     r=[pks, "SF%d_%d" % (cur, j), "SD"], w=["SF%d_%d" % (nxt, j)])
                if not ret:
                    P.op("dve", lambda e, pn=pn, j=j, c=c, h=h, cur=cur, nxt=nxt: e.scalar_tensor_tensor(SF[nxt][:, j, 256:257], SF[cur][:, j, 256:257], SD[:, c, h:h + 1], pn[:, j:j + 1], ALU.mult, ALU.add),
                         r=[pkn, "SF%d_%d" % (cur, j), "SD"], w=["SF%d_%d" % (nxt, j)])
                P.op("act", lambda e, j=j, nxt=nxt, nc_=(256 if ret else 257): e.copy(SBF[nxt][:, j, 0:nc_], SF[nxt][:, j, 0:nc_]), r=["SF%d_%d" % (nxt, j)], w=["SBF%d_%d" % (nxt, j)])
            SCM, kscm = S["SCM"], "SCM%d" % sid
            sbk = ["SBF%d_0" % cur, "SBF%d_1" % cur]
            if ret:
                ps, pk = bank()

                def sc_mm(e, ps=ps, cb=cb):
                    for j in range(2):
                        ins = e.matmul(ps[:, 0:128], KT[:, j, cb], QT[:, j, cb], start=(j == 0), stop=(j == 1))
                    return ins
                P.op("pe", sc_mm, r=qk + kk, w=[pk])
                P.op("dve", lambda e, ps=ps, h=h, SCM=SCM: e.tensor_tensor(SCM[:], ps[:, 0:128], RMASK[:, h, :], ALU.mult), r=[pk, "C"], w=[kscm])
                po, pko = bank()

                def o_mm(e, po=po, cb=cb, c=c, SCM=SCM, cur=cur):
                    e.matmul(po[:, 0:256], SCM[:], VEXT[:, c, 0:256], start=True, stop=False)
                    for j in range(2):
                        ins = e.matmul(po[:, 0:256], QT[:, j, cb], SBF[cur][:, j, 0:256], start=False, stop=(j == 1))
                    return ins
                P.op("pe", o_mm, r=[kscm, "VEXT%d" % c] + sbk + qk, w=[pko])
                norm_gate_out(S, po[:, 0:256], pko, 128, QD[:, h:h + 1], QD2[:, h:h + 1], ["C"], GATE[:, c, :], "GATE%d" % c, hh * 2, c * 128)
            else:
                DW, kdw = S["DW"], "DW%d" % sid
                SCMT, kscmt = S["SCMT"], "SCMT%d" % sid
                INTRA, kin = S["INTRA"], "INTRA%d" % sid
                NUMER, knu = S["NUMER"], "NUMER%d" % sid
                psa, pka = bank()
                P.op("pe", lambda e, psa=psa, c=c, h=h: e.matmul(psa[:, 0:128], AA[:, c, h:h + 1].to_broadcast([128, 128]), IDF, start=True, stop=True),
                     r=["AA", "C"], w=[pka])
                P.op("dve", lambda e, psa=psa, DW=DW: e.tensor_tensor(DW[:], psa[:, 0:128], NEGM, ALU.add), r=[pka, "C"], w=[kdw] + (ALIAS if sid >= NSET else []))
                P.op("act", lambda e, c=c, h=h, DW=DW: e.activation(DW[:], DW[:], AF.Exp, bias=NEGMM[:, c, h:h + 1]), r=[kdw, "NEGMM"], w=[kdw])
                ps, pk = bank()

                def sc_mm(e, ps=ps, cb=cb):
                    for j in range(2):
                        ins = e.matmul(ps[:, 0:128], QT[:, j, cb], KT[:, j, cb], start=(j == 0), stop=(j == 1))
                    return ins
                P.op("pe", sc_mm, r=qk + kk, w=[pk])
                P.op("dve", lambda e, ps=ps, SCM=SCM, DW=DW: e.tensor_tensor(SCM[:], ps[:, 0:128], DW[:], ALU.mult), r=[pk, kdw], w=[kscm])
                pb, pkb = bbank()
                P.op("pe", lambda e, pb=pb, SCM=SCM: e.transpose(pb[:, 0:128], SCM[:], IDB), r=[kscm], w=[pkb])
                copy_op("act", SCMT[:], pb[:, 0:128], r=[pkb], w=[kscmt])
                pi, pki = bank()
                P.op("pe", lambda e, pi=pi, c=c, SCMT=SCMT: e.matmul(pi[:, 0:257], SCMT[:], VEXT[:, c, 0:257], start=True, stop=True), r=[kscmt, "VEXT%d" % c, "VONE"], w=[pki])
                pe_, pke = bank()

                def in_mm(e, pe_=pe_, cb=cb, cur=cur):
                    for j in range(2):
                        ins = e.matmul(pe_[:, 0:257], QT[:, j, cb], SBF[cur][:, j, 0:257], start=(j == 0), stop=(j == 1))
                    return ins
                P.op("pe", in_mm, r=qk + sbk + [kscmt], w=[pke])
                copy_op("act", INTRA[:, 0:257], pi[:, 0:257], r=[pki], w=[kin])
                iw = IW[:, c, h:h + 1]
                SC1 = S["SC1"]
                den, rr, rr2 = SC1[:, 1:2], SC1[:, 2:3], SC1[:, 3:4]
                kd, kr, kr2 = "den%d" % sid, "rr%d" % sid, "rr2_%d" % sid
                P.op("dve", lambda e, pe_=pe_, iw=iw, den=den, INTRA=INTRA: e.scalar_tensor_tensor(den, pe_[:, 256:257], iw, INTRA[:, 256:257], ALU.mult, ALU.add), r=[pke, kin, "IW"], w=[kd])
                P.op("dve", lambda e, den=den, rr=rr: e.tensor_scalar(rr, den, -1.0, None, ALU.mult), r=[kd], w=[kr])
                P.op("dve", lambda e, den=den, rr=rr: e.tensor_tensor(den, den, rr, ALU.max), r=[kd, kr], w=[kd])
                P.op("dve", lambda e, c=c, h=h, den=den: e.tensor_tensor(den, den, EMT[:, c, h:h + 1], ALU.max), r=[kd, "EMT"], w=[kd])
                P.op("dve", lambda e, den=den, rr=rr: e.reciprocal(rr, den), r=[kd], w=[kr])
                P.op("dve", lambda e, rr=rr, rr2=rr2: e.tensor_tensor(rr2, rr, rr, ALU.mult), r=[kr], w=[kr2])
                P.op("dve", lambda e, pe_=pe_, iw=iw, NUMER=NUMER, INTRA=INTRA: e.scalar_tensor_tensor(NUMER[:], pe_[:, 0:256], iw, INTRA[:, 0:256], ALU.mult, ALU.add), r=[pke, kin, "IW"], w=[knu])
                norm_gate_out(S, NUMER[:], knu, 128, rr, rr2, [kr, kr2], GATE[:, c, :], "GATE%d" % c, hh * 2, c * 128)
            sample_token(2 * c)
            sample_token(2 * c + 1)
        dst = (o_retS_p if ret else o_C_p)[h].rearrange("(j p) v -> p j v", p=128)
        P.dma("sp", dst, SF[0][:, :, 0:256], r=["SF0_0", "SF0_1"], sem="st_SF")
        if not ret:
            for j in range(2):
                P.dma("sp", bass.AP(o_n_p.tensor, h * 256 + j * 128, [[1, 128], [1, 1]]), SF[0][:, j, 256:257], r=["SF0_0", "SF0_1"], sem="st_SF")
        S = SETS[0]
        if ret:
            norm_gate_out(S, pos_[0:16, 0:256], pkos, 16, None, None, [], GATE[0:16, 8, :], "GATE8", hh * 2, 1024)
        else:
            norm_gate_out(S, pos_[0:16, 0:256], pkos, 16, s_r[:, h:h + 1], s_t[:, h:h + 1], ["s_r%d" % h, "s_t%d" % h], GATE[0:16, 8, :], "GATE8", hh * 2, 1024)
    P.dma("sp", o_n_s, NNEW[:], r=["NNEW%d" % h for h in range(4)], sem="st_ns")

    P.barrier(lambda e: e.memset(BARS[:, 0:1], 0.0))
    if dbg:
        dbg_mixt = dout("dbg_mixt", [128, 16, TOK])
        P.dma("pool", dbg_mixt, MIXT[:], sem="dbg1")

    AR.off = markA
    X1 = AR.alloc([128, 9, D], F32)
    X1T = AR.alloc([128, 16, TOK], BF16)
    LNG = AR.alloc([128, D], F32)
    LNB = AR.alloc([128, D], F32)
    LNT = [dict(ST8=AR.alloc([128, 4, 6], F32), LNA=AR.alloc([128, 2], F32), LNR=AR.alloc([128, 1], F32), NMR=AR.alloc([128, 1], F32)) for _ in range(3)]
    markB = AR.off
    WO = [AR.alloc([128, 16, 512], BF16) for _ in range(2)]

    def bcast_row(row_ap):
        return bass.AP(row_ap.tensor, row_ap.offset, [[0, 128], [1, D]])

    for t2 in range(0, 8, 2):
        P.dma("sp", X1[:, t2:t2 + 2, :], x_own[t2 * 128:(t2 + 2) * 128, :].rearrange("(t p) d -> p t d", p=128), r=["XCH"],
              w=["X1_%d" % t2, "X1_%d" % (t2 + 1), "XCH"], sem="X1_%d" % t2, cost=7.0)
    P.dma("sp", X1[0:16, 8, :], x_smp, r=["XCH"], w=["X1_8", "XCH"], sem="X1_8")
    P.dma("sp", LNG[:], bcast_row(ln_d[0]), r=["XCH"], w=["LNG", "XCH"], sem="LNG", cost=5.0)
    P.dma("sp", LNB[:], bcast_row(ln_d[1]), r=["XCH"], w=["LNB", "XCH"], sem="LNB", cost=5.0)
    P.op("act", lambda e: e.mul(LNG[:], LNG[:], ALPHA), r=["LNG"], w=["LNG"], cost=1.9)
    P.op("act", lambda e: e.mul(LNB[:], LNB[:], ALPHA), r=["LNB"], w=["LNB"], cost=1.9)
    mix_keys = lambda tt: ["MIXT%d_%d" % (et, t0) for et in range(16) for t0 in ([tt * 128] if tt < 8 else [1024])]
    w_out_v = w_out.rearrange("(et p) c -> p et c", p=128)

    def dense_tm(ACT_T, akeys_fn, wv, nk, ring, rname, evac):
        for cb in range(4):
            s = cb % len(ring)
            P.dma("pool", ring[s][:, 0:nk, :], wv[:, :, cb * 512:(cb + 1) * 512], w=["%s%d" % (rname, s)], sem="%s%d" % (rname, s), cost=16.0)
            for tt in range(9):
                np_ = 128 if tt < 8 else 16
                ps, pk = bank()

                def mm(e, ps=ps, tt=tt, np_=np_, s=s):
                    for kt in range(nk):
                        ins = e.matmul(ps[0:np_, :], ACT_T[:, kt, tt * 128:tt * 128 + np_], ring[s][:, kt, :], start=(kt == 0), stop=(kt == nk - 1))
                    return ins
                P.op("pe", mm, r=["%s%d" % (rname, s)] + akeys_fn(tt), w=[pk], cost=0.3 + nk * 0.215)
                evac(ps, pk, tt, np_, cb)

    def ev_out(ps, pk, tt, np_, cb):
        dst = X1[0:np_, tt, cb * 512:(cb + 1) * 512]
        P.op("dve", lambda e: e.scalar_tensor_tensor(dst, dst, ALPHA, ps[0:np_, :], ALU.mult, ALU.add), r=[pk, "X1_%d" % tt], w=["X1_%d" % tt])

    dense_tm(MIXT, mix_keys, w_out_v, 16, WO, "WO", ev_out)

    def layer_norm(tt, np_, extra_r):
        xk = "X1_%d" % tt
        xv = X1[0:np_, tt, :]
        L = LNT[tt % 3]
        i = tt % 3
        ST8, LNA, LNR, NMR = L["ST8"], L["LNA"], L["LNR"], L["NMR"]
        k8, ka, kr, kn = "ST8_%d" % i, "LNA%d" % i, "LNR%d" % i, "NMR%d" % i

        def bn(e):
            for q in range(4):
                ins = e.bn_stats(ST8[0:np_, q, :], X1[0:np_, tt, q * 512:(q + 1) * 512])
            return ins
        P.op("dve", bn, r=[xk], w=[k8], cost=2.4)
        P.op("dve", lambda e: e.bn_aggr(LNA[0:np_, :], ST8[0:np_, :, :]), r=[k8], w=[ka])
        P.op("act", lambda e: e.activation(LNR[0:np_, :], LNA[0:np_, 1:2], AF.Ln, bias=EPSL[0:np_, :]), r=[ka], w=[kr])
        P.op("act", lambda e: e.activation(LNR[0:np_, :], LNR[0:np_, :], AF.Exp, scale=-0.5), r=[kr], w=[kr])
        P.op("dve", lambda e: e.tensor_scalar(NMR[0:np_, :], LNA[0:np_, 0:1], -1.0, LNR[0:np_, :], ALU.mult, ALU.mult), r=[ka, kr], w=[kn])
        P.op("act", lambda e: e.activation(xv, xv, AF.Identity, bias=NMR[0:np_, :], scale=LNR[0:np_, :]), r=[xk, kr, kn], w=[xk], cost=1.9)
        P.op("dve", lambda e: e.tensor_tensor(xv, xv, LNG[0:np_, :], ALU.mult), r=[xk, "LNG"] + extra_r, w=[xk], cost=2.3)
        P.op("pool", lambda e: e.tensor_tensor(xv, xv, LNB[0:np_, :], ALU.add), r=[xk, "LNB"], w=[xk], cost=4.0)

    for tt in range(9):
        np_ = 128 if tt < 8 else 16
        layer_norm(tt, np_, [])
        for g in range(4):
            ps, pk = bank()

            def tr(e, ps=ps, g=g, tt=tt, np_=np_):
                for q in range(4):
                    dt = g * 4 + q
                    ins = e.transpose(ps[:, q * 128:q * 128 + np_], X1[0:np_, tt, dt * 128:(dt + 1) * 128], IDF[0:np_, 0:np_])
                return ins
            P.op("pe", tr, r=["X1_%d" % tt, "C"], w=[pk])
            out = X1T[:, g * 4:(g + 1) * 4, tt * 128:tt * 128 + np_]
            inn = ps[:].rearrange("p (q t) -> p q t", q=4)[:, :, 0:np_]
            copy_op(ev_eng(), out, inn, r=[pk], w=["X1T%d_%d" % (tt, g)], scale=1.0 / ALPHA)

    P.barrier(lambda e: e.memset(BARS[:, 0:1], 0.0))
    if dbg:
        dbg_x1t = dout("dbg_x1t", [128, 16, TOK])
        P.dma("pool", dbg_x1t, X1T[:], sem="dbg2")

    AR.off = markB
    WB = [AR.alloc([128, 8, 512], BF16) for _ in range(3)]
    PT = AR.alloc([128, 2, TOK], BF16)
    SIG = AR.alloc([128, 512], F32)
    WPE = AR.alloc([128, 2, D], BF16)
    AR.off = mixt_off
    HT = AR.alloc([128, 8, TOK], BF16)
    W1 = [AR.alloc([128, 16, 128], BF16) for _ in range(4)]
    assert AR.off <= markA, (AR.off, markA)
    PIN = HT[:, 0:3, :].rearrange("p a t -> p (a t)")[:, 0:9 * 256].rearrange("p (t c) -> p t c", t=9)

    x1t_keys = lambda tt: ["X1T%d_%d" % (tt, g) for g in range(4)]
    P.dma("pool", PIN[:, 0:8, :], p_own.rearrange("(t p) c -> p t c", p=128), w=["PIN"], sem="PIN")
    P.dma("pool", PIN[0:16, 8, :], p_smp, w=["PIN8"], sem="PIN8")
    P.dma("pool", WPE[:], w_pe.rearrange("(pt p) c -> p pt c", p=128), w=["WPE"], sem="WPE")
    for tt in range(9):
        np_ = 128 if tt < 8 else 16
        pb, pkb = bbank()

        def trp(e, pb=pb, tt=tt, np_=np_):
            for j in range(2):
                ins = e.transpose(pb[:, j * 128:j * 128 + np_], PIN[0:np_, tt, j * 128:(j + 1) * 128], IDB[0:np_, 0:np_])
            return ins
        P.op("pe", trp, r=["PIN", "PIN8"], w=[pkb])
        copy_op(ev_eng(), PT[:, :, tt * 128:tt * 128 + np_], pb[:, 0:256].rearrange("p (j t) -> p j t", j=2)[:, :, 0:np_], r=[pkb], w=["PT%d" % tt])
    w_pg_v = w_pg.rearrange("(dt p) c -> p dt c", p=128)
    for cb in range(8):
        s = cb % 3
        P.dma("pool", WB[s][:].rearrange("p a b -> p (a b)").rearrange("p (dt c) -> p dt c", dt=16), w_pg_v[:, :, cb * 256:(cb + 1) * 256], w=["WB%d" % s], sem="WB%d" % s, cost=9.0)
        Wb = WB[s][:].rearrange("p a b -> p (a b)").rearrange("p (dt c) -> p dt c", dt=16)
        for tt in range(9):
            np_ = 128 if tt < 8 else 16
            ps, pk = bank()

            def mm(e, ps=ps, tt=tt, np_=np_, Wb=Wb, cb=cb):
                for dt in range(16):
                    e.matmul(ps[0:np_, 0:256], X1T[:, dt, tt * 128:tt * 128 + np_], Wb[:, dt, :], start=(dt == 0), stop=(dt == 15))
                for pt in range(2):
                    ins = e.matmul(ps[0:np_, 256:512], PT[:, pt, tt * 128:tt * 128 + np_], WPE[:, pt, cb * 256:(cb + 1) * 256], start=(pt == 0), stop=(pt == 1))
                return ins
            P.op("pe", mm, r=["WB%d" % s, "WPE", "PT%d" % tt] + x1t_keys(tt), w=[pk], cost=2.3)
            P.op("act", lambda e, ps=ps, np_=np_: e.activation(SIG[0:np_, 0:256], ps[0:np_, 0:256], AF.Sigmoid), r=[pk], w=["SIG"])
            P.op("dve", lambda e, ps=ps, np_=np_: e.tensor_tensor(SIG[0:np_, 256:512], ps[0:np_, 256:512], SIG[0:np_, 0:256], ALU.mult), r=[pk, "SIG"], w=["SIG2"])
            dst = X1[0:np_, tt, cb * 256:(cb + 1) * 256]
            P.op("pool", lambda e, dst=dst, np_=np_: e.tensor_tensor(dst, dst, SIG[0:np_, 256:512], ALU.add), r=["SIG2", "X1_%d" % tt], w=["X1_%d" % tt])
    P.dma("sp", LNG[:], bcast_row(ln_d[2]), r=["WPE", "PIN"], w=["LNG"], sem="LNG")
    P.dma("sp", LNB[:], bcast_row(ln_d[3]), r=["WPE", "PIN"], w=["LNB"], sem="LNB")
    w1_v = w_ff1.rearrange("(dt p) f -> p dt f", p=128)
    w2_v = w_ff2.rearrange("(ft p) c -> p ft c", p=128)
    w1i = 0
    for fs in range(8):
        for fl in range(8):
            ft = fs * 8 + fl
            s = w1i % 4; w1i += 1
            P.dma("pool", W1[s][:], w1_v[:, :, ft * 128:(ft + 1) * 128], w=["W1_%d" % s], sem="W1_%d" % s, cost=5.5)
            for t0, n in ((0, 512), (512, 512), (1024, 16)):
                ps, pk = bank()

                def mm(e, ps=ps, s=s, t0=t0, n=n):
                    for dt in range(16):
                        ins = e.matmul(ps[:, 0:n], W1[s][:, dt, :], X1T[:, dt, t0:t0 + n], start=(dt == 0), stop=(dt == 15))
                    return ins
                tiles = range(t0 // 128, (t0 + n - 1) // 128 + 1)
                P.op("pe", mm, r=["W1_%d" % s] + [k for tt in tiles for k in x1t_keys(tt)], w=[pk], cost=0.3 + 16 * n / 2400.0)
                P.op("act", lambda e, ps=ps, n=n: e.activation(SIG[:, 0:n], ps[:, 0:n], AF.Relu), r=[pk], w=["SIG", "SIG2"])
                P.op("dve", lambda e, fl=fl, t0=t0, n=n: e.tensor_tensor(HT[:, fl, t0:t0 + n], SIG[:, 0:n], SIG[:, 0:n], ALU.mult), r=["SIG", "SIG2"], w=["HT%d_%d" % (fl, t0)])
        groups = [list(range(9))] if fs < 7 else [[0, 1, 2, 3, 4, 5], [6, 7, 8]]
        for gi, grp in enumerate(groups):
            for cb in range(4):
                s = (fs * 4 + cb + 2 + gi) % 3
                P.dma("pool", WB[s][:], w2_v[:, fs * 8:(fs + 1) * 8, cb * 512:(cb + 1) * 512], w=["WB%d" % s], sem="WB%d" % s, cost=9.0)
                for tt in grp:
                    np_ = 128 if tt < 8 else 16
                    t0k = (tt // 4) * 512 if tt < 8 else 1024
                    ps, pk = bank()

                    def mm(e, ps=ps, tt=tt, np_=np_, s=s):
                        for fl in range(8):
                            ins = e.matmul(ps[0:np_, :], HT[:, fl, tt * 128:tt * 128 + np_], WB[s][:, fl, :], start=(fl == 0), stop=(fl == 7))
                        return ins
                    P.op("pe", mm, r=["WB%d" % s] + ["HT%d_%d" % (fl, t0k) for fl in range(8)], w=[pk], cost=2.0)
                    dst = X1[0:np_, tt, cb * 512:(cb + 1) * 512]
                    P.op("dve", lambda e, dst=dst, ps=ps, np_=np_: e.tensor_tensor(dst, dst, ps[0:np_, :], ALU.add), r=[pk, "X1_%d" % tt], w=["X1_%d" % tt])
    for tt in range(9):
        np_ = 128 if tt < 8 else 16
        layer_norm(tt, np_, [])
        if tt < 8:
            P.dma("sp", y_own[tt * 128:(tt + 1) * 128, :], X1[:, tt, :], r=["X1_%d" % tt], sem="st_y%d" % tt)
        else:
            P.dma("sp", y_smp, X1[0:16, 8, :], r=["X1_8"], sem="st_y8")
    P.emit()
    return nc, P, AR


_CACHE = {}
DBG = False


def _consts(hf):
    f32 = np.float32
    t = np.arange(128)
    sq = np.zeros((128, 6, 128), f32)
    sq[:, 0] = np.eye(128)
    sq[:, 1] = (t[:, None] <= t[None, :])
    sq[:, 2] = np.where(t[None, :] <= t[:, None], 0.0, -1e30)
    sq[127, 3, :] = 1.0
    sq[:, 4] = 1.0
    g = np.array(G, np.float64)
    rmask = np.zeros((128, 4, 128), f32)
    for h in range(4):
        rmask[:, h, :] = np.where(t[None, :] >= t[:, None], g[h] ** (-(t[:, None] + 1.0)) / 16.0, 0.0)
    small = np.zeros((128, 48), f32)
    for h in range(4):
        small[:, h] = g[h] ** (t + 1.0)
        small[:, 4 + h] = (g[h] ** (t + 1.0)) ** 2
        small[:, 8 + h] = g[h] ** (127.0 - t) / 16.0
        for c in range(8):
            small[:, 12 + c * 4 + h] = g[h] ** (1023.0 - (c * 128 + t)) / 16.0
    small[:, 44] = float(hf)
    small[:, 45] = GN_EPS
    small[:, 46] = LN_EPS
    id16 = np.tile(np.eye(16, dtype=f32).reshape(1, 256), (128, 1))
    inv = 10000.0 ** (-(np.arange(128, dtype=np.float64) * 2.0) / 256.0)
    pos_own = np.concatenate([hf * 1024 + np.arange(1024), np.full(16, 16384)]).astype(np.float64)
    pos_pre = np.arange(1024).astype(np.float64)

    def cs(pos):
        ang = pos[None, :] * inv[:, None]
        return np.ascontiguousarray(np.stack([np.cos(ang), np.sin(ang)], axis=1).astype(f32))
    return dict(c_sq=sq, c_rmask=rmask, c_small=small, c_id16=id16, cs_own=cs(pos_own), cs_pre=cs(pos_pre))


def kernel(x_prompt, x_sample, state_ret, state_mlstm_C, state_mlstm_n, state_mlstm_m,
           p_prompt, p_sample, w_in, b_gate, ret_gn_w, mlstm_gn_w, w_out, ln1_g, ln1_b,
           w_ff1, w_ff2, w_pe, w_pe_gate, ln2_g, ln2_b):
    if "nc" not in _CACHE:
        _CACHE["nc"] = build_program(dbg=DBG)[0]
    nc = _CACHE["nc"]
    A = lambda a: np.ascontiguousarray(np.asarray(a, dtype=np.float32))
    shared = dict(
        w_in=A(w_in[0]), w_out=A(w_out[0]), w_ff1=A(w_ff1[0]), w_ff2=A(w_ff2[0]), w_pe=A(w_pe[0]),
        w_pe_gate=A(w_pe_gate[0]), ln_all=A(np.stack([ln1_g[0], ln1_b[0], ln2_g[0], ln2_b[0]])),
        b_gate=A(b_gate), gnw=A(np.concatenate([ret_gn_w[0], mlstm_gn_w[0]]).reshape(16, 128).T),
    )
    in_maps = []
    for c in range(NCORES):
        s, hf = c // 2, c % 2
        m = dict(shared)
        m.update(_consts(hf))
        m["x_own"] = A(x_prompt[s, hf * 1024:(hf + 1) * 1024])
        m["x_pre"] = A(x_prompt[s, 0:1024])
        m["x_smp"] = A(x_sample[c * 16:(c + 1) * 16, 0])
        m["p_own"] = A(p_prompt[0, s, hf * 1024:(hf + 1) * 1024])
        m["p_smp"] = A(p_sample[0, c * 16:(c + 1) * 16, 0])
        m["st_ret"] = A(state_ret[0, c * 16:(c + 1) * 16])
        m["st_C"] = A(state_mlstm_C[0, c * 16:(c + 1) * 16])
        m["st_n"] = A(state_mlstm_n[0, c * 16:(c + 1) * 16].reshape(16, 1024))
        m["st_m"] = A(state_mlstm_m[0, c * 16:(c + 1) * 16])
        in_maps.append(m)
    res = run_bass_kernel_spmd(nc, in_maps, core_ids=list(range(NCORES)))
    R = res.results
    _CACHE["R"] = R
    yp = np.zeros((4, 2048, D), np.float32)
    for c in range(NCORES):
        yp[c // 2, (c % 2) * 1024:(c % 2 + 1) * 1024] = R[c]["y_own"]
    ys = np.concatenate([R[c]["y_smp"] for c in range(NCORES)], 0).reshape(128, 1, D)
    odd = [1, 3, 5, 7]
    rS_p = np.stack([R[c]["retS_p"] for c in odd])[None]
    C_p = np.stack([R[c]["C_p"] for c in odd])[None]
    n_p = np.stack([R[c]["n_p"] for c in odd])[None]
    m_p = np.stack([R[c]["m_p"][0] for c in odd])[None]
    rS_s = np.concatenate([R[c]["retS_s"] for c in range(NCORES)], 0)[None]
    C_s = np.concatenate([R[c]["C_s"] for c in range(NCORES)], 0)[None]
    n_s = np.concatenate([R[c]["n_s"] for c in range(NCORES)], 0).reshape(1, 128, 4, 256)
    m_s = np.concatenate([R[c]["m_s"] for c in range(NCORES)], 0)[None]
    f = lambda a: np.ascontiguousarray(a, dtype=np.float32)
    return (f(yp), f(ys), f(rS_p), f(C_p), f(n_p), f(m_p), f(rS_s), f(C_s), f(n_s), f(m_s))
```

```python
import numpy as np
import concourse.bass as bass
import concourse.mybir as mybir
from concourse.bass_utils import run_bass_kernel_spmd

F32 = mybir.dt.float32
BF16 = mybir.dt.bfloat16
AF = mybir.ActivationFunctionType
ALU = mybir.AluOpType
AX = mybir.AxisListType

ENGS = ("pe", "act", "dve", "pool", "sp")
NCORES = 8
D = 2048
TOK = 1040
ALPHA = 2.0 ** 0.25
GN_EPS = 1e-6
LN_EPS = 1e-5
G = [1.0 - 2.0 ** (-5.0 - h) for h in range(4)]


class Prog:
    DEF_COST = dict(pe=0.3, act=0.45, dve=0.45, pool=0.8, sp=0.3)

    def __init__(self, nc):
        self.nc = nc
        self.ops = []
        self.last_w = {}
        self.readers = {}
        self.dma_sem_count = {}
        self.bar = None
        self.bar_start = 0

    def _add(self, eng, fn, r, w, dma_sem=None, cost=None):
        i = len(self.ops)
        deps = set()
        if self.bar is not None:
            deps.add(self.bar)
        for k in list(r) + list(w):
            if k in self.last_w:
                deps.add(self.last_w[k])
        for k in w:
            for x in self.readers.get(k, ()):
                deps.add(x)
        deps.discard(i)
        for k in w:
            self.last_w[k] = i
            self.readers[k] = []
        for k in r:
            self.readers.setdefault(k, []).append(i)
        odeps = set(deps)
        if eng == "pe":
            deps = {d for d in deps if self.ops[d]["eng"] != "pe" or self.ops[d]["dma_sem"] is not None}
        if cost is None:
            cost = 2.5 if dma_sem is not None else self.DEF_COST[eng]
        op = dict(id=i, eng=eng, fn=fn, deps=deps, odeps=odeps, dma_sem=dma_sem, has_dep=False, has_any=False, cost=cost)
        if dma_sem is not None:
            c = self.dma_sem_count.get(dma_sem, 0) + 1
            self.dma_sem_count[dma_sem] = c
            op["target"] = 16 * c
        for d in deps:
            self.ops[d]["has_dep"] = True
        for d in odeps:
            self.ops[d]["has_any"] = True
        self.ops.append(op)
        return i

    def op(self, eng, fn, r=(), w=(), cost=None):
        return self._add(eng, fn, r, w, cost=cost)

    def dma(self, eng, out, in_, r=(), w=(), sem=None, cost=None):
        return self._add(eng, lambda e: e.dma_start(out=out, in_=in_), r, w, dma_sem=sem, cost=cost)

    def barrier(self, fn):
        i = len(self.ops)
        deps = {o["id"] for o in self.ops[self.bar_start:] if not o["has_any"]}
        if self.bar is not None:
            deps.add(self.bar)
        op = dict(id=i, eng="dve", fn=fn, deps=deps, odeps=set(deps), dma_sem=None, has_dep=True, has_any=True, cost=0.2)
        for d in deps:
            self.ops[d]["has_dep"] = True
            self.ops[d]["has_any"] = True
        self.ops.append(op)
        self.bar = i
        self.bar_start = i
        self.last_w = {}
        self.readers = {}
        return i

    def schedule(self):
        import heapq
        ops = self.ops
        n = len(ops)
        succ = [[] for _ in range(n)]
        indeg = [0] * n
        for o in ops:
            indeg[o["id"]] = len(o["odeps"])
            for d in o["odeps"]:
                succ[d].append(o["id"])
        ready = {e: [] for e in ENGS}
        rtime = [0.0] * n
        finish = [0.0] * n
        free = {e: 0.0 for e in ENGS}
        order = {e: [] for e in ENGS}
        for o in ops:
            if indeg[o["id"]] == 0:
                heapq.heappush(ready[o["eng"]], (0.0, o["id"]))
        done = 0
        while done < n:
            best = None
            for e in ENGS:
                h = ready[e]
                if not h:
                    continue
                t_free = free[e]
                cand = None
                avail = [x for x in h if x[0] <= t_free]
                if avail:
                    cid = min(x[1] for x in avail)
                    cand = (t_free, cid)
                else:
                    rt, cid = min(h)
                    cand = (rt, cid)
                if best is None or cand < best[0]:
                    best = (cand, e)
            (start, i), e = best
            h = ready[e]
            for k, x in enumerate(h):
                if x[1] == i:
                    h[k] = h[-1]
                    h.pop()
                    break
            heapq.heapify(h)
            o = ops[i]
            if o["dma_sem"] is not None:
                free[e] = start + 0.35
                finish[i] = start + o["cost"]
            else:
                free[e] = start + o["cost"]
                finish[i] = free[e]
            order[e].append(i)
            done += 1
            for sidx in succ[i]:
                so = ops[sidx]
                lat = 0.2 if (so["eng"] == e and o["dma_sem"] is None) else 0.9
                rtime[sidx] = max(rtime[sidx], finish[i] + lat)
                indeg[sidx] -= 1
                if indeg[sidx] == 0:
                    heapq.heappush(ready[so["eng"]], (rtime[sidx], sidx))
        self.sim_time = max(finish)
        return order

    def emit(self, reorder=True):
        nc = self.nc
        esem = {e: nc.alloc_semaphore("s_" + e) for e in ENGS}
        dsem = {k: nc.alloc_semaphore("d_%d" % j) for j, k in enumerate(self.dma_sem_count)}
        ops = self.ops
        if reorder:
            order = self.schedule()
        else:
            order = {e: [o["id"] for o in ops if o["eng"] == e] for e in ENGS}
        cnt = {e: 0 for e in ENGS}
        for e in ENGS:
            for i in order[e]:
                op = ops[i]
                if op["dma_sem"] is None and op["has_dep"]:
                    cnt[e] += 1
                    op["ms"] = cnt[e]
        stats = dict(waits=0, ops=len(ops))

        def run(eng_name, e):
            seen = {}
            for i in order[eng_name]:
                op = ops[i]
                need = {}
                for d in op["deps"]:
                    p = ops[d]
                    if p["dma_sem"] is not None:
                        key, val = ("d", p["dma_sem"]), p["target"]
                    else:
                        key, val = ("e", p["eng"]), p["ms"]
                    if val > need.get(key, 0):
                        need[key] = val
                for key, val in need.items():
                    if seen.get(key, 0) >= val:
                        continue
                    sm = dsem[key[1]] if key[0] == "d" else esem[key[1]]
                    e.wait_ge(sm, val)
                    stats["waits"] += 1
                    seen[key] = val
                ins = op["fn"](e)
                if op["dma_sem"] is not None:
                    ins.then_inc(dsem[op["dma_sem"]], 16)
                elif op["has_dep"]:
                    ins.then_inc(esem[op["eng"]], 1)
            if eng_name == "sp":
                for k, c in self.dma_sem_count.items():
                    if seen.get(("d", k), 0) < 16 * c:
                        e.wait_ge(dsem[k], 16 * c)

        with nc.Block() as block:
            @block.tensor
            def _(e):
                run("pe", e)

            @block.scalar
            def _(e):
                run("act", e)

            @block.vector
            def _(e):
                run("dve", e)

            @block.gpsimd
            def _(e):
                run("pool", e)

            @block.sync
            def _(e):
                run("sp", e)
        self.stats = stats


class Arena:
    def __init__(self, nc, nbytes):
        self.t = nc.alloc_sbuf_tensor("arena", [128, nbytes // 2], BF16)
        self.off = 0
        self.cap = nbytes
        self.peak = 0

    def alloc(self, shape, dtype):
        n = 1
        for s in shape[1:]:
            n *= s
        esz = 4 if dtype == F32 else 2
        nb = (n * esz + 31) // 32 * 32
        o = self.off
        self.off += nb
        self.peak = max(self.peak, self.off)
        assert self.off <= self.cap, ("SBUF arena overflow", self.off, self.cap)
        v = self.t[0:shape[0], o // 2:o // 2 + n * esz // 2]
        if dtype == F32:
            v = v.bitcast(F32)
        if len(shape) == 3:
            v = v.rearrange("p (a b) -> p a b", a=shape[1])
        elif len(shape) == 4:
            v = v.rearrange("p (a b c) -> p a b c", a=shape[1], b=shape[2])
        return v


def build_program(dbg=False):
    nc = bass.Bass("TRN2", target_bir_lowering=False)

    def din(name, shape):
        return nc.dram_tensor(name, list(shape), F32, kind="ExternalInput").ap()

    def dout(name, shape):
        return nc.dram_tensor(name, list(shape), F32, kind="ExternalOutput").ap()

    x_own = din("x_own", [1024, D]); x_pre = din("x_pre", [1024, D]); x_smp = din("x_smp", [16, D])
    p_own = din("p_own", [1024, 256]); p_smp = din("p_smp", [16, 256])
    st_ret = din("st_ret", [16, 4, 256, 256]); st_C = din("st_C", [16, 4, 256, 256])
    st_n = din("st_n", [16, 1024]); st_m = din("st_m", [16, 4])
    w_in = din("w_in", [D, 8200]); w_out = din("w_out", [D, D]); w_ff1 = din("w_ff1", [D, 8192])
    w_ff2 = din("w_ff2", [8192, D]); w_pe = din("w_pe", [256, D]); w_pg = din("w_pe_gate", [D, D])
    ln_d = din("ln_all", [4, D])
    bg_d = din("b_gate", [1, 8])
    gnw_d = din("gnw", [128, 16])
    c_sq = din("c_sq", [128, 6, 128])
    c_rmask = din("c_rmask", [128, 4, 128])
    c_small = din("c_small", [128, 48])
    c_id16 = din("c_id16", [128, 256])
    cs_own_d = din("cs_own", [128, 2, TOK]); cs_pre_d = din("cs_pre", [128, 2, 1024])

    y_own = dout("y_own", [1024, D]); y_smp = dout("y_smp", [16, D])
    o_retS_p = dout("retS_p", [4, 256, 256]); o_C_p = dout("C_p", [4, 256, 256])
    o_n_p = dout("n_p", [4, 256]); o_m_p = dout("m_p", [1, 4])
    o_retS_s = dout("retS_s", [16, 4, 256, 256]); o_C_s = dout("C_s", [16, 4, 256, 256])
    o_n_s = dout("n_s", [16, 1024]); o_m_s = dout("m_s", [16, 4])

    P = Prog(nc)
    AR = Arena(nc, 206 * 1024)
    PS = [nc.alloc_psum_tensor("ps%d" % i, [128, 512], F32) for i in range(5)]
    PSX = nc.alloc_psum_tensor("psx", [128, 512], F32)
    PB = [nc.alloc_psum_tensor("pb%d" % i, [128, 1024], BF16) for i in range(2)]
    st = dict(ps=0, pb=0, ev=0, wr=0)

    def bank():
        i = st["ps"]; st["ps"] = (i + 1) % 5
        return PS[i], "PS%d" % i

    def bbank():
        i = st["pb"]; st["pb"] = (i + 1) % 2
        return PB[i], "PB%d" % i

    def ev_eng():
        st["ev"] ^= 1
        return "dve" if st["ev"] else "act"

    def copy_op(eng, out, in_, r, w, scale=None):
        if eng == "act":
            if scale is None:
                P.op("act", lambda e: e.copy(out, in_), r=r, w=w)
            else:
                P.op("act", lambda e: e.mul(out, in_, scale), r=r, w=w)
        else:
            if scale is None:
                P.op(eng, lambda e: e.tensor_copy(out, in_), r=r, w=w)
            else:
                P.op(eng, lambda e: e.tensor_scalar(out, in_, scale, None, ALU.mult), r=r, w=w)

    SQ = AR.alloc([128, 6, 128], F32)
    IDF, TRI, NEGM, SEL127, ONES = (SQ[:, i, :] for i in range(5))
    IDB = AR.alloc([128, 128], BF16)
    RMASK = AR.alloc([128, 4, 128], F32)
    SM = AR.alloc([128, 48], F32)
    QD, QD2, KDEC = SM[:, 0:4], SM[:, 4:8], SM[:, 8:12]
    KPRE = SM[:, 12:44].rearrange("p (c h) -> p c h", c=8)
    FLAG = SM[:, 44:45]
    EPSG = SM[:, 45:46]
    EPSL = SM[:, 46:47]
    ID16 = AR.alloc([128, 16, 16], BF16)
    BG = AR.alloc([128, 8], F32)
    GNW = AR.alloc([128, 16], F32)
    NSM = AR.alloc([16, 1024], F32)
    NNEW = AR.alloc([16, 1024], F32)
    MS0 = AR.alloc([16, 4], F32)
    BARS = AR.alloc([128, 8], F32)

    cl = lambda out, in_, q="sp": P.dma(q, out, in_, w=["C"], sem=("C" if q == "sp" else "Cp"))
    cl(SQ, c_sq); cl(RMASK, c_rmask); cl(SM, c_small); cl(GNW, gnw_d)
    cl(BG, bass.AP(bg_d.tensor, 0, [[0, 128], [1, 8]]))
    cl(NSM, st_n); cl(MS0, st_m)
    cl(ID16.rearrange("p a b -> p (a b)"), c_id16, "pool")
    cl(IDB, c_sq[:, 0, :], "pool")
    P.barrier(lambda e: e.memset(BARS[:, 0:1], 0.0))

    mixt_off = AR.off
    MIXT = AR.alloc([128, 16, TOK], BF16)
    markA = AR.off
    XT = AR.alloc([128, 16, TOK], BF16)
    WR = [AR.alloc([128, 16, 256], BF16) for _ in range(4)]
    ovl = AR.off
    XIN = [AR.alloc([128, D], BF16) for _ in range(2)]
    AR.off = ovl
    ROT = [AR.alloc([128, 512], F32) for _ in range(4)]
    CS = AR.alloc([128, 2, TOK], F32)
    AR.off = max(AR.off, ovl + 2 * D * 4)
    QT = AR.alloc([128, 2, TOK], BF16)
    KT = AR.alloc([128, 2, TOK], BF16)
    KTOK = AR.alloc([128, 8, 256], BF16)
    VEXT = AR.alloc([128, 9, 260], BF16)
    GATE = AR.alloc([128, 9, 256], BF16)
    SINIT = AR.alloc([128, 8, 2, 260], F32)
    SF = [AR.alloc([128, 2, 260], F32) for _ in range(2)]
    SBF = [AR.alloc([128, 2, 260], BF16) for _ in range(2)]
    NSLOT = 3
    SST = [AR.alloc([128, 2, 256], F32) for _ in range(NSLOT)]
    SNB = [AR.alloc([128, 2, 256], BF16) for _ in range(NSLOT)]
    WG = AR.alloc([128, 16, 8], BF16)
    GP = AR.alloc([128, 9, 8], F32)
    IG = AR.alloc([128, 9, 4], F32)
    SP = AR.alloc([128, 9, 4], F32)
    CSA = AR.alloc([128, 2, 8, 4], F32)
    AA = AR.alloc([128, 8, 4], F32)
    LST = AR.alloc([128, 2, 8, 4], F32)
    MCH = AR.alloc([128, 9, 4], F32)
    MML = AR.alloc([128, 8, 4], F32)
    ECH = AR.alloc([128, 9, 4], F32)
    T32 = AR.alloc([128, 8, 4], F32)
    SWT = AR.alloc([128, 8, 4], F32)
    MM = AR.alloc([128, 8, 4], F32)
    NEGMM = AR.alloc([128, 8, 4], F32)
    IW = AR.alloc([128, 8, 4], F32)
    EMT = AR.alloc([128, 8, 4], F32)
    SD = AR.alloc([128, 8, 4], F32)
    MINIT = AR.alloc([128, 4], F32)
    TMPA = AR.alloc([128, 4, 128], F32)
    NSET = 3
    SETS = []
    for i in range(NSET):
        SETS.append(dict(id=i, DW=AR.alloc([128, 128], F32), SCM=AR.alloc([128, 128], BF16), SCMT=AR.alloc([128, 128], BF16),
                         INTRA=AR.alloc([128, 260], F32), NUMER=AR.alloc([128, 256], F32), YF=AR.alloc([128, 256], F32),
                         Y2=AR.alloc([128, 256], BF16), BNS=AR.alloc([128, 6], F32), BNA=AR.alloc([128, 2], F32), SC1=AR.alloc([128, 8], F32)))
    _save = AR.off
    AR.off = ovl
    for i in range(NSET, NSET + 3):
        SETS.append(dict(id=i, DW=AR.alloc([128, 128], F32), SCM=AR.alloc([128, 128], BF16), SCMT=AR.alloc([128, 128], BF16),
                         INTRA=AR.alloc([128, 260], F32), NUMER=AR.alloc([128, 256], F32), YF=AR.alloc([128, 256], F32),
                         Y2=AR.alloc([128, 256], BF16), BNS=AR.alloc([128, 6], F32), BNA=AR.alloc([128, 2], F32), SC1=AR.alloc([128, 8], F32)))
    assert AR.off <= ovl + 4 * 2048 + 2 * TOK * 4, (AR.off - ovl)
    AR.off = _save
    ALIAS = ["R0", "R1", "R2", "R3", "CS"]
    QM = AR.alloc([128, 2, 16, 16], BF16)
    KMB = [AR.alloc([16, 256], BF16) for _ in range(NSLOT)]
    KS = AR.alloc([16, 256], F32)
    KSD = AR.alloc([16, 256], F32)
    QS = AR.alloc([16, 256], F32)
    SMP = AR.alloc([16, 32], F32)
    DG = AR.alloc([16, 16, 4], F32)
    IWB = AR.alloc([128, 16, 4], F32)
    TQ = AR.alloc([16, 256], F32)

    w_in_v = w_in.rearrange("(dt p) c -> p dt c", p=128)

    def load_w(c0, ncols=256):
        s = st["wr"]; st["wr"] = (s + 1) % 4
        P.dma("pool", WR[s][:, :, 0:ncols], w_in_v[:, :, c0:c0 + ncols], w=["WR%d" % s], sem="WR%d" % s, cost=9.0)
        return WR[s], "WR%d" % s

    def xt_keys(t0, n):
        tiles = range(t0 // 128, (t0 + n - 1) // 128 + 1)
        return ["XT%d_%d" % (tt, g) for tt in tiles for g in range(4)]

    def build_xt(src, ntiles, smp_src=None):
        jobs = [(src[tt * 128:(tt + 1) * 128, :], 128, tt) for tt in range(ntiles)]
        if smp_src is not None:
            jobs.append((smp_src, 16, 8))
        for i, (sap, np_, tt) in enumerate(jobs):
            xin = XIN[i % 2]; xk = "XIN%d" % (i % 2)
            P.dma("pool", xin[0:np_, :], sap, w=[xk], sem=xk, cost=5.5)
            for g in range(4):
                ps, pk = bank()
                psb = ps[:].bitcast(BF16)

                def tr(e, psb=psb, xin=xin, g=g, np_=np_):
                    for q in range(4):
                        dt = g * 4 + q
                        ins = e.transpose(psb[:, q * 128:q * 128 + np_], xin[0:np_, dt * 128:(dt + 1) * 128], IDB[0:np_, 0:np_])
                    return ins
                P.op("pe", tr, r=[xk], w=[pk])
                out = XT[:, g * 4:(g + 1) * 4, tt * 128:tt * 128 + np_]
                inn = psb[:, 0:512].rearrange("p (q t) -> p q t", q=4)[:, :, 0:np_]
                copy_op(ev_eng(), out, inn, r=[pk], w=["XT%d_%d" % (tt, g)])

    def proj_fm(W, wk, ct, t0, n):
        ps, pk = bank()

        def mm(e):
            for dt in range(16):
                ins = e.matmul(ps[:, 0:n], W[:, dt, ct * 128:(ct + 1) * 128], XT[:, dt, t0:t0 + n],
                               start=(dt == 0), stop=(dt == 15))
            return ins
        P.op("pe", mm, r=[wk] + xt_keys(t0, n), w=[pk], cost=0.3 + 16 * n / 2400.0)
        return ps, pk

    def rotary(W, wk, DST, dkey, t0, n):
        a, ak = proj_fm(W, wk, 0, t0, n)
        b, bk = proj_fm(W, wk, 1, t0, n)
        cos, sin = CS[:, 0, t0:t0 + n], CS[:, 1, t0:t0 + n]
        r0, r1, r2, r3 = (ROT[i][:, 0:n] for i in range(4))
        P.op("dve", lambda e: e.tensor_tensor(r0, a[:, 0:n], cos, ALU.mult), r=[ak, "CS"], w=["R0"])
        P.op("dve", lambda e: e.tensor_tensor(r1, b[:, 0:n], sin, ALU.mult), r=[bk, "CS"], w=["R1"])
        P.op("dve", lambda e: e.tensor_tensor(r2, a[:, 0:n], sin, ALU.mult), r=[ak, "CS"], w=["R2"])
        P.op("dve", lambda e: e.tensor_tensor(r3, b[:, 0:n], cos, ALU.mult), r=[bk, "CS"], w=["R3"])
        P.op("pool", lambda e: e.tensor_tensor(DST[:, 0, t0:t0 + n], r0, r1, ALU.subtract), r=["R0", "R1"], w=[dkey + "0_%d" % t0])
        P.op("pool", lambda e: e.tensor_tensor(DST[:, 1, t0:t0 + n], r2, r3, ALU.add), r=["R2", "R3"], w=[dkey + "1_%d" % t0])

    def plain_fm(W, wk, DST, dkey, t0, n, scale):
        for ct in range(2):
            a, ak = proj_fm(W, wk, ct, t0, n)
            copy_op(ev_eng(), DST[:, ct, t0:t0 + n], a[:, 0:n], r=[ak], w=[dkey + "%d_%d" % (ct, t0)], scale=scale)

    def blk_keys(dkey, t0, n):
        out = []
        for b0 in (0, 512, 1024):
            bn = 512 if b0 < 1024 else 16
            if t0 < b0 + bn and t0 + n > b0:
                out += [dkey + "0_%d" % b0, dkey + "1_%d" % b0]
        return out

    def proj_tm(W, wk, tt, np_, ncols=256):
        ps, pk = bank()

        def mm(e):
            for dt in range(16):
                ins = e.matmul(ps[0:np_, 0:ncols], XT[:, dt, tt * 128:tt * 128 + np_], W[:, dt, 0:ncols],
                               start=(dt == 0), stop=(dt == 15))
            return ins
        P.op("pe", mm, r=[wk] + xt_keys(tt * 128, np_), w=[pk], cost=0.3 + 16 * ncols / 2400.0)
        return ps, pk

    def make_ktok(c, scale_ap):
        pb, pk = bbank()

        def tr(e):
            for j in range(2):
                ins = e.transpose(pb[:, j * 128:(j + 1) * 128], KT[:, j, c * 128:(c + 1) * 128], IDB)
            return ins
        P.op("pe", tr, r=blk_keys("KT", c * 128, 128), w=[pk])
        copy_op(ev_eng(), KTOK[:, c, :], pb[:, 0:256], r=[pk, "SWT", "C"], w=["KTOK%d" % c], scale=scale_ap)

    def gates(ntiles_full, with_sample):
        ps, pk = bank()
        tiles = [(tt, 128) for tt in range(ntiles_full)] + ([(8, 16)] if with_sample else [])

        def mm(e):
            for tt, np_ in tiles:
                for dt in range(16):
                    ins = e.matmul(ps[0:np_, tt * 8:(tt + 1) * 8], XT[:, dt, tt * 128:tt * 128 + np_], WG[:, dt, :],
                                   start=(dt == 0), stop=(dt == 15))
            return ins
        P.op("pe", mm, r=["WG"] + xt_keys(0, TOK if with_sample else 1024), w=[pk])
        nt = len(tiles)
        for tt, np_ in tiles:
            pass
        full = ps[:, 0:8 * 8].rearrange("p (t g) -> p t g", t=8)
        P.op("dve", lambda e: e.tensor_tensor(IG[:, 0:8, :], full[:, :, 0:4], BG[:, 0:4].unsqueeze(1).to_broadcast([128, 8, 4]), ALU.add), r=[pk], w=["IG"])
        P.op("dve", lambda e: e.tensor_tensor(SP[:, 0:8, :], full[:, :, 4:8], BG[:, 4:8].unsqueeze(1).to_broadcast([128, 8, 4]), ALU.add), r=[pk], w=["SP"])
        if with_sample:
            P.op("dve", lambda e: e.tensor_tensor(IG[0:16, 8, :], ps[0:16, 64:68], BG[0:16, 0:4], ALU.add), r=[pk], w=["IGs"])
            P.op("dve", lambda e: e.tensor_tensor(SP[0:16, 8, :], ps[0:16, 68:72], BG[0:16, 4:8], ALU.add), r=[pk], w=["SPs"])
            P.op("act", lambda e: e.activation(SP[0:16, 8, :], SP[0:16, 8, :], AF.Exp, scale=-1.0), r=["SPs"], w=["SPs"])
            P.op("act", lambda e: e.activation(SP[0:16, 8, :], SP[0:16, 8, :], AF.Ln, bias=1.0), r=["SPs"], w=["SPs"])
        P.op("act", lambda e: e.activation(SP[:, 0:8, :], SP[:, 0:8, :], AF.Exp, scale=-1.0), r=["SP"], w=["SP"])
        P.op("act", lambda e: e.activation(SP[:, 0:8, :], SP[:, 0:8, :], AF.Ln, bias=1.0), r=["SP"], w=["SP"])
        ps2, pk2 = bank()
        P.op("pe", lambda e: e.matmul(ps2[:, 0:32], TRI, SP[:, 0:8, :].rearrange("p c h -> p (c h)"), start=True, stop=True), r=["SP", "C"], w=[pk2])
        cs3 = ps2[:, 0:32].rearrange("p (c h) -> p c h", c=8)
        P.op("dve", lambda e: e.tensor_copy(CSA[:, 0, :, :], cs3), r=[pk2], w=["CSA0"])
        P.op("dve", lambda e: e.tensor_tensor(AA[:], IG[:, 0:8, :], CSA[:, 0, :, :], ALU.add), r=["IG", "CSA0"], w=["AA"])
        for c in range(8):
            psa, pka = bank()

            def mm2(e, psa=psa, c=c):
                for h in range(4):
                    ins = e.matmul(psa[:, h * 128:(h + 1) * 128], AA[:, c, h:h + 1].to_broadcast([128, 128]), IDF, start=True, stop=True)
                return ins
            P.op("pe", mm2, r=["AA", "C"], w=[pka])
            P.op("dve", lambda e, psa=psa: e.tensor_tensor(TMPA[:], psa[:].rearrange("p (h s) -> p h s", h=4),
                                                           NEGM.unsqueeze(1).to_broadcast([128, 4, 128]), ALU.add), r=[pka, "C"], w=["TMPA"])
            P.op("dve", lambda e, c=c: e.tensor_reduce(CSA[:, 1, c, :], TMPA[:], AX.X, ALU.max), r=["TMPA"], w=["CSA1_%d" % c])
        ps3, pk3 = bank()
        P.op("pe", lambda e: e.matmul(ps3[:, 0:64], SEL127, CSA[:].rearrange("p a c h -> p (a c h)"), start=True, stop=True),
             r=["CSA0", "C"] + ["CSA1_%d" % c for c in range(8)], w=[pk3])
        P.op("dve", lambda e: e.tensor_copy(LST[:].rearrange("p a c h -> p (a c h)"), ps3[:, 0:64]), r=[pk3], w=["LST"])

    def m_chain():
        for c in range(8):
            P.op("dve", lambda e, c=c: e.tensor_tensor(MML[:, c, :], MCH[:, c, :], LST[:, 1, c, :], ALU.max), r=["MCH", "LST"], w=["MML"])
            P.op("dve", lambda e, c=c: e.tensor_tensor(MCH[:, c + 1, :], MML[:, c, :], LST[:, 0, c, :], ALU.subtract), r=["MML", "LST"], w=["MCH"])

    P.dma("pool", WG[:], w_in_v[:, :, 8192:8200], w=["WG"], sem="WG")
    build_xt(x_pre, 8)
    P.dma("sp", CS[:, :, 0:1024], cs_pre_d, w=["CS", "XIN0", "XIN1"], sem="CS")
    P.op("pool", lambda e: e.memset(VEXT[:, :, 256:257], 1.0), w=["VONE"])
    gates(8, False)
    P.op("dve", lambda e: e.memset(MCH[:, 0, :], 0.0), w=["MCH"])
    m_chain()
    P.op("dve", lambda e: e.tensor_tensor(T32[:], MCH[:, 0:8, :], MML[:], ALU.subtract), r=["MCH", "MML"], w=["T32"])
    P.op("dve", lambda e: e.memset(ECH[:, 7, :], 0.0), w=["ECH"])
    for c in range(6, -1, -1):
        P.op("dve", lambda e, c=c: e.tensor_tensor(ECH[:, c, :], ECH[:, c + 1, :], T32[:, c + 1, :], ALU.add), r=["ECH", "T32"], w=["ECH"])
    P.op("dve", lambda e: e.tensor_tensor(T32[:], ECH[:, 0:8, :], MML[:], ALU.subtract), r=["ECH", "MML", "T32"], w=["T32"])
    P.op("dve", lambda e: e.tensor_tensor(T32[:], T32[:], AA[:], ALU.add), r=["T32", "AA"], w=["T32"])
    P.op("act", lambda e: e.activation(SWT[:], T32[:], AF.Exp), r=["T32"], w=["SWT"])
    P.op("dve", lambda e: e.tensor_scalar(MINIT[:], MCH[:, 8, :], FLAG, None, ALU.mult), r=["MCH", "C"], w=["MINIT"])

    for hh in range(8):
        ret = hh < 4
        h = hh % 4
        if hh == 0:
            nxtP = (load_w(1024), load_w(2048))
        (Wk, wkk), (Wv, wvk) = nxtP
        for t0 in (0, 512):
            if ret:
                rotary(Wk, wkk, KT, "KT", t0, 512)
            else:
                plain_fm(Wk, wkk, KT, "KT", t0, 512, 0.0625)
        for tt in range(8):
            ps, pk = proj_tm(Wv, wvk, tt, 128)
            copy_op(ev_eng(), VEXT[:, tt, 0:256], ps[:, 0:256], r=[pk], w=["VEXT%d" % tt])
        if hh < 7:
            r2, h2 = (hh + 1) < 4, (hh + 1) % 4
            nxtP = (load_w((1024 if r2 else 5120) + 256 * h2), load_w((2048 if r2 else 6144) + 256 * h2))
        for c in range(8):
            make_ktok(c, KPRE[:, c, h:h + 1] if ret else SWT[:, c, h:h + 1])
        for j in range(2):
            ps, pk = bank()

            def acc(e, ps=ps, j=j):
                for c in range(8):
                    ins = e.matmul(ps[:, 0:257], KTOK[:, c, j * 128:(j + 1) * 128], VEXT[:, c, 0:257], start=(c == 0), stop=(c == 7))
                return ins
            P.op("pe", acc, r=["KTOK%d" % c for c in range(8)] + ["VEXT%d" % c for c in range(8)] + ["VONE"], w=[pk])
            P.op("dve", lambda e, ps=ps, j=j, hh=hh: e.tensor_scalar(SINIT[:, hh, j, 0:257], ps[:, 0:257], FLAG, None, ALU.mult),
                 r=[pk, "C"], w=["SINIT%d_%d" % (hh, j)])

    P.barrier(lambda e: e.memset(BARS[:, 0:1], 0.0))

    build_xt(x_own, 8, x_smp)
    P.dma("sp", CS[:], cs_own_d, w=["CS", "XIN0", "XIN1"], sem="CS")
    gates(8, True)
    P.op("dve", lambda e: e.tensor_copy(MCH[:, 0, :], MINIT[:]), r=["MINIT"], w=["MCH"])
    m_chain()
    P.dma("sp", o_m_p, MCH[0:1, 8, :], r=["MCH"], sem="st_mp")
    P.op("dve", lambda e: e.tensor_tensor(MM[:], MCH[:, 0:8, :], CSA[:, 1, :, :], ALU.max), r=["MCH"] + ["CSA1_%d" % c for c in range(8)], w=["MM"])
    P.op("dve", lambda e: e.tensor_scalar(NEGMM[:], MM[:], -1.0, None, ALU.mult), r=["MM"], w=["NEGMM"])
    P.op("dve", lambda e: e.tensor_tensor(T32[:], MCH[:, 0:8, :], MM[:], ALU.subtract), r=["MCH", "MM"], w=["T32"])
    P.op("act", lambda e: e.activation(IW[:], T32[:], AF.Exp), r=["T32"], w=["IW"])
    P.op("dve", lambda e: e.tensor_tensor(T32[:], CSA[:, 0, :, :], MM[:], ALU.subtract), r=["CSA0", "MM", "IW"], w=["T32"])
    P.op("act", lambda e: e.activation(EMT[:], T32[:], AF.Exp), r=["T32"], w=["EMT"])
    P.op("dve", lambda e: e.tensor_tensor(T32[:], AA[:], MML[:], ALU.subtract), r=["AA", "MML", "EMT"], w=["T32"])
    P.op("act", lambda e: e.activation(SWT[:], T32[:], AF.Exp), r=["T32"], w=["SWT"])
    P.op("dve", lambda e: e.tensor_tensor(T32[:], MCH[:, 0:8, :], MML[:], ALU.subtract), r=["MCH", "MML", "SWT"], w=["T32"])
    P.op("act", lambda e: e.activation(SD[:], T32[:], AF.Exp), r=["T32"], w=["SD"])
    s_mt, s_dw, s_iw, s_emt, s_den, s_r, s_t = (SMP[:, i * 4:(i + 1) * 4] for i in range(7))
    P.op("dve", lambda e: e.tensor_tensor(s_t, MS0[:], SP[0:16, 8, :], ALU.subtract), r=["C", "SPs"], w=["s_t"])
    P.op("dve", lambda e: e.tensor_tensor(s_mt, s_t, IG[0:16, 8, :], ALU.max), r=["s_t", "IGs"], w=["s_mt"])
    P.op("dve", lambda e: e.tensor_tensor(s_t, s_t, s_mt, ALU.subtract), r=["s_t", "s_mt"], w=["s_t"])
    P.op("act", lambda e: e.activation(s_iw, s_t, AF.Exp), r=["s_t"], w=["s_iw"])
    P.op("dve", lambda e: e.tensor_tensor(s_t, IG[0:16, 8, :], s_mt, ALU.subtract), r=["s_iw", "IGs", "s_mt"], w=["s_t"])
    P.op("act", lambda e: e.activation(s_dw, s_t, AF.Exp), r=["s_t"], w=["s_dw"])
    P.op("act", lambda e: e.activation(s_emt, s_mt, AF.Exp, scale=-1.0), r=["s_mt"], w=["s_emt"])
    P.dma("sp", o_m_s, s_mt, r=["s_mt"], sem="st_ms")
    P.op("dve", lambda e: e.tensor_tensor(DG[:], s_iw.unsqueeze(1).to_broadcast([16, 16, 4]),
                                          IDF[0:16, 0:16].unsqueeze(2).to_broadcast([16, 16, 4]), ALU.mult), r=["s_iw", "C"], w=["DG"])
    psw, pkw = bank()
    P.op("pe", lambda e: e.matmul(psw[:, 0:64], ONES[0:16, :], DG[:].rearrange("p b h -> p (b h)"), start=True, stop=True), r=["DG", "C"], w=[pkw])
    P.op("dve", lambda e: e.tensor_copy(IWB[:].rearrange("p b h -> p (b h)"), psw[:, 0:64]), r=[pkw], w=["IWB"])

    def norm_gate_out(S, src, skey, np_, fac, fac2, fkeys, gate, gkey, et0, t0):
        i = S["id"]
        BNS, BNA, YF, Y2 = S["BNS"], S["BNA"], S["YF"], S["Y2"]
        kb, ka, ks2, ky, ky2 = "BNS%d" % i, "BNA%d" % i, "s2_%d" % i, "YF%d" % i, "Y2_%d" % i
        P.op("dve", lambda e: e.bn_stats(BNS[0:np_, :], src), r=[skey], w=[kb])
        P.op("dve", lambda e: e.bn_aggr(BNA[0:np_, :], BNS[0:np_, :]), r=[kb], w=[ka])
        s2 = S["SC1"][0:np_, 0:1]
        if fac is None:
            P.op("act", lambda e: e.activation(s2, BNA[0:np_, 1:2], AF.Ln, bias=EPSG[0:np_, :]), r=[ka], w=[ks2])
            P.op("act", lambda e: e.activation(s2, s2, AF.Exp, scale=-0.5), r=[ks2], w=[ks2])
        else:
            P.op("act", lambda e: e.activation(s2, BNA[0:np_, 1:2], AF.Ln, bias=EPSG[0:np_, :], scale=fac2), r=[ka] + fkeys, w=[ks2])
            P.op("act", lambda e: e.activation(s2, s2, AF.Exp, scale=-0.5), r=[ks2], w=[ks2])
            P.op("dve", lambda e: e.tensor_scalar(s2, s2, fac, None, ALU.mult), r=[ks2] + fkeys, w=[ks2])
        P.op("dve", lambda e: e.tensor_scalar(YF[0:np_, :], src, BNA[0:np_, 0:1], s2, ALU.subtract, ALU.mult), r=[skey, ka, ks2], w=[ky])
        P.op("pool", lambda e: e.tensor_tensor(Y2[0:np_, :], YF[0:np_, :], gate, ALU.mult), r=[ky, gkey], w=[ky2])
        pb, pk = bbank()

        def tr(e):
            for j in range(2):
                ins = e.transpose(pb[:, j * 128:j * 128 + np_], Y2[0:np_, j * 128:(j + 1) * 128], IDB[0:np_, 0:np_])
            return ins
        P.op("pe", tr, r=[ky2], w=[pk])
        for j in range(2):
            P.op("act", lambda e, j=j: e.mul(MIXT[:, et0 + j, t0:t0 + np_], pb[:, j * 128:j * 128 + np_], GNW[:, et0 + j:et0 + j + 1]),
                 r=[pk, "C"], w=["MIXT%d_%d" % (et0 + j, t0)])

    smp_ctr = [0]

    for hh in range(8):
        ret = hh < 4
        h = hh % 4
        base = 0 if ret else 4096
        if hh == 0:
            nxtO = [load_w(q * 1024) for q in range(4)]
        (Wq, wqk), (Wk, wkk), (Wv, wvk), (Wg, wgk) = nxtO
        for (W, wk, DST, dk, sc) in ((Wq, wqk, QT, "QT", None), (Wk, wkk, KT, "KT", 0.0625)):
            for t0, n in ((0, 512), (512, 512), (1024, 16)):
                if ret:
                    rotary(W, wk, DST, dk, t0, n)
                else:
                    plain_fm(W, wk, DST, dk, t0, n, sc)
        for tt in range(9):
            np_ = 128 if tt < 8 else 16
            ps, pk = proj_tm(Wv, wvk, tt, np_)
            copy_op("dve", VEXT[0:np_, tt, 0:256], ps[0:np_, 0:256], r=[pk], w=["VEXT%d" % tt])
            ps, pk = proj_tm(Wg, wgk, tt, np_)
            gfn = AF.Silu if ret else AF.Sigmoid
            P.op("act", lambda e, ps=ps, tt=tt, np_=np_, gfn=gfn: e.activation(GATE[0:np_, tt, :], ps[0:np_, 0:256], gfn),
                 r=[pk], w=["GATE%d" % tt])
        if hh < 7:
            b2 = 0 if (hh + 1) < 4 else 4096
            nxtO = [load_w(b2 + q * 1024 + 256 * ((hh + 1) % 4)) for q in range(4)]
        for c in range(8):
            make_ktok(c, KDEC[:, h:h + 1] if ret else SWT[:, c, h:h + 1])
        for j in range(2):
            P.op("pool", lambda e, j=j, hh=hh: e.tensor_copy(SF[0][:, j, 0:257], SINIT[:, hh, j, 0:257]), r=["SINIT%d_%d" % (hh, j)], w=["SF0_%d" % j])
            P.op("act", lambda e, j=j: e.copy(SBF[0][:, j, 0:257], SF[0][:, j, 0:257]), r=["SF0_%d" % j], w=["SBF0_%d" % j])
        skeys = blk_keys("QT", 1024, 16)
        P.op("dve", lambda e: e.tensor_tensor(QM[:], QT[:, :, 1024:1040].unsqueeze(3).to_broadcast([128, 2, 16, 16]),
                                              ID16[:].unsqueeze(1).to_broadcast([128, 2, 16, 16]), ALU.mult), r=skeys + ["C"], w=["QM"])
        pb, pkb = bbank()

        def trs(e, pb=pb):
            for j in range(2):
                e.transpose(pb[0:16, j * 128:(j + 1) * 128], KT[:, j, 1024:1040], IDB)
            for j in range(2):
                ins = e.transpose(pb[0:16, 256 + j * 128:256 + (j + 1) * 128], QT[:, j, 1024:1040], IDB)
            return ins
        P.op("pe", trs, r=skeys + blk_keys("KT", 1024, 16), w=[pkb])
        if ret:
            P.op("dve", lambda e, pb=pb: e.tensor_scalar(KSD[:], pb[0:16, 0:256], 0.0625, None, ALU.mult), r=[pkb], w=["KSD"])
        else:
            P.op("dve", lambda e, pb=pb, h=h: e.tensor_scalar(KSD[:], pb[0:16, 0:256], s_dw[:, h:h + 1], None, ALU.mult), r=[pkb, "s_dw"], w=["KSD"])
            P.op("dve", lambda e, pb=pb: e.tensor_copy(QS[:], pb[0:16, 256:512]), r=[pkb], w=["QS"])
            P.op("dve", lambda e, h=h: e.scalar_tensor_tensor(NNEW[:, h * 256:(h + 1) * 256], NSM[:, h * 256:(h + 1) * 256], s_iw[:, h:h + 1], KSD[:], ALU.mult, ALU.add),
                 r=["C", "s_iw", "KSD"], w=["NNEW%d" % h])
            P.op("dve", lambda e, h=h: e.tensor_tensor(TQ[:], QS[:], NNEW[:, h * 256:(h + 1) * 256], ALU.mult), r=["QS", "NNEW%d" % h], w=["TQ"])
            P.op("dve", lambda e, h=h: e.tensor_reduce(s_den[:, h:h + 1], TQ[:], AX.X, ALU.add), r=["TQ"], w=["s_den%d" % h])
            P.op("dve", lambda e, h=h: e.tensor_scalar(s_r[:, h:h + 1], s_den[:, h:h + 1], -1.0, None, ALU.mult), r=["s_den%d" % h], w=["s_r%d" % h])
            P.op("dve", lambda e, h=h: e.tensor_tensor(s_den[:, h:h + 1], s_den[:, h:h + 1], s_r[:, h:h + 1], ALU.max), r=["s_den%d" % h, "s_r%d" % h], w=["s_den%d" % h])
            P.op("dve", lambda e, h=h: e.tensor_tensor(s_den[:, h:h + 1], s_den[:, h:h + 1], s_emt[:, h:h + 1], ALU.max), r=["s_den%d" % h, "s_emt"], w=["s_den%d" % h])
            P.op("dve", lambda e, h=h: e.reciprocal(s_r[:, h:h + 1], s_den[:, h:h + 1]), r=["s_den%d" % h], w=["s_r%d" % h])
            P.op("dve", lambda e, h=h: e.tensor_tensor(s_t[:, h:h + 1], s_r[:, h:h + 1], s_r[:, h:h + 1], ALU.mult), r=["s_r%d" % h], w=["s_t%d" % h])
        pos_, pkos = PSX, "PSX"
        src_state = st_ret if ret else st_C
        dst_state = o_retS_s if ret else o_C_s

        def sample_token(b):
            sl = smp_ctr[0] % NSLOT
            smp_ctr[0] += 1
            P.dma("sp", SST[sl][:], src_state[b, h].rearrange("(j p) v -> p j v", p=128), w=["SST%d" % sl], sem="SST%d" % sl)
            P.op("dve", lambda e: e.tensor_scalar(KMB[sl][:], KSD[:], IDF[0:16, b:b + 1], None, ALU.mult), r=["KSD", "C"], w=["KMB%d" % sl])
            pst, pks = bank()

            def ou_mm(e):
                for j in range(2):
                    ins = e.matmul(pst[:, j * 256:(j + 1) * 256], KMB[sl][:, j * 128:(j + 1) * 128], VEXT[0:16, 8, 0:256], start=True, stop=True)
                return ins
            P.op("pe", ou_mm, r=["KMB%d" % sl, "VEXT8"], w=[pks])
            scal = G[h] if ret else IWB[:, b, h:h + 1]
            flat = SST[sl][:].rearrange("p j v -> p (j v)")
            P.op("dve", lambda e: e.scalar_tensor_tensor(flat, flat, scal, pst[:, 0:512], ALU.mult, ALU.add),
                 r=[pks, "SST%d" % sl, "IWB"], w=["SST%d" % sl])
            P.op("act", lambda e: e.copy(SNB[sl][:], SST[sl][:]), r=["SST%d" % sl], w=["SNB%d" % sl])
            P.dma("sp", dst_state[b, h].rearrange("(j p) v -> p j v", p=128), SST[sl][:], r=["SST%d" % sl], sem="st_SST%d" % sl)

            def os_mm(e):
                for j in range(2):
                    ins = e.matmul(pos_[0:16, 0:256], QM[:, j, b, :], SNB[sl][:, j, :], start=(b == 0 and j == 0), stop=(b == 15 and j == 1))
                return ins
            P.op("pe", os_mm, r=["QM", "SNB%d" % sl], w=[pkos])

        for c in range(8):
            S = SETS[c % (NSET if ret else NSET + 3)]
            sid = S["id"]
            cur, nxt = c % 2, (c + 1) % 2
            cb = slice(c * 128, (c + 1) * 128)
            qk = blk_keys("QT", c * 128, 128); kk = blk_keys("KT", c * 128, 128)
            pst, pks = bank()

            def st_mm(e, pst=pst, c=c):
                for j in range(2):
                    ins = e.matmul(pst[:, j * 256:j * 256 + 256], KTOK[:, c, j * 128:(j + 1) * 128], VEXT[:, c, 0:256], start=True, stop=True)
                return ins
            P.op("pe", st_mm, r=["KTOK%d" % c, "VEXT%d" % c], w=[pks])
            if not ret:
                pn, pkn = bank()

                def n_mm(e, pn=pn, c=c):
                    for j in range(2):
                        ins = e.matmul(pn[:, j:j + 1], KTOK[:, c, j * 128:(j + 1) * 128], VEXT[:, c, 256:257], start=True, stop=True)
                    return ins
                P.op("pe", n_mm, r=["KTOK%d" % c, "VONE"], w=[pkn])
            for j in range(2):
                scal = G[h] ** 128 if ret else SD[:, c, h:h + 1]
                P.op("dve", lambda e, pst=pst, j=j, scal=scal, cur=cur, nxt=nxt: e.scalar_tensor_tensor(SF[nxt][:, j, 0:256], SF[cur][:, j, 0:256], scal, pst[:, j * 256:(j + 1) * 256], ALU.mult, ALU.add),
                     r=[pks, "SF%d_%d" % (cur, j), "SD"], w=["SF%d_%d" % (nxt, j)])
                if not ret:
                    P.op("dve", lambda e, pn=pn, j=j, c=c, h=h, cur=cur, nxt=nxt: e.scalar_tensor_tensor(SF[nxt][:, j, 256:257], SF[cur][:, j, 256:257], SD[:, c, h:h + 1], pn[:, j:j + 1], ALU.mult, ALU.add),
                         r=[pkn, "SF%d_%d" % (cur, j), "SD"], w=["SF%d_%d" % (nxt, j)])
                P.op("act", lambda e, j=j, nxt=nxt, nc_=(256 if ret else 257): e.copy(SBF[nxt][:, j, 0:nc_], SF[nxt][:, j, 0:nc_]), r=["SF%d_%d" % (nxt, j)], w=["SBF%d_%d" % (nxt, j)])
            SCM, kscm = S["SCM"], "SCM%d" % sid
            sbk = ["SBF%d_0" % cur, "SBF%d_1" % cur]
            if ret:
                ps, pk = bank()

                def sc_mm(e, ps=ps, cb=cb):
                    for j in range(2):
                        ins = e.matmul(ps[:, 0:128], KT[:, j, cb], QT[:, j, cb], start=(j == 0), stop=(j == 1))
                    return ins
                P.op("pe", sc_mm, r=qk + kk, w=[pk])
                P.op("dve", lambda e, ps=ps, h=h, SCM=SCM: e.tensor_tensor(SCM[:], ps[:, 0:128], RMASK[:, h, :], ALU.mult), r=[pk, "C"], w=[kscm])
                po, pko = bank()

                def o_mm(e, po=po, cb=cb, c=c, SCM=SCM, cur=cur):
                    e.matmul(po[:, 0:256], SCM[:], VEXT[:, c, 0:256], start=True, stop=False)
                    for j in range(2):
                        ins = e.matmul(po[:, 0:256], QT[:, j, cb], SBF[cur][:, j, 0:256], start=False, stop=(j == 1))
                    return ins
                P.op("pe", o_mm, r=[kscm, "VEXT%d" % c] + sbk + qk, w=[pko])
                OSB, kosb = S["NUMER"], "NUMER%d" % sid
                P.op("act", lambda e, po=po, OSB=OSB: e.copy(OSB[:], po[:, 0:256]), r=[pko], w=[kosb])
                norm_gate_out(S, OSB[:], kosb, 128, QD[:, h:h + 1], QD2[:, h:h + 1], ["C"], GATE[:, c, :], "GATE%d" % c, hh * 2, c * 128)
            else:
                DW, kdw = S["DW"], "DW%d" % sid
                SCMT, kscmt = S["SCMT"], "SCMT%d" % sid
                INTRA, kin = S["INTRA"], "INTRA%d" % sid
                NUMER, knu = S["NUMER"], "NUMER%d" % sid
                psa, pka = bank()
                P.op("pe", lambda e, psa=psa, c=c, h=h: e.matmul(psa[:, 0:128], AA[:, c, h:h + 1].to_broadcast([128, 128]), IDF, start=True, stop=True),
                     r=["AA", "C"], w=[pka])
                P.op("dve", lambda e, psa=psa, DW=DW: e.tensor_tensor(DW[:], psa[:, 0:128], NEGM, ALU.add), r=[pka, "C"], w=[kdw] + (ALIAS if sid >= NSET else []))
                P.op("act", lambda e, c=c, h=h, DW=DW: e.activation(DW[:], DW[:], AF.Exp, bias=NEGMM[:, c, h:h + 1]), r=[kdw, "NEGMM"], w=[kdw])
                ps, pk = bank()

                def sc_mm(e, ps=ps, cb=cb):
                    for j in range(2):
                        ins = e.matmul(ps[:, 0:128], QT[:, j, cb], KT[:, j, cb], start=(j == 0), stop=(j == 1))
                    return ins
                P.op("pe", sc_mm, r=qk + kk, w=[pk])
                P.op("dve", lambda e, ps=ps, SCM=SCM, DW=DW: e.tensor_tensor(SCM[:], ps[:, 0:128], DW[:], ALU.mult), r=[pk, kdw], w=[kscm])
                pb, pkb = bbank()
                P.op("pe", lambda e, pb=pb, SCM=SCM: e.transpose(pb[:, 0:128], SCM[:], IDB), r=[kscm], w=[pkb])
                copy_op("act", SCMT[:], pb[:, 0:128], r=[pkb], w=[kscmt])
                pi, pki = bank()
                P.op("pe", lambda e, pi=pi, c=c, SCMT=SCMT: e.matmul(pi[:, 0:257], SCMT[:], VEXT[:, c, 0:257], start=True, stop=True), r=[kscmt, "VEXT%d" % c, "VONE"], w=[pki])
                pe_, pke = bank()

                def in_mm(e, pe_=pe_, cb=cb, cur=cur):
                    for j in range(2):
                        ins = e.matmul(pe_[:, 0:257], QT[:, j, cb], SBF[cur][:, j, 0:257], start=(j == 0), stop=(j == 1))
                    return ins
                P.op("pe", in_mm, r=qk + sbk + [kscmt], w=[pke])
                copy_op("act", INTRA[:, 0:257], pi[:, 0:257], r=[pki], w=[kin])
                iw = IW[:, c, h:h + 1]
                SC1 = S["SC1"]
                den, rr, rr2 = SC1[:, 1:2], SC1[:, 2:3], SC1[:, 3:4]
                kd, kr, kr2 = "den%d" % sid, "rr%d" % sid, "rr2_%d" % sid
                P.op("dve", lambda e, pe_=pe_, iw=iw, den=den, INTRA=INTRA: e.scalar_tensor_tensor(den, pe_[:, 256:257], iw, INTRA[:, 256:257], ALU.mult, ALU.add), r=[pke, kin, "IW"], w=[kd])
                P.op("dve", lambda e, den=den, rr=rr: e.tensor_scalar(rr, den, -1.0, None, ALU.mult), r=[kd], w=[kr])
                P.op("dve", lambda e, den=den, rr=rr: e.tensor_tensor(den, den, rr, ALU.max), r=[kd, kr], w=[kd])
                P.op("dve", lambda e, c=c, h=h, den=den: e.tensor_tensor(den, den, EMT[:, c, h:h + 1], ALU.max), r=[kd, "EMT"], w=[kd])
                P.op("dve", lambda e, den=den, rr=rr: e.reciprocal(rr, den), r=[kd], w=[kr])
                P.op("dve", lambda e, rr=rr, rr2=rr2: e.tensor_tensor(rr2, rr, rr, ALU.mult), r=[kr], w=[kr2])
                P.op("dve", lambda e, pe_=pe_, iw=iw, NUMER=NUMER, INTRA=INTRA: e.scalar_tensor_tensor(NUMER[:], pe_[:, 0:256], iw, INTRA[:, 0:256], ALU.mult, ALU.add), r=[pke, kin, "IW"], w=[knu])
                norm_gate_out(S, NUMER[:], knu, 128, rr, rr2, [kr, kr2], GATE[:, c, :], "GATE%d" % c, hh * 2, c * 128)
            sample_token(2 * c)
            sample_token(2 * c + 1)
        dst = (o_retS_p if ret else o_C_p)[h].rearrange("(j p) v -> p j v", p=128)
        P.dma("sp", dst, SF[0][:, :, 0:256], r=["SF0_0", "SF0_1"], sem="st_SF")
        if not ret:
            for j in range(2):
                P.dma("sp", bass.AP(o_n_p.tensor, h * 256 + j * 128, [[1, 128], [1, 1]]), SF[0][:, j, 256:257], r=["SF0_0", "SF0_1"], sem="st_SF")
        S = SETS[0]
        if ret:
            norm_gate_out(S, pos_[0:16, 0:256], pkos, 16, None, None, [], GATE[0:16, 8, :], "GATE8", hh * 2, 1024)
        else:
            norm_gate_out(S, pos_[0:16, 0:256], pkos, 16, s_r[:, h:h + 1], s_t[:, h:h + 1], ["s_r%d" % h, "s_t%d" % h], GATE[0:16, 8, :], "GATE8", hh * 2, 1024)
    P.dma("sp", o_n_s, NNEW[:], r=["NNEW%d" % h for h in range(4)], sem="st_ns")

    P.barrier(lambda e: e.memset(BARS[:, 0:1], 0.0))
    if dbg:
        dbg_mixt = dout("dbg_mixt", [128, 16, TOK])
        P.dma("pool", dbg_mixt, MIXT[:], sem="dbg1")

    AR.off = markA
    X1 = AR.alloc([128, 9, D], F32)
    X1T = AR.alloc([128, 16, TOK], BF16)
    LNG = AR.alloc([128, D], F32)
    LNB = AR.alloc([128, D], F32)
    LNT = [dict(ST8=AR.alloc([128, 4, 6], F32), LNA=AR.alloc([128, 2], F32), LNR=AR.alloc([128, 1], F32), NMR=AR.alloc([128, 1], F32)) for _ in range(3)]
    markB = AR.off
    WO = [AR.alloc([128, 16, 512], BF16) for _ in range(2)]

    def bcast_row(row_ap):
        return bass.AP(row_ap.tensor, row_ap.offset, [[0, 128], [1, D]])

    for t2 in range(0, 8, 2):
        P.dma("sp", X1[:, t2:t2 + 2, :], x_own[t2 * 128:(t2 + 2) * 128, :].rearrange("(t p) d -> p t d", p=128), r=["XCH"],
              w=["X1_%d" % t2, "X1_%d" % (t2 + 1), "XCH"], sem="X1_%d" % t2, cost=7.0)
    P.dma("sp", X1[0:16, 8, :], x_smp, r=["XCH"], w=["X1_8", "XCH"], sem="X1_8")
    P.dma("sp", LNG[:], bcast_row(ln_d[0]), r=["XCH"], w=["LNG", "XCH"], sem="LNG", cost=5.0)
    P.dma("sp", LNB[:], bcast_row(ln_d[1]), r=["XCH"], w=["LNB", "XCH"], sem="LNB", cost=5.0)
    P.op("act", lambda e: e.mul(LNG[:], LNG[:], ALPHA), r=["LNG"], w=["LNG"], cost=1.9)
    P.op("act", lambda e: e.mul(LNB[:], LNB[:], ALPHA), r=["LNB"], w=["LNB"], cost=1.9)
    mix_keys = lambda tt: ["MIXT%d_%d" % (et, t0) for et in range(16) for t0 in ([tt * 128] if tt < 8 else [1024])]
    w_out_v = w_out.rearrange("(et p) c -> p et c", p=128)

    def dense_tm(ACT_T, akeys_fn, wv, nk, ring, rname, evac):
        for cb in range(4):
            s = cb % len(ring)
            P.dma("pool", ring[s][:, 0:nk, :], wv[:, :, cb * 512:(cb + 1) * 512], w=["%s%d" % (rname, s)], sem="%s%d" % (rname, s), cost=16.0)
            for tt in range(9):
                np_ = 128 if tt < 8 else 16
                ps, pk = bank()

                def mm(e, ps=ps, tt=tt, np_=np_, s=s):
                    for kt in range(nk):
                        ins = e.matmul(ps[0:np_, :], ACT_T[:, kt, tt * 128:tt * 128 + np_], ring[s][:, kt, :], start=(kt == 0), stop=(kt == nk - 1))
                    return ins
                P.op("pe", mm, r=["%s%d" % (rname, s)] + akeys_fn(tt), w=[pk], cost=0.3 + nk * 0.215)
                evac(ps, pk, tt, np_, cb)

    def ev_out(ps, pk, tt, np_, cb):
        dst = X1[0:np_, tt, cb * 512:(cb + 1) * 512]
        P.op("dve", lambda e: e.scalar_tensor_tensor(dst, dst, ALPHA, ps[0:np_, :], ALU.mult, ALU.add), r=[pk, "X1_%d" % tt], w=["X1_%d" % tt])

    dense_tm(MIXT, mix_keys, w_out_v, 16, WO, "WO", ev_out)

    def layer_norm(tt, np_, extra_r):
        xk = "X1_%d" % tt
        xv = X1[0:np_, tt, :]
        L = LNT[tt % 3]
        i = tt % 3
        ST8, LNA, LNR, NMR = L["ST8"], L["LNA"], L["LNR"], L["NMR"]
        k8, ka, kr, kn = "ST8_%d" % i, "LNA%d" % i, "LNR%d" % i, "NMR%d" % i

        def bn(e):
            for q in range(4):
                ins = e.bn_stats(ST8[0:np_, q, :], X1[0:np_, tt, q * 512:(q + 1) * 512])
            return ins
        P.op("dve", bn, r=[xk], w=[k8], cost=2.4)
        P.op("dve", lambda e: e.bn_aggr(LNA[0:np_, :], ST8[0:np_, :, :]), r=[k8], w=[ka])
        P.op("act", lambda e: e.activation(LNR[0:np_, :], LNA[0:np_, 1:2], AF.Ln, bias=EPSL[0:np_, :]), r=[ka], w=[kr])
        P.op("act", lambda e: e.activation(LNR[0:np_, :], LNR[0:np_, :], AF.Exp, scale=-0.5), r=[kr], w=[kr])
        P.op("dve", lambda e: e.tensor_scalar(NMR[0:np_, :], LNA[0:np_, 0:1], -1.0, LNR[0:np_, :], ALU.mult, ALU.mult), r=[ka, kr], w=[kn])
        P.op("act", lambda e: e.activation(xv, xv, AF.Identity, bias=NMR[0:np_, :], scale=LNR[0:np_, :]), r=[xk, kr, kn], w=[xk], cost=1.9)
        P.op("dve", lambda e: e.tensor_tensor(xv, xv, LNG[0:np_, :], ALU.mult), r=[xk, "LNG"] + extra_r, w=[xk], cost=2.3)
        P.op("pool", lambda e: e.tensor_tensor(xv, xv, LNB[0:np_, :], ALU.add), r=[xk, "LNB"], w=[xk], cost=4.0)

    for tt in range(9):
        np_ = 128 if tt < 8 else 16
        layer_norm(tt, np_, [])
        for g in range(4):
            ps, pk = bank()

            def tr(e, ps=ps, g=g, tt=tt, np_=np_):
                for q in range(4):
                    dt = g * 4 + q
                    ins = e.transpose(ps[:, q * 128:q * 128 + np_], X1[0:np_, tt, dt * 128:(dt + 1) * 128], IDF[0:np_, 0:np_])
                return ins
            P.op("pe", tr, r=["X1_%d" % tt, "C"], w=[pk])
            out = X1T[:, g * 4:(g + 1) * 4, tt * 128:tt * 128 + np_]
            inn = ps[:].rearrange("p (q t) -> p q t", q=4)[:, :, 0:np_]
            copy_op(ev_eng(), out, inn, r=[pk], w=["X1T%d_%d" % (tt, g)], scale=1.0 / ALPHA)

    P.barrier(lambda e: e.memset(BARS[:, 0:1], 0.0))
    if dbg:
        dbg_x1t = dout("dbg_x1t", [128, 16, TOK])
        P.dma("pool", dbg_x1t, X1T[:], sem="dbg2")

    AR.off = markB
    WB = [AR.alloc([128, 8, 512], BF16) for _ in range(3)]
    PT = AR.alloc([128, 2, TOK], BF16)
    SIG = AR.alloc([128, 512], F32)
    WPE = AR.alloc([128, 2, D], BF16)
    AR.off = mixt_off
    HT = AR.alloc([128, 8, TOK], BF16)
    W1 = [AR.alloc([128, 16, 128], BF16) for _ in range(4)]
    assert AR.off <= markA, (AR.off, markA)
    PIN = HT[:, 0:3, :].rearrange("p a t -> p (a t)")[:, 0:9 * 256].rearrange("p (t c) -> p t c", t=9)

    x1t_keys = lambda tt: ["X1T%d_%d" % (tt, g) for g in range(4)]
    P.dma("pool", PIN[:, 0:8, :], p_own.rearrange("(t p) c -> p t c", p=128), w=["PIN"], sem="PIN")
    P.dma("pool", PIN[0:16, 8, :], p_smp, w=["PIN8"], sem="PIN8")
    P.dma("pool", WPE[:], w_pe.rearrange("(pt p) c -> p pt c", p=128), w=["WPE"], sem="WPE")
    for tt in range(9):
        np_ = 128 if tt < 8 else 16
        pb, pkb = bbank()

        def trp(e, pb=pb, tt=tt, np_=np_):
            for j in range(2):
                ins = e.transpose(pb[:, j * 128:j * 128 + np_], PIN[0:np_, tt, j * 128:(j + 1) * 128], IDB[0:np_, 0:np_])
            return ins
        P.op("pe", trp, r=["PIN", "PIN8"], w=[pkb])
        copy_op(ev_eng(), PT[:, :, tt * 128:tt * 128 + np_], pb[:, 0:256].rearrange("p (j t) -> p j t", j=2)[:, :, 0:np_], r=[pkb], w=["PT%d" % tt])
    w_pg_v = w_pg.rearrange("(dt p) c -> p dt c", p=128)
    for cb in range(8):
        s = cb % 3
        P.dma("pool", WB[s][:].rearrange("p a b -> p (a b)").rearrange("p (dt c) -> p dt c", dt=16), w_pg_v[:, :, cb * 256:(cb + 1) * 256], w=["WB%d" % s], sem="WB%d" % s, cost=9.0)
        Wb = WB[s][:].rearrange("p a b -> p (a b)").rearrange("p (dt c) -> p dt c", dt=16)
        for tt in range(9):
            np_ = 128 if tt < 8 else 16
            ps, pk = bank()

            def mm(e, ps=ps, tt=tt, np_=np_, Wb=Wb, cb=cb):
                for dt in range(16):
                    e.matmul(ps[0:np_, 0:256], X1T[:, dt, tt * 128:tt * 128 + np_], Wb[:, dt, :], start=(dt == 0), stop=(dt == 15))
                for pt in range(2):
                    ins = e.matmul(ps[0:np_, 256:512], PT[:, pt, tt * 128:tt * 128 + np_], WPE[:, pt, cb * 256:(cb + 1) * 256], start=(pt == 0), stop=(pt == 1))
                return ins
            P.op("pe", mm, r=["WB%d" % s, "WPE", "PT%d" % tt] + x1t_keys(tt), w=[pk], cost=2.3)
            P.op("act", lambda e, ps=ps, np_=np_: e.activation(SIG[0:np_, 0:256], ps[0:np_, 0:256], AF.Sigmoid), r=[pk], w=["SIG"])
            P.op("dve", lambda e, ps=ps, np_=np_: e.tensor_tensor(SIG[0:np_, 256:512], ps[0:np_, 256:512], SIG[0:np_, 0:256], ALU.mult), r=[pk, "SIG"], w=["SIG2"])
            dst = X1[0:np_, tt, cb * 256:(cb + 1) * 256]
            P.op("pool", lambda e, dst=dst, np_=np_: e.tensor_tensor(dst, dst, SIG[0:np_, 256:512], ALU.add), r=["SIG2", "X1_%d" % tt], w=["X1_%d" % tt])
    P.dma("sp", LNG[:], bcast_row(ln_d[2]), r=["WPE", "PIN"], w=["LNG"], sem="LNG")
    P.dma("sp", LNB[:], bcast_row(ln_d[3]), r=["WPE", "PIN"], w=["LNB"], sem="LNB")
    w1_v = w_ff1.rearrange("(dt p) f -> p dt f", p=128)
    w2_v = w_ff2.rearrange("(ft p) c -> p ft c", p=128)
    w1i = 0
    for fs in range(8):
        for fl in range(8):
            ft = fs * 8 + fl
            s = w1i % 4; w1i += 1
            P.dma("pool", W1[s][:], w1_v[:, :, ft * 128:(ft + 1) * 128], w=["W1_%d" % s], sem="W1_%d" % s, cost=5.5)
            for t0, n in ((0, 512), (512, 512), (1024, 16)):
                ps, pk = bank()

                def mm(e, ps=ps, s=s, t0=t0, n=n):
                    for dt in range(16):
                        ins = e.matmul(ps[:, 0:n], W1[s][:, dt, :], X1T[:, dt, t0:t0 + n], start=(dt == 0), stop=(dt == 15))
                    return ins
                tiles = range(t0 // 128, (t0 + n - 1) // 128 + 1)
                P.op("pe", mm, r=["W1_%d" % s] + [k for tt in tiles for k in x1t_keys(tt)], w=[pk], cost=0.3 + 16 * n / 2400.0)
                P.op("act", lambda e, ps=ps, n=n: e.activation(SIG[:, 0:n], ps[:, 0:n], AF.Relu), r=[pk], w=["SIG", "SIG2"])
                P.op("dve", lambda e, fl=fl, t0=t0, n=n: e.tensor_tensor(HT[:, fl, t0:t0 + n], SIG[:, 0:n], SIG[:, 0:n], ALU.mult), r=["SIG", "SIG2"], w=["HT%d_%d" % (fl, t0)])
        groups = [list(range(9))] if fs < 7 else [[0, 1, 2, 3, 4, 5], [6, 7, 8]]
        for gi, grp in enumerate(groups):
            for cb in range(4):
                s = (fs * 4 + cb + 2 + gi) % 3
                P.dma("pool", WB[s][:], w2_v[:, fs * 8:(fs + 1) * 8, cb * 512:(cb + 1) * 512], w=["WB%d" % s], sem="WB%d" % s, cost=9.0)
                for tt in grp:
                    np_ = 128 if tt < 8 else 16
                    t0k = (tt // 4) * 512 if tt < 8 else 1024
                    ps, pk = bank()

                    def mm(e, ps=ps, tt=tt, np_=np_, s=s):
                        for fl in range(8):
                            ins = e.matmul(ps[0:np_, :], HT[:, fl, tt * 128:tt * 128 + np_], WB[s][:, fl, :], start=(fl == 0), stop=(fl == 7))
                        return ins
                    P.op("pe", mm, r=["WB%d" % s] + ["HT%d_%d" % (fl, t0k) for fl in range(8)], w=[pk], cost=2.0)
                    dst = X1[0:np_, tt, cb * 512:(cb + 1) * 512]
                    P.op("dve", lambda e, dst=dst, ps=ps, np_=np_: e.tensor_tensor(dst, dst, ps[0:np_, :], ALU.add), r=[pk, "X1_%d" % tt], w=["X1_%d" % tt])
    for tt in range(9):
        np_ = 128 if tt < 8 else 16
        layer_norm(tt, np_, [])
        if tt < 8:
            P.dma("sp", y_own[tt * 128:(tt + 1) * 128, :], X1[:, tt, :], r=["X1_%d" % tt], sem="st_y%d" % tt)
        else:
            P.dma("sp", y_smp, X1[0:16, 8, :], r=["X1_8"], sem="st_y8")
    P.emit()
    return nc, P, AR


_CACHE = {}
DBG = False


def _consts(hf):
    f32 = np.float32
    t = np.arange(128)
    sq = np.zeros((128, 6, 128), f32)
    sq[:, 0] = np.eye(128)
    sq[:, 1] = (t[:, None] <= t[None, :])
    sq[:, 2] = np.where(t[None, :] <= t[:, None], 0.0, -1e30)
    sq[127, 3, :] = 1.0
    sq[:, 4] = 1.0
    g = np.array(G, np.float64)
    rmask = np.zeros((128, 4, 128), f32)
    for h in range(4):
        rmask[:, h, :] = np.where(t[None, :] >= t[:, None], g[h] ** (-(t[:, None] + 1.0)) / 16.0, 0.0)
    small = np.zeros((128, 48), f32)
    for h in range(4):
        small[:, h] = g[h] ** (t + 1.0)
        small[:, 4 + h] = (g[h] ** (t + 1.0)) ** 2
        small[:, 8 + h] = g[h] ** (127.0 - t) / 16.0
        for c in range(8):
            small[:, 12 + c * 4 + h] = g[h] ** (1023.0 - (c * 128 + t)) / 16.0
    small[:, 44] = float(hf)
    small[:, 45] = GN_EPS
    small[:, 46] = LN_EPS
    id16 = np.tile(np.eye(16, dtype=f32).reshape(1, 256), (128, 1))
    inv = 10000.0 ** (-(np.arange(128, dtype=np.float64) * 2.0) / 256.0)
    pos_own = np.concatenate([hf * 1024 + np.arange(1024), np.full(16, 16384)]).astype(np.float64)
    pos_pre = np.arange(1024).astype(np.float64)

    def cs(pos):
        ang = pos[None, :] * inv[:, None]
        return np.ascontiguousarray(np.stack([np.cos(ang), np.sin(ang)], axis=1).astype(f32))
    return dict(c_sq=sq, c_rmask=rmask, c_small=small, c_id16=id16, cs_own=cs(pos_own), cs_pre=cs(pos_pre))


def kernel(x_prompt, x_sample, state_ret, state_mlstm_C, state_mlstm_n, state_mlstm_m,
           p_prompt, p_sample, w_in, b_gate, ret_gn_w, mlstm_gn_w, w_out, ln1_g, ln1_b,
           w_ff1, w_ff2, w_pe, w_pe_gate, ln2_g, ln2_b):
    if "nc" not in _CACHE:
        _CACHE["nc"] = build_program(dbg=DBG)[0]
    nc = _CACHE["nc"]
    A = lambda a: np.ascontiguousarray(np.asarray(a, dtype=np.float32))
    shared = dict(
        w_in=A(w_in[0]), w_out=A(w_out[0]), w_ff1=A(w_ff1[0]), w_ff2=A(w_ff2[0]), w_pe=A(w_pe[0]),
        w_pe_gate=A(w_pe_gate[0]), ln_all=A(np.stack([ln1_g[0], ln1_b[0], ln2_g[0], ln2_b[0]])),
        b_gate=A(b_gate), gnw=A(np.concatenate([ret_gn_w[0], mlstm_gn_w[0]]).reshape(16, 128).T),
    )
    in_maps = []
    for c in range(NCORES):
        s, hf = c // 2, c % 2
        m = dict(shared)
        m.update(_consts(hf))
        m["x_own"] = A(x_prompt[s, hf * 1024:(hf + 1) * 1024])
        m["x_pre"] = A(x_prompt[s, 0:1024])
        m["x_smp"] = A(x_sample[c * 16:(c + 1) * 16, 0])
        m["p_own"] = A(p_prompt[0, s, hf * 1024:(hf + 1) * 1024])
        m["p_smp"] = A(p_sample[0, c * 16:(c + 1) * 16, 0])
        m["st_ret"] = A(state_ret[0, c * 16:(c + 1) * 16])
        m["st_C"] = A(state_mlstm_C[0, c * 16:(c + 1) * 16])
        m["st_n"] = A(state_mlstm_n[0, c * 16:(c + 1) * 16].reshape(16, 1024))
        m["st_m"] = A(state_mlstm_m[0, c * 16:(c + 1) * 16])
        in_maps.append(m)
    res = run_bass_kernel_spmd(nc, in_maps, core_ids=list(range(NCORES)))
    R = res.results
    _CACHE["R"] = R
    yp = np.zeros((4, 2048, D), np.float32)
    for c in range(NCORES):
        yp[c // 2, (c % 2) * 1024:(c % 2 + 1) * 1024] = R[c]["y_own"]
    ys = np.concatenate([R[c]["y_smp"] for c in range(NCORES)], 0).reshape(128, 1, D)
    odd = [1, 3, 5, 7]
    rS_p = np.stack([R[c]["retS_p"] for c in odd])[None]
    C_p = np.stack([R[c]["C_p"] for c in odd])[None]
    n_p = np.stack([R[c]["n_p"] for c in odd])[None]
    m_p = np.stack([R[c]["m_p"][0] for c in odd])[None]
    rS_s = np.concatenate([R[c]["retS_s"] for c in range(NCORES)], 0)[None]
    C_s = np.concatenate([R[c]["C_s"] for c in range(NCORES)], 0)[None]
    n_s = np.concatenate([R[c]["n_s"] for c in range(NCORES)], 0).reshape(1, 128, 4, 256)
    m_s = np.concatenate([R[c]["m_s"] for c in range(NCORES)], 0)[None]
    f = lambda a: np.ascontiguousarray(a, dtype=np.float32)
    return (f(yp), f(ys), f(rS_p), f(C_p), f(n_p), f(m_p), f(rS_s), f(C_s), f(n_s), f(m_s))
```

```python
import numpy as np
import concourse.bass as bass
import concourse.mybir as mybir
from concourse.bass_utils import run_bass_kernel_spmd

F32 = mybir.dt.float32
BF16 = mybir.dt.bfloat16
AF = mybir.ActivationFunctionType
ALU = mybir.AluOpType
AX = mybir.AxisListType

ENGS = ("pe", "act", "dve", "pool", "sp")
NCORES = 8
D = 2048
TOK = 1040
ALPHA = 2.0 ** 0.25
GN_EPS = 1e-6
LN_EPS = 1e-5
G = [1.0 - 2.0 ** (-5.0 - h) for h in range(4)]


class Prog:
    DEF_COST = dict(pe=0.3, act=0.45, dve=0.45, pool=0.8, sp=0.3)

    def __init__(self, nc):
        self.nc = nc
        self.ops = []
        self.last_w = {}
        self.readers = {}
        self.dma_sem_count = {}
        self.bar = None
        self.bar_start = 0

    def _add(self, eng, fn, r, w, dma_sem=None, cost=None):
        i = len(self.ops)
        deps = set()
        if self.bar is not None:
            deps.add(self.bar)
        for k in list(r) + list(w):
            if k in self.last_w:
                deps.add(self.last_w[k])
        for k in w:
            for x in self.readers.get(k, ()):
                deps.add(x)
        deps.discard(i)
        for k in w:
            self.last_w[k] = i
            self.readers[k] = []
        for k in r:
            self.readers.setdefault(k, []).append(i)
        odeps = set(deps)
        if eng == "pe":
            deps = {d for d in deps if self.ops[d]["eng"] != "pe" or self.ops[d]["dma_sem"] is not None}
        if cost is None:
            cost = 2.5 if dma_sem is not None else self.DEF_COST[eng]
        op = dict(id=i, eng=eng, fn=fn, deps=deps, odeps=odeps, dma_sem=dma_sem, has_dep=False, has_any=False, cost=cost)
        if dma_sem is not None:
            c = self.dma_sem_count.get(dma_sem, 0) + 1
            self.dma_sem_count[dma_sem] = c
            op["target"] = 16 * c
        for d in deps:
            self.ops[d]["has_dep"] = True
        for d in odeps:
            self.ops[d]["has_any"] = True
        self.ops.append(op)
        return i

    def op(self, eng, fn, r=(), w=(), cost=None):
        return self._add(eng, fn, r, w, cost=cost)

    def dma(self, eng, out, in_, r=(), w=(), sem=None, cost=None):
        return self._add(eng, lambda e: e.dma_start(out=out, in_=in_), r, w, dma_sem=sem, cost=cost)

    def barrier(self, fn):
        i = len(self.ops)
        deps = {o["id"] for o in self.ops[self.bar_start:] if not o["has_any"]}
        if self.bar is not None:
            deps.add(self.bar)
        op = dict(id=i, eng="dve", fn=fn, deps=deps, odeps=set(deps), dma_sem=None, has_dep=True, has_any=True, cost=0.2)
        for d in deps:
            self.ops[d]["has_dep"] = True
            self.ops[d]["has_any"] = True
        self.ops.append(op)
        self.bar = i
        self.bar_start = i
        self.last_w = {}
        self.readers = {}
        return i

    def schedule(self):
        import heapq
        ops = self.ops
        n = len(ops)
        succ = [[] for _ in range(n)]
        indeg = [0] * n
        for o in ops:
            indeg[o["id"]] = len(o["odeps"])
            for d in o["odeps"]:
                succ[d].append(o["id"])
        ready = {e: [] for e in ENGS}
        rtime = [0.0] * n
        finish = [0.0] * n
        free = {e: 0.0 for e in ENGS}
        order = {e: [] for e in ENGS}
        for o in ops:
            if indeg[o["id"]] == 0:
                heapq.heappush(ready[o["eng"]], (0.0, o["id"]))
        done = 0
        while done < n:
            best = None
            for e in ENGS:
                h = ready[e]
                if not h:
                    continue
                t_free = free[e]
                cand = None
                avail = [x for x in h if x[0] <= t_free]
                if avail:
                    cid = min(x[1] for x in avail)
                    cand = (t_free, cid)
                else:
                    rt, cid = min(h)
                    cand = (rt, cid)
                if best is None or cand < best[0]:
                    best = (cand, e)
            (start, i), e = best
            h = ready[e]
            for k, x in enumerate(h):
                if x[1] == i:
                    h[k] = h[-1]
                    h.pop()
                    break
            heapq.heapify(h)
            o = ops[i]
            if o["dma_sem"] is not None:
                free[e] = start + 0.35
                finish[i] = start + o["cost"]
            else:
                free[e] = start + o["cost"]
                finish[i] = free[e]
            order[e].append(i)
            done += 1
            for sidx in succ[i]:
                so = ops[sidx]
                lat = 0.2 if (so["eng"] == e and o["dma_sem"] is None) else 1.2
                rtime[sidx] = max(rtime[sidx], finish[i] + lat)
                indeg[sidx] -= 1
                if indeg[sidx] == 0:
                    heapq.heappush(ready[so["eng"]], (rtime[sidx], sidx))
        self.sim_time = max(finish)
        return order

    def emit(self, reorder=True):
        nc = self.nc
        esem = {e: nc.alloc_semaphore("s_" + e) for e in ENGS}
        dsem = {k: nc.alloc_semaphore("d_%d" % j) for j, k in enumerate(self.dma_sem_count)}
        ops = self.ops
        if reorder:
            order = self.schedule()
        else:
            order = {e: [o["id"] for o in ops if o["eng"] == e] for e in ENGS}
        cnt = {e: 0 for e in ENGS}
        for e in ENGS:
            for i in order[e]:
                op = ops[i]
                if op["dma_sem"] is None and op["has_dep"]:
                    cnt[e] += 1
                    op["ms"] = cnt[e]
        stats = dict(waits=0, ops=len(ops))

        def run(eng_name, e):
            seen = {}
            for i in order[eng_name]:
                op = ops[i]
                need = {}
                for d in op["deps"]:
                    p = ops[d]
                    if p["dma_sem"] is not None:
                        key, val = ("d", p["dma_sem"]), p["target"]
                    else:
                        key, val = ("e", p["eng"]), p["ms"]
                    if val > need.get(key, 0):
                        need[key] = val
                for key, val in need.items():
                    if seen.get(key, 0) >= val:
                        continue
                    sm = dsem[key[1]] if key[0] == "d" else esem[key[1]]
                    e.wait_ge(sm, val)
                    stats["waits"] += 1
                    seen[key] = val
                ins = op["fn"](e)
                if op["dma_sem"] is not None:
                    ins.then_inc(dsem[op["dma_sem"]], 16)
                elif op["has_dep"]:
                    ins.then_inc(esem[op["eng"]], 1)
            if eng_name == "sp":
                for k, c in self.dma_sem_count.items():
                    if seen.get(("d", k), 0) < 16 * c:
                        e.wait_ge(dsem[k], 16 * c)

        with nc.Block() as block:
            @block.tensor
            def _(e):
                run("pe", e)

            @block.scalar
            def _(e):
                run("act", e)

            @block.vector
            def _(e):
                run("dve", e)

            @block.gpsimd
            def _(e):
                run("pool", e)

            @block.sync
            def _(e):
                run("sp", e)
        self.stats = stats


class Arena:
    def __init__(self, nc, nbytes):
        self.t = nc.alloc_sbuf_tensor("arena", [128, nbytes // 2], BF16)
        self.off = 0
        self.cap = nbytes
        self.peak = 0

    def alloc(self, shape, dtype):
        n = 1
        for s in shape[1:]:
            n *= s
        esz = 4 if dtype == F32 else 2
        nb = (n * esz + 31) // 32 * 32
        o = self.off
        self.off += nb
        self.peak = max(self.peak, self.off)
        assert self.off <= self.cap, ("SBUF arena overflow", self.off, self.cap)
        v = self.t[0:shape[0], o // 2:o // 2 + n * esz // 2]
        if dtype == F32:
            v = v.bitcast(F32)
        if len(shape) == 3:
            v = v.rearrange("p (a b) -> p a b", a=shape[1])
        elif len(shape) == 4:
            v = v.rearrange("p (a b c) -> p a b c", a=shape[1], b=shape[2])
        return v


def build_program(dbg=False):
    nc = bass.Bass("TRN2", target_bir_lowering=False)

    def din(name, shape):
        return nc.dram_tensor(name, list(shape), F32, kind="ExternalInput").ap()

    def dout(name, shape):
        return nc.dram_tensor(name, list(shape), F32, kind="ExternalOutput").ap()

    x_own = din("x_own", [1024, D]); x_pre = din("x_pre", [1024, D]); x_smp = din("x_smp", [16, D])
    p_own = din("p_own", [1024, 256]); p_smp = din("p_smp", [16, 256])
    st_ret = din("st_ret", [16, 4, 256, 256]); st_C = din("st_C", [16, 4, 256, 256])
    st_n = din("st_n", [16, 1024]); st_m = din("st_m", [16, 4])
    w_in = din("w_in", [D, 8200]); w_out = din("w_out", [D, D]); w_ff1 = din("w_ff1", [D, 8192])
    w_ff2 = din("w_ff2", [8192, D]); w_pe = din("w_pe", [256, D]); w_pg = din("w_pe_gate", [D, D])
    ln_d = din("ln_all", [4, D])
    bg_d = din("b_gate", [1, 8])
    gnw_d = din("gnw", [128, 16])
    c_sq = din("c_sq", [128, 6, 128])
    c_rmask = din("c_rmask", [128, 4, 128])
    c_small = din("c_small", [128, 48])
    c_id16 = din("c_id16", [128, 256])
    cs_own_d = din("cs_own", [128, 2, TOK]); cs_pre_d = din("cs_pre", [128, 2, 1024])

    y_own = dout("y_own", [1024, D]); y_smp = dout("y_smp", [16, D])
    o_retS_p = dout("retS_p", [4, 256, 256]); o_C_p = dout("C_p", [4, 256, 256])
    o_n_p = dout("n_p", [4, 256]); o_m_p = dout("m_p", [1, 4])
    o_retS_s = dout("retS_s", [16, 4, 256, 256]); o_C_s = dout("C_s", [16, 4, 256, 256])
    o_n_s = dout("n_s", [16, 1024]); o_m_s = dout("m_s", [16, 4])

    P = Prog(nc)
    AR = Arena(nc, 206 * 1024)
    PS = [nc.alloc_psum_tensor("ps%d" % i, [128, 512], F32) for i in range(5)]
    PSX = nc.alloc_psum_tensor("psx", [128, 512], F32)
    PB = [nc.alloc_psum_tensor("pb%d" % i, [128, 1024], BF16) for i in range(2)]
    st = dict(ps=0, pb=0, ev=0, wr=0)

    def bank():
        i = st["ps"]; st["ps"] = (i + 1) % 5
        return PS[i], "PS%d" % i

    def bbank():
        i = st["pb"]; st["pb"] = (i + 1) % 2
        return PB[i], "PB%d" % i

    def ev_eng():
        st["ev"] ^= 1
        return "dve" if st["ev"] else "act"

    def copy_op(eng, out, in_, r, w, scale=None):
        if eng == "act":
            if scale is None:
                P.op("act", lambda e: e.copy(out, in_), r=r, w=w)
            else:
                P.op("act", lambda e: e.mul(out, in_, scale), r=r, w=w)
        else:
            if scale is None:
                P.op(eng, lambda e: e.tensor_copy(out, in_), r=r, w=w)
            else:
                P.op(eng, lambda e: e.tensor_scalar(out, in_, scale, None, ALU.mult), r=r, w=w)

    SQ = AR.alloc([128, 6, 128], F32)
    IDF, TRI, NEGM, SEL127, ONES = (SQ[:, i, :] for i in range(5))
    IDB = AR.alloc([128, 128], BF16)
    RMASK = AR.alloc([128, 4, 128], F32)
    SM = AR.alloc([128, 48], F32)
    QD, QD2, KDEC = SM[:, 0:4], SM[:, 4:8], SM[:, 8:12]
    KPRE = SM[:, 12:44].rearrange("p (c h) -> p c h", c=8)
    FLAG = SM[:, 44:45]
    EPSG = SM[:, 45:46]
    EPSL = SM[:, 46:47]
    ID16 = AR.alloc([128, 16, 16], BF16)
    BG = AR.alloc([128, 8], F32)
    GNW = AR.alloc([128, 16], F32)
    NSM = AR.alloc([16, 1024], F32)
    NNEW = AR.alloc([16, 1024], F32)
    MS0 = AR.alloc([16, 4], F32)
    BARS = AR.alloc([128, 8], F32)

    cl = lambda out, in_, q="sp": P.dma(q, out, in_, w=["C"], sem=("C" if q == "sp" else "Cp"))
    cl(SQ, c_sq); cl(RMASK, c_rmask); cl(SM, c_small); cl(GNW, gnw_d)
    cl(BG, bass.AP(bg_d.tensor, 0, [[0, 128], [1, 8]]))
    cl(NSM, st_n); cl(MS0, st_m)
    cl(ID16.rearrange("p a b -> p (a b)"), c_id16, "pool")
    cl(IDB, c_sq[:, 0, :], "pool")
    P.barrier(lambda e: e.memset(BARS[:, 0:1], 0.0))

    mixt_off = AR.off
    MIXT = AR.alloc([128, 16, TOK], BF16)
    markA = AR.off
    XT = AR.alloc([128, 16, TOK], BF16)
    WR = [AR.alloc([128, 16, 256], BF16) for _ in range(4)]
    ovl = AR.off
    XIN = [AR.alloc([128, D], BF16) for _ in range(2)]
    AR.off = ovl
    ROT = [AR.alloc([128, 512], F32) for _ in range(4)]
    CS = AR.alloc([128, 2, TOK], F32)
    AR.off = max(AR.off, ovl + 2 * D * 4)
    QT = AR.alloc([128, 2, TOK], BF16)
    KT = AR.alloc([128, 2, TOK], BF16)
    KTOK = AR.alloc([128, 8, 256], BF16)
    VEXT = AR.alloc([128, 9, 260], BF16)
    GATE = AR.alloc([128, 9, 256], BF16)
    SINIT = AR.alloc([128, 8, 2, 260], F32)
    SF = [AR.alloc([128, 2, 260], F32) for _ in range(2)]
    SBF = [AR.alloc([128, 2, 260], BF16) for _ in range(2)]
    NSLOT = 3
    SST = [AR.alloc([128, 2, 256], F32) for _ in range(NSLOT)]
    SNB = [AR.alloc([128, 2, 256], BF16) for _ in range(NSLOT)]
    WG = AR.alloc([128, 16, 8], BF16)
    GP = AR.alloc([128, 9, 8], F32)
    IG = AR.alloc([128, 9, 4], F32)
    SP = AR.alloc([128, 9, 4], F32)
    CSA = AR.alloc([128, 2, 8, 4], F32)
    AA = AR.alloc([128, 8, 4], F32)
    LST = AR.alloc([128, 2, 8, 4], F32)
    MCH = AR.alloc([128, 9, 4], F32)
    MML = AR.alloc([128, 8, 4], F32)
    ECH = AR.alloc([128, 9, 4], F32)
    T32 = AR.alloc([128, 8, 4], F32)
    SWT = AR.alloc([128, 8, 4], F32)
    MM = AR.alloc([128, 8, 4], F32)
    NEGMM = AR.alloc([128, 8, 4], F32)
    IW = AR.alloc([128, 8, 4], F32)
    EMT = AR.alloc([128, 8, 4], F32)
    SD = AR.alloc([128, 8, 4], F32)
    MINIT = AR.alloc([128, 4], F32)
    TMPA = AR.alloc([128, 4, 128], F32)
    NSET = 3
    SETS = []
    for i in range(NSET):
        SETS.append(dict(id=i, DW=AR.alloc([128, 128], F32), SCM=AR.alloc([128, 128], BF16), SCMT=AR.alloc([128, 128], BF16),
                         INTRA=AR.alloc([128, 260], F32), NUMER=AR.alloc([128, 256], F32), YF=AR.alloc([128, 256], F32),
                         Y2=AR.alloc([128, 256], BF16), BNS=AR.alloc([128, 6], F32), BNA=AR.alloc([128, 2], F32), SC1=AR.alloc([128, 8], F32)))
    _save = AR.off
    AR.off = ovl
    for i in range(NSET, NSET + 3):
        SETS.append(dict(id=i, DW=AR.alloc([128, 128], F32), SCM=AR.alloc([128, 128], BF16), SCMT=AR.alloc([128, 128], BF16),
                         INTRA=AR.alloc([128, 260], F32), NUMER=AR.alloc([128, 256], F32), YF=AR.alloc([128, 256], F32),
                         Y2=AR.alloc([128, 256], BF16), BNS=AR.alloc([128, 6], F32), BNA=AR.alloc([128, 2], F32), SC1=AR.alloc([128, 8], F32)))
    assert AR.off <= ovl + 4 * 2048 + 2 * TOK * 4, (AR.off - ovl)
    AR.off = _save
    ALIAS = ["R0", "R1", "R2", "R3", "CS"]
    QM = AR.alloc([128, 2, 16, 16], BF16)
    KMB = [AR.alloc([16, 256], BF16) for _ in range(NSLOT)]
    KS = AR.alloc([16, 256], F32)
    KSD = AR.alloc([16, 256], F32)
    QS = AR.alloc([16, 256], F32)
    SMP = AR.alloc([16, 32], F32)
    DG = AR.alloc([16, 16, 4], F32)
    IWB = AR.alloc([128, 16, 4], F32)
    TQ = AR.alloc([16, 256], F32)

    w_in_v = w_in.rearrange("(dt p) c -> p dt c", p=128)

    def load_w(c0, ncols=256):
        s = st["wr"]; st["wr"] = (s + 1) % 4
        P.dma("pool", WR[s][:, :, 0:ncols], w_in_v[:, :, c0:c0 + ncols], w=["WR%d" % s], sem="WR%d" % s, cost=9.0)
        return WR[s], "WR%d" % s

    def xt_keys(t0, n):
        tiles = range(t0 // 128, (t0 + n - 1) // 128 + 1)
        return ["XT%d_%d" % (tt, g) for tt in tiles for g in range(4)]

    def build_xt(src, ntiles, smp_src=None):
        jobs = [(src[tt * 128:(tt + 1) * 128, :], 128, tt) for tt in range(ntiles)]
        if smp_src is not None:
            jobs.append((smp_src, 16, 8))
        for i, (sap, np_, tt) in enumerate(jobs):
            xin = XIN[i % 2]; xk = "XIN%d" % (i % 2)
            P.dma("pool", xin[0:np_, :], sap, w=[xk], sem=xk, cost=5.5)
            for g in range(4):
                ps, pk = bank()
                psb = ps[:].bitcast(BF16)

                def tr(e, psb=psb, xin=xin, g=g, np_=np_):
                    for q in range(4):
                        dt = g * 4 + q
                        ins = e.transpose(psb[:, q * 128:q * 128 + np_], xin[0:np_, dt * 128:(dt + 1) * 128], IDB[0:np_, 0:np_])
                    return ins
                P.op("pe", tr, r=[xk], w=[pk])
                out = XT[:, g * 4:(g + 1) * 4, tt * 128:tt * 128 + np_]
                inn = psb[:, 0:512].rearrange("p (q t) -> p q t", q=4)[:, :, 0:np_]
                copy_op(ev_eng(), out, inn, r=[pk], w=["XT%d_%d" % (tt, g)])

    def proj_fm(W, wk, ct, t0, n):
        ps, pk = bank()

        def mm(e):
            for dt in range(16):
                ins = e.matmul(ps[:, 0:n], W[:, dt, ct * 128:(ct + 1) * 128], XT[:, dt, t0:t0 + n],
                               start=(dt == 0), stop=(dt == 15))
            return ins
        P.op("pe", mm, r=[wk] + xt_keys(t0, n), w=[pk], cost=0.3 + 16 * n / 2400.0)
        return ps, pk

    def rotary(W, wk, DST, dkey, t0, n):
        a, ak = proj_fm(W, wk, 0, t0, n)
        b, bk = proj_fm(W, wk, 1, t0, n)
        cos, sin = CS[:, 0, t0:t0 + n], CS[:, 1, t0:t0 + n]
        r0, r1, r2, r3 = (ROT[i][:, 0:n] for i in range(4))
        P.op("dve", lambda e: e.tensor_tensor(r0, a[:, 0:n], cos, ALU.mult), r=[ak, "CS"], w=["R0"])
        P.op("dve", lambda e: e.tensor_tensor(r1, b[:, 0:n], sin, ALU.mult), r=[bk, "CS"], w=["R1"])
        P.op("dve", lambda e: e.tensor_tensor(r2, a[:, 0:n], sin, ALU.mult), r=[ak, "CS"], w=["R2"])
        P.op("dve", lambda e: e.tensor_tensor(r3, b[:, 0:n], cos, ALU.mult), r=[bk, "CS"], w=["R3"])
        P.op("pool", lambda e: e.tensor_tensor(DST[:, 0, t0:t0 + n], r0, r1, ALU.subtract), r=["R0", "R1"], w=[dkey + "0_%d" % t0])
        P.op("pool", lambda e: e.tensor_tensor(DST[:, 1, t0:t0 + n], r2, r3, ALU.add), r=["R2", "R3"], w=[dkey + "1_%d" % t0])

    def plain_fm(W, wk, DST, dkey, t0, n, scale):
        for ct in range(2):
            a, ak = proj_fm(W, wk, ct, t0, n)
            copy_op(ev_eng(), DST[:, ct, t0:t0 + n], a[:, 0:n], r=[ak], w=[dkey + "%d_%d" % (ct, t0)], scale=scale)

    def blk_keys(dkey, t0, n):
        out = []
        for b0 in (0, 512, 1024):
            bn = 512 if b0 < 1024 else 16
            if t0 < b0 + bn and t0 + n > b0:
                out += [dkey + "0_%d" % b0, dkey + "1_%d" % b0]
        return out

    def proj_tm(W, wk, tt, np_, ncols=256):
        ps, pk = bank()

        def mm(e):
            for dt in range(16):
                ins = e.matmul(ps[0:np_, 0:ncols], XT[:, dt, tt * 128:tt * 128 + np_], W[:, dt, 0:ncols],
                               start=(dt == 0), stop=(dt == 15))
            return ins
        P.op("pe", mm, r=[wk] + xt_keys(tt * 128, np_), w=[pk], cost=0.3 + 16 * ncols / 2400.0)
        return ps, pk

    def make_ktok(c, scale_ap):
        pb, pk = bbank()

        def tr(e):
            for j in range(2):
                ins = e.transpose(pb[:, j * 128:(j + 1) * 128], KT[:, j, c * 128:(c + 1) * 128], IDB)
            return ins
        P.op("pe", tr, r=blk_keys("KT", c * 128, 128), w=[pk])
        copy_op(ev_eng(), KTOK[:, c, :], pb[:, 0:256], r=[pk, "SWT", "C"], w=["KTOK%d" % c], scale=scale_ap)

    def gates(ntiles_full, with_sample):
        ps, pk = bank()
        tiles = [(tt, 128) for tt in range(ntiles_full)] + ([(8, 16)] if with_sample else [])

        def mm(e):
            for tt, np_ in tiles:
                for dt in range(16):
                    ins = e.matmul(ps[0:np_, tt * 8:(tt + 1) * 8], XT[:, dt, tt * 128:tt * 128 + np_], WG[:, dt, :],
                                   start=(dt == 0), stop=(dt == 15))
            return ins
        P.op("pe", mm, r=["WG"] + xt_keys(0, TOK if with_sample else 1024), w=[pk])
        nt = len(tiles)
        for tt, np_ in tiles:
            pass
        full = ps[:, 0:8 * 8].rearrange("p (t g) -> p t g", t=8)
        P.op("dve", lambda e: e.tensor_tensor(IG[:, 0:8, :], full[:, :, 0:4], BG[:, 0:4].unsqueeze(1).to_broadcast([128, 8, 4]), ALU.add), r=[pk], w=["IG"])
        P.op("dve", lambda e: e.tensor_tensor(SP[:, 0:8, :], full[:, :, 4:8], BG[:, 4:8].unsqueeze(1).to_broadcast([128, 8, 4]), ALU.add), r=[pk], w=["SP"])
        if with_sample:
            P.op("dve", lambda e: e.tensor_tensor(IG[0:16, 8, :], ps[0:16, 64:68], BG[0:16, 0:4], ALU.add), r=[pk], w=["IGs"])
            P.op("dve", lambda e: e.tensor_tensor(SP[0:16, 8, :], ps[0:16, 68:72], BG[0:16, 4:8], ALU.add), r=[pk], w=["SPs"])
            P.op("act", lambda e: e.activation(SP[0:16, 8, :], SP[0:16, 8, :], AF.Exp, scale=-1.0), r=["SPs"], w=["SPs"])
            P.op("act", lambda e: e.activation(SP[0:16, 8, :], SP[0:16, 8, :], AF.Ln, bias=1.0), r=["SPs"], w=["SPs"])
        P.op("act", lambda e: e.activation(SP[:, 0:8, :], SP[:, 0:8, :], AF.Exp, scale=-1.0), r=["SP"], w=["SP"])
        P.op("act", lambda e: e.activation(SP[:, 0:8, :], SP[:, 0:8, :], AF.Ln, bias=1.0), r=["SP"], w=["SP"])
        ps2, pk2 = bank()
        P.op("pe", lambda e: e.matmul(ps2[:, 0:32], TRI, SP[:, 0:8, :].rearrange("p c h -> p (c h)"), start=True, stop=True), r=["SP", "C"], w=[pk2])
        cs3 = ps2[:, 0:32].rearrange("p (c h) -> p c h", c=8)
        P.op("dve", lambda e: e.tensor_copy(CSA[:, 0, :, :], cs3), r=[pk2], w=["CSA0"])
        P.op("dve", lambda e: e.tensor_tensor(AA[:], IG[:, 0:8, :], CSA[:, 0, :, :], ALU.add), r=["IG", "CSA0"], w=["AA"])
        for c in range(8):
            psa, pka = bank()

            def mm2(e, psa=psa, c=c):
                for h in range(4):
                    ins = e.matmul(psa[:, h * 128:(h + 1) * 128], AA[:, c, h:h + 1].to_broadcast([128, 128]), IDF, start=True, stop=True)
                return ins
            P.op("pe", mm2, r=["AA", "C"], w=[pka])
            P.op("dve", lambda e, psa=psa: e.tensor_tensor(TMPA[:], psa[:].rearrange("p (h s) -> p h s", h=4),
                                                           NEGM.unsqueeze(1).to_broadcast([128, 4, 128]), ALU.add), r=[pka, "C"], w=["TMPA"])
            P.op("dve", lambda e, c=c: e.tensor_reduce(CSA[:, 1, c, :], TMPA[:], AX.X, ALU.max), r=["TMPA"], w=["CSA1_%d" % c])
        ps3, pk3 = bank()
        P.op("pe", lambda e: e.matmul(ps3[:, 0:64], SEL127, CSA[:].rearrange("p a c h -> p (a c h)"), start=True, stop=True),
             r=["CSA0", "C"] + ["CSA1_%d" % c for c in range(8)], w=[pk3])
        P.op("dve", lambda e: e.tensor_copy(LST[:].rearrange("p a c h -> p (a c h)"), ps3[:, 0:64]), r=[pk3], w=["LST"])

    def m_chain():
        for c in range(8):
            P.op("dve", lambda e, c=c: e.tensor_tensor(MML[:, c, :], MCH[:, c, :], LST[:, 1, c, :], ALU.max), r=["MCH", "LST"], w=["MML"])
            P.op("dve", lambda e, c=c: e.tensor_tensor(MCH[:, c + 1, :], MML[:, c, :], LST[:, 0, c, :], ALU.subtract), r=["MML", "LST"], w=["MCH"])

    P.dma("pool", WG[:], w_in_v[:, :, 8192:8200], w=["WG"], sem="WG")
    build_xt(x_pre, 8)
    P.dma("sp", CS[:, :, 0:1024], cs_pre_d, w=["CS", "XIN0", "XIN1"], sem="CS")
    P.op("pool", lambda e: e.memset(VEXT[:, :, 256:257], 1.0), w=["VONE"])
    gates(8, False)
    P.op("dve", lambda e: e.memset(MCH[:, 0, :], 0.0), w=["MCH"])
    m_chain()
    P.op("dve", lambda e: e.tensor_tensor(T32[:], MCH[:, 0:8, :], MML[:], ALU.subtract), r=["MCH", "MML"], w=["T32"])
    P.op("dve", lambda e: e.memset(ECH[:, 7, :], 0.0), w=["ECH"])
    for c in range(6, -1, -1):
        P.op("dve", lambda e, c=c: e.tensor_tensor(ECH[:, c, :], ECH[:, c + 1, :], T32[:, c + 1, :], ALU.add), r=["ECH", "T32"], w=["ECH"])
    P.op("dve", lambda e: e.tensor_tensor(T32[:], ECH[:, 0:8, :], MML[:], ALU.subtract), r=["ECH", "MML", "T32"], w=["T32"])
    P.op("dve", lambda e: e.tensor_tensor(T32[:], T32[:], AA[:], ALU.add), r=["T32", "AA"], w=["T32"])
    P.op("act", lambda e: e.activation(SWT[:], T32[:], AF.Exp), r=["T32"], w=["SWT"])
    P.op("dve", lambda e: e.tensor_scalar(MINIT[:], MCH[:, 8, :], FLAG, None, ALU.mult), r=["MCH", "C"], w=["MINIT"])

    for hh in range(8):
        ret = hh < 4
        h = hh % 4
        if hh == 0:
            nxtP = (load_w(1024), load_w(2048))
        (Wk, wkk), (Wv, wvk) = nxtP
        for t0 in (0, 512):
            if ret:
                rotary(Wk, wkk, KT, "KT", t0, 512)
            else:
                plain_fm(Wk, wkk, KT, "KT", t0, 512, 0.0625)
        for tt in range(8):
            ps, pk = proj_tm(Wv, wvk, tt, 128)
            copy_op(ev_eng(), VEXT[:, tt, 0:256], ps[:, 0:256], r=[pk], w=["VEXT%d" % tt])
        if hh < 7:
            r2, h2 = (hh + 1) < 4, (hh + 1) % 4
            nxtP = (load_w((1024 if r2 else 5120) + 256 * h2), load_w((2048 if r2 else 6144) + 256 * h2))
        for c in range(8):
            make_ktok(c, KPRE[:, c, h:h + 1] if ret else SWT[:, c, h:h + 1])
        for j in range(2):
            ps, pk = bank()

            def acc(e, ps=ps, j=j):
                for c in range(8):
                    ins = e.matmul(ps[:, 0:257], KTOK[:, c, j * 128:(j + 1) * 128], VEXT[:, c, 0:257], start=(c == 0), stop=(c == 7))
                return ins
            P.op("pe", acc, r=["KTOK%d" % c for c in range(8)] + ["VEXT%d" % c for c in range(8)] + ["VONE"], w=[pk])
            P.op("dve", lambda e, ps=ps, j=j, hh=hh: e.tensor_scalar(SINIT[:, hh, j, 0:257], ps[:, 0:257], FLAG, None, ALU.mult),
                 r=[pk, "C"], w=["SINIT%d_%d" % (hh, j)])

    P.barrier(lambda e: e.memset(BARS[:, 0:1], 0.0))

    build_xt(x_own, 8, x_smp)
    P.dma("sp", CS[:], cs_own_d, w=["CS", "XIN0", "XIN1"], sem="CS")
    gates(8, True)
    P.op("dve", lambda e: e.tensor_copy(MCH[:, 0, :], MINIT[:]), r=["MINIT"], w=["MCH"])
    m_chain()
    P.dma("sp", o_m_p, MCH[0:1, 8, :], r=["MCH"], sem="st_mp")
    P.op("dve", lambda e: e.tensor_tensor(MM[:], MCH[:, 0:8, :], CSA[:, 1, :, :], ALU.max), r=["MCH"] + ["CSA1_%d" % c for c in range(8)], w=["MM"])
    P.op("dve", lambda e: e.tensor_scalar(NEGMM[:], MM[:], -1.0, None, ALU.mult), r=["MM"], w=["NEGMM"])
    P.op("dve", lambda e: e.tensor_tensor(T32[:], MCH[:, 0:8, :], MM[:], ALU.subtract), r=["MCH", "MM"], w=["T32"])
    P.op("act", lambda e: e.activation(IW[:], T32[:], AF.Exp), r=["T32"], w=["IW"])
    P.op("dve", lambda e: e.tensor_tensor(T32[:], CSA[:, 0, :, :], MM[:], ALU.subtract), r=["CSA0", "MM", "IW"], w=["T32"])
    P.op("act", lambda e: e.activation(EMT[:], T32[:], AF.Exp), r=["T32"], w=["EMT"])
    P.op("dve", lambda e: e.tensor_tensor(T32[:], AA[:], MML[:], ALU.subtract), r=["AA", "MML", "EMT"], w=["T32"])
    P.op("act", lambda e: e.activation(SWT[:], T32[:], AF.Exp), r=["T32"], w=["SWT"])
    P.op("dve", lambda e: e.tensor_tensor(T32[:], MCH[:, 0:8, :], MML[:], ALU.subtract), r=["MCH", "MML", "SWT"], w=["T32"])
    P.op("act", lambda e: e.activation(SD[:], T32[:], AF.Exp), r=["T32"], w=["SD"])
    s_mt, s_dw, s_iw, s_emt, s_den, s_r, s_t = (SMP[:, i * 4:(i + 1) * 4] for i in range(7))
    P.op("dve", lambda e: e.tensor_tensor(s_t, MS0[:], SP[0:16, 8, :], ALU.subtract), r=["C", "SPs"], w=["s_t"])
    P.op("dve", lambda e: e.tensor_tensor(s_mt, s_t, IG[0:16, 8, :], ALU.max), r=["s_t", "IGs"], w=["s_mt"])
    P.op("dve", lambda e: e.tensor_tensor(s_t, s_t, s_mt, ALU.subtract), r=["s_t", "s_mt"], w=["s_t"])
    P.op("act", lambda e: e.activation(s_iw, s_t, AF.Exp), r=["s_t"], w=["s_iw"])
    P.op("dve", lambda e: e.tensor_tensor(s_t, IG[0:16, 8, :], s_mt, ALU.subtract), r=["s_iw", "IGs", "s_mt"], w=["s_t"])
    P.op("act", lambda e: e.activation(s_dw, s_t, AF.Exp), r=["s_t"], w=["s_dw"])
    P.op("act", lambda e: e.activation(s_emt, s_mt, AF.Exp, scale=-1.0), r=["s_mt"], w=["s_emt"])
    P.dma("sp", o_m_s, s_mt, r=["s_mt"], sem="st_ms")
    P.op("dve", lambda e: e.tensor_tensor(DG[:], s_iw.unsqueeze(1).to_broadcast([16, 16, 4]),
                                          IDF[0:16, 0:16].unsqueeze(2).to_broadcast([16, 16, 4]), ALU.mult), r=["s_iw", "C"], w=["DG"])
    psw, pkw = bank()
    P.op("pe", lambda e: e.matmul(psw[:, 0:64], ONES[0:16, :], DG[:].rearrange("p b h -> p (b h)"), start=True, stop=True), r=["DG", "C"], w=[pkw])
    P.op("dve", lambda e: e.tensor_copy(IWB[:].rearrange("p b h -> p (b h)"), psw[:, 0:64]), r=[pkw], w=["IWB"])

    def norm_gate_out(S, src, skey, np_, fac, fac2, fkeys, gate, gkey, et0, t0):
        i = S["id"]
        BNS, BNA, YF, Y2 = S["BNS"], S["BNA"], S["YF"], S["Y2"]
        kb, ka, ks2, ky, ky2 = "BNS%d" % i, "BNA%d" % i, "s2_%d" % i, "YF%d" % i, "Y2_%d" % i
        P.op("dve", lambda e: e.bn_stats(BNS[0:np_, :], src), r=[skey], w=[kb])
        P.op("dve", lambda e: e.bn_aggr(BNA[0:np_, :], BNS[0:np_, :]), r=[kb], w=[ka])
        s2 = S["SC1"][0:np_, 0:1]
        if fac is None:
            P.op("act", lambda e: e.activation(s2, BNA[0:np_, 1:2], AF.Ln, bias=EPSG[0:np_, :]), r=[ka], w=[ks2])
            P.op("act", lambda e: e.activation(s2, s2, AF.Exp, scale=-0.5), r=[ks2], w=[ks2])
        else:
            P.op("act", lambda e: e.activation(s2, BNA[0:np_, 1:2], AF.Ln, bias=EPSG[0:np_, :], scale=fac2), r=[ka] + fkeys, w=[ks2])
            P.op("act", lambda e: e.activation(s2, s2, AF.Exp, scale=-0.5), r=[ks2], w=[ks2])
            P.op("dve", lambda e: e.tensor_scalar(s2, s2, fac, None, ALU.mult), r=[ks2] + fkeys, w=[ks2])
        P.op("dve", lambda e: e.tensor_scalar(YF[0:np_, :], src, BNA[0:np_, 0:1], s2, ALU.subtract, ALU.mult), r=[skey, ka, ks2], w=[ky])
        P.op("pool", lambda e: e.tensor_tensor(Y2[0:np_, :], YF[0:np_, :], gate, ALU.mult), r=[ky, gkey], w=[ky2])
        pb, pk = bbank()

        def tr(e):
            for j in range(2):
                ins = e.transpose(pb[:, j * 128:j * 128 + np_], Y2[0:np_, j * 128:(j + 1) * 128], IDB[0:np_, 0:np_])
            return ins
        P.op("pe", tr, r=[ky2], w=[pk])
        for j in range(2):
            P.op("act", lambda e, j=j: e.mul(MIXT[:, et0 + j, t0:t0 + np_], pb[:, j * 128:j * 128 + np_], GNW[:, et0 + j:et0 + j + 1]),
                 r=[pk, "C"], w=["MIXT%d_%d" % (et0 + j, t0)])

    smp_ctr = [0]

    for hh in range(8):
        ret = hh < 4
        h = hh % 4
        base = 0 if ret else 4096
        if hh == 0:
            nxtO = [load_w(q * 1024) for q in range(4)]
        (Wq, wqk), (Wk, wkk), (Wv, wvk), (Wg, wgk) = nxtO
        for (W, wk, DST, dk, sc) in ((Wq, wqk, QT, "QT", None), (Wk, wkk, KT, "KT", 0.0625)):
            for t0, n in ((0, 512), (512, 512), (1024, 16)):
                if ret:
                    rotary(W, wk, DST, dk, t0, n)
                else:
                    plain_fm(W, wk, DST, dk, t0, n, sc)
        for tt in range(9):
            np_ = 128 if tt < 8 else 16
            ps, pk = proj_tm(Wv, wvk, tt, np_)
            copy_op("dve", VEXT[0:np_, tt, 0:256], ps[0:np_, 0:256], r=[pk], w=["VEXT%d" % tt])
            ps, pk = proj_tm(Wg, wgk, tt, np_)
            gfn = AF.Silu if ret else AF.Sigmoid
            P.op("act", lambda e, ps=ps, tt=tt, np_=np_, gfn=gfn: e.activation(GATE[0:np_, tt, :], ps[0:np_, 0:256], gfn),
                 r=[pk], w=["GATE%d" % tt])
        if hh < 7:
            b2 = 0 if (hh + 1) < 4 else 4096
            nxtO = [load_w(b2 + q * 1024 + 256 * ((hh + 1) % 4)) for q in range(4)]
        for c in range(8):
            make_ktok(c, KDEC[:, h:h + 1] if ret else SWT[:, c, h:h + 1])
        for j in range(2):
            P.op("pool", lambda e, j=j, hh=hh: e.tensor_copy(SF[0][:, j, 0:257], SINIT[:, hh, j, 0:257]), r=["SINIT%d_%d" % (hh, j)], w=["SF0_%d" % j])
            P.op("act", lambda e, j=j: e.copy(SBF[0][:, j, 0:257], SF[0][:, j, 0:257]), r=["SF0_%d" % j], w=["SBF0_%d" % j])
        skeys = blk_keys("QT", 1024, 16)
        P.op("dve", lambda e: e.tensor_tensor(QM[:], QT[:, :, 1024:1040].unsqueeze(3).to_broadcast([128, 2, 16, 16]),
                                              ID16[:].unsqueeze(1).to_broadcast([128, 2, 16, 16]), ALU.mult), r=skeys + ["C"], w=["QM"])
        pb, pkb = bbank()

        def trs(e, pb=pb):
            for j in range(2):
                e.transpose(pb[0:16, j * 128:(j + 1) * 128], KT[:, j, 1024:1040], IDB)
            for j in range(2):
                ins = e.transpose(pb[0:16, 256 + j * 128:256 + (j + 1) * 128], QT[:, j, 1024:1040], IDB)
            return ins
        P.op("pe", trs, r=skeys + blk_keys("KT", 1024, 16), w=[pkb])
        if ret:
            P.op("dve", lambda e, pb=pb: e.tensor_scalar(KSD[:], pb[0:16, 0:256], 0.0625, None, ALU.mult), r=[pkb], w=["KSD"])
        else:
            P.op("dve", lambda e, pb=pb, h=h: e.tensor_scalar(KSD[:], pb[0:16, 0:256], s_dw[:, h:h + 1], None, ALU.mult), r=[pkb, "s_dw"], w=["KSD"])
            P.op("dve", lambda e, pb=pb: e.tensor_copy(QS[:], pb[0:16, 256:512]), r=[pkb], w=["QS"])
            P.op("dve", lambda e, h=h: e.scalar_tensor_tensor(NNEW[:, h * 256:(h + 1) * 256], NSM[:, h * 256:(h + 1) * 256], s_iw[:, h:h + 1], KSD[:], ALU.mult, ALU.add),
                 r=["C", "s_iw", "KSD"], w=["NNEW%d" % h])
            P.op("dve", lambda e, h=h: e.tensor_tensor(TQ[:], QS[:], NNEW[:, h * 256:(h + 1) * 256], ALU.mult), r=["QS", "NNEW%d" % h], w=["TQ"])
            P.op("dve", lambda e, h=h: e.tensor_reduce(s_den[:, h:h + 1], TQ[:], AX.X, ALU.add), r=["TQ"], w=["s_den%d" % h])
            P.op("dve", lambda e, h=h: e.tensor_scalar(s_r[:, h:h + 1], s_den[:, h:h + 1], -1.0, None, ALU.mult), r=["s_den%d" % h], w=["s_r%d" % h])
            P.op("dve", lambda e, h=h: e.tensor_tensor(s_den[:, h:h + 1], s_den[:, h:h + 1], s_r[:, h:h + 1], ALU.max), r=["s_den%d" % h, "s_r%d" % h], w=["s_den%d" % h])
            P.op("dve", lambda e, h=h: e.tensor_tensor(s_den[:, h:h + 1], s_den[:, h:h + 1], s_emt[:, h:h + 1], ALU.max), r=["s_den%d" % h, "s_emt"], w=["s_den%d" % h])
            P.op("dve", lambda e, h=h: e.reciprocal(s_r[:, h:h + 1], s_den[:, h:h + 1]), r=["s_den%d" % h], w=["s_r%d" % h])
            P.op("dve", lambda e, h=h: e.tensor_tensor(s_t[:, h:h + 1], s_r[:, h:h + 1], s_r[:, h:h + 1], ALU.mult), r=["s_r%d" % h], w=["s_t%d" % h])
        pos_, pkos = PSX, "PSX"
        src_state = st_ret if ret else st_C
        dst_state = o_retS_s if ret else o_C_s

        def sample_token(b):
            sl = smp_ctr[0] % NSLOT
            smp_ctr[0] += 1
            P.dma("sp", SST[sl][:], src_state[b, h].rearrange("(j p) v -> p j v", p=128), w=["SST%d" % sl], sem="SST%d" % sl)
            P.op("dve", lambda e: e.tensor_scalar(KMB[sl][:], KSD[:], IDF[0:16, b:b + 1], None, ALU.mult), r=["KSD", "C"], w=["KMB%d" % sl])
            pst, pks = bank()

            def ou_mm(e):
                for j in range(2):
                    ins = e.matmul(pst[:, j * 256:(j + 1) * 256], KMB[sl][:, j * 128:(j + 1) * 128], VEXT[0:16, 8, 0:256], start=True, stop=True)
                return ins
            P.op("pe", ou_mm, r=["KMB%d" % sl, "VEXT8"], w=[pks])
            scal = G[h] if ret else IWB[:, b, h:h + 1]
            flat = SST[sl][:].rearrange("p j v -> p (j v)")
            P.op("dve", lambda e: e.scalar_tensor_tensor(flat, flat, scal, pst[:, 0:512], ALU.mult, ALU.add),
                 r=[pks, "SST%d" % sl, "IWB"], w=["SST%d" % sl])
            P.op("act", lambda e: e.copy(SNB[sl][:], SST[sl][:]), r=["SST%d" % sl], w=["SNB%d" % sl])
            P.dma("sp", dst_state[b, h].rearrange("(j p) v -> p j v", p=128), SST[sl][:], r=["SST%d" % sl], sem="st_SST%d" % sl)

            def os_mm(e):
                for j in range(2):
                    ins = e.matmul(pos_[0:16, 0:256], QM[:, j, b, :], SNB[sl][:, j, :], start=(b == 0 and j == 0), stop=(b == 15 and j == 1))
                return ins
            P.op("pe", os_mm, r=["QM", "SNB%d" % sl], w=[pkos])

        for c in range(8):
            S = SETS[c % (NSET if ret else NSET + 3)]
            sid = S["id"]
            cur, nxt = c % 2, (c + 1) % 2
            cb = slice(c * 128, (c + 1) * 128)
            qk = blk_keys("QT", c * 128, 128); kk = blk_keys("KT", c * 128, 128)
            pst, pks = bank()

            def st_mm(e, pst=pst, c=c):
                for j in range(2):
                    ins = e.matmul(pst[:, j * 256:j * 256 + 256], KTOK[:, c, j * 128:(j + 1) * 128], VEXT[:, c, 0:256], start=True, stop=True)
                return ins
            P.op("pe", st_mm, r=["KTOK%d" % c, "VEXT%d" % c], w=[pks])
            if not ret:
                pn, pkn = bank()

                def n_mm(e, pn=pn, c=c):
                    for j in range(2):
                        ins = e.matmul(pn[:, j:j + 1], KTOK[:, c, j * 128:(j + 1) * 128], VEXT[:, c, 256:257], start=True, stop=True)
                    return ins
                P.op("pe", n_mm, r=["KTOK%d" % c, "VONE"], w=[pkn])
            for j in range(2):
                scal = G[h] ** 128 if ret else SD[:, c, h:h + 1]
                P.op("dve", lambda e, pst=pst, j=j, scal=scal, cur=cur, nxt=nxt: e.scalar_tensor_tensor(SF[nxt][:, j, 0:256], SF[cur][:, j, 0:256], scal, pst[:, j * 256:(j + 1) * 256], ALU.mult, ALU.add),
                     r=[pks, "SF%d_%d" % (cur, j), "SD"], w=["SF%d_%d" % (nxt, j)])
                if not ret:
                    P.op("dve", lambda e, pn=pn, j=j, c=c, h=h, cur=cur, nxt=nxt: e.scalar_tensor_tensor(SF[nxt][:, j, 256:257], SF[cur][:, j, 256:257], SD[:, c, h:h + 1], pn[:, j:j + 1], ALU.mult, ALU.add),
                         r=[pkn, "SF%d_%d" % (cur, j), "SD"], w=["SF%d_%d" % (nxt, j)])
                P.op("act", lambda e, j=j, nxt=nxt, nc_=(256 if ret else 257): e.copy(SBF[nxt][:, j, 0:nc_], SF[nxt][:, j, 0:nc_]), r=["SF%d_%d" % (nxt, j)], w=["SBF%d_%d" % (nxt, j)])
            SCM, kscm = S["SCM"], "SCM%d" % sid
            sbk = ["SBF%d_0" % cur, "SBF%d_1" % cur]
            if ret:
                ps, pk = bank()

                def sc_mm(e, ps=ps, cb=cb):
                    for j in range(2):
                        ins = e.matmul(ps[:, 0:128], KT[:, j, cb], QT[:, j, cb], start=(j == 0), stop=(j == 1))
                    return ins
                P.op("pe", sc_mm, r=qk + kk, w=[pk])
                P.op("dve", lambda e, ps=ps, h=h, SCM=SCM: e.tensor_tensor(SCM[:], ps[:, 0:128], RMASK[:, h, :], ALU.mult), r=[pk, "C"], w=[kscm])
                po, pko = bank()

                def o_mm(e, po=po, cb=cb, c=c, SCM=SCM, cur=cur):
                    e.matmul(po[:, 0:256], SCM[:], VEXT[:, c, 0:256], start=True, stop=False)
                    for j in range(2):
                        ins = e.matmul(po[:, 0:256], QT[:, j, cb], SBF[cur][:, j, 0:256], start=False, stop=(j == 1))
                    return ins
                P.op("pe", o_mm, r=[kscm, "VEXT%d" % c] + sbk + qk, w=[pko])
                OSB, kosb = S["NUMER"], "NUMER%d" % sid
                P.op("act", lambda e, po=po, OSB=OSB: e.copy(OSB[:], po[:, 0:256]), r=[pko], w=[kosb])
                norm_gate_out(S, OSB[:], kosb, 128, QD[:, h:h + 1], QD2[:, h:h + 1], ["C"], GATE[:, c, :], "GATE%d" % c, hh * 2, c * 128)
            else:
                DW, kdw = S["DW"], "DW%d" % sid
                SCMT, kscmt = S["SCMT"], "SCMT%d" % sid
                INTRA, kin = S["INTRA"], "INTRA%d" % sid
                NUMER, knu = S["NUMER"], "NUMER%d" % sid
                psa, pka = bank()
                P.op("pe", lambda e, psa=psa, c=c, h=h: e.matmul(psa[:, 0:128], AA[:, c, h:h + 1].to_broadcast([128, 128]), IDF, start=True, stop=True),
                     r=["AA", "C"], w=[pka])
                P.op("dve", lambda e, psa=psa, DW=DW: e.tensor_tensor(DW[:], psa[:, 0:128], NEGM, ALU.add), r=[pka, "C"], w=[kdw] + (ALIAS if sid >= NSET else []))
                P.op("act", lambda e, c=c, h=h, DW=DW: e.activation(DW[:], DW[:], AF.Exp, bias=NEGMM[:, c, h:h + 1]), r=[kdw, "NEGMM"], w=[kdw])
                ps, pk = bank()

                def sc_mm(e, ps=ps, cb=cb):
                    for j in range(2):
                        ins = e.matmul(ps[:, 0:128], QT[:, j, cb], KT[:, j, cb], start=(j == 0), stop=(j == 1))
                    return ins
                P.op("pe", sc_mm, r=qk + kk, w=[pk])
                P.op("dve", lambda e, ps=ps, SCM=SCM, DW=DW: e.tensor_tensor(SCM[:], ps[:, 0:128], DW[:], ALU.mult), r=[pk, kdw], w=[kscm])
                pb, pkb = bbank()
                P.op("pe", lambda e, pb=pb, SCM=SCM: e.transpose(pb[:, 0:128], SCM[:], IDB), r=[kscm], w=[pkb])
                copy_op("act", SCMT[:], pb[:, 0:128], r=[pkb], w=[kscmt])
                pi, pki = bank()
                P.op("pe", lambda e, pi=pi, c=c, SCMT=SCMT: e.matmul(pi[:, 0:257], SCMT[:], VEXT[:, c, 0:257], start=True, stop=True), r=[kscmt, "VEXT%d" % c, "VONE"], w=[pki])
                pe_, pke = bank()

                def in_mm(e, pe_=pe_, cb=cb, cur=cur):
                    for j in range(2):
                        ins = e.matmul(pe_[:, 0:257], QT[:, j, cb], SBF[cur][:, j, 0:257], start=(j == 0), stop=(j == 1))
                    return ins
                P.op("pe", in_mm, r=qk + sbk + [kscmt], w=[pke])
                copy_op("act", INTRA[:, 0:257], pi[:, 0:257], r=[pki], w=[kin])
                iw = IW[:, c, h:h + 1]
                SC1 = S["SC1"]
                den, rr, rr2 = SC1[:, 1:2], SC1[:, 2:3], SC1[:, 3:4]
                kd, kr, kr2 = "den%d" % sid, "rr%d" % sid, "rr2_%d" % sid
                P.op("dve", lambda e, pe_=pe_, iw=iw, den=den, INTRA=INTRA: e.scalar_tensor_tensor(den, pe_[:, 256:257], iw, INTRA[:, 256:257], ALU.mult, ALU.add), r=[pke, kin, "IW"], w=[kd])
                P.op("dve", lambda e, den=den, rr=rr: e.tensor_scalar(rr, den, -1.0, None, ALU.mult), r=[kd], w=[kr])
                P.op("dve", lambda e, den=den, rr=rr: e.tensor_tensor(den, den, rr, ALU.max), r=[kd, kr], w=[kd])
                P.op("dve", lambda e, c=c, h=h, den=den: e.tensor_tensor(den, den, EMT[:, c, h:h + 1], ALU.max), r=[kd, "EMT"], w=[kd])
                P.op("dve", lambda e, den=den, rr=rr: e.reciprocal(rr, den), r=[kd], w=[kr])
                P.op("dve", lambda e, rr=rr, rr2=rr2: e.tensor_tensor(rr2, rr, rr, ALU.mult), r=[kr], w=[kr2])
                P.op("dve", lambda e, pe_=pe_, iw=iw, NUMER=NUMER, INTRA=INTRA: e.scalar_tensor_tensor(NUMER[:], pe_[:, 0:256], iw, INTRA[:, 0:256], ALU.mult, ALU.add), r=[pke, kin, "IW"], w=[knu])
                norm_gate_out(S, NUMER[:], knu, 128, rr, rr2, [kr, kr2], GATE[:, c, :], "GATE%d" % c, hh * 2, c * 128)
            sample_token(2 * c)
            sample_token(2 * c + 1)
        dst = (o_retS_p if ret else o_C_p)[h].rearrange("(j p) v -> p j v", p=128)
        P.dma("sp", dst, SF[0][:, :, 0:256], r=["SF0_0", "SF0_1"], sem="st_SF")
        if not ret:
            for j in range(2):
                P.dma("sp", bass.AP(o_n_p.tensor, h * 256 + j * 128, [[1, 128], [1, 1]]), SF[0][:, j, 256:257], r=["SF0_0", "SF0_1"], sem="st_SF")
        S = SETS[0]
        if ret:
            norm_gate_out(S, pos_[0:16, 0:256], pkos, 16, None, None, [], GATE[0:16, 8, :], "GATE8", hh * 2, 1024)
        else:
            norm_gate_out(S, pos_[0:16, 0:256], pkos, 16, s_r[:, h:h + 1], s_t[:, h:h + 1], ["s_r%d" % h, "s_t%d" % h], GATE[0:16, 8, :], "GATE8", hh * 2, 1024)
    P.dma("sp", o_n_s, NNEW[:], r=["NNEW%d" % h for h in range(4)], sem="st_ns")

    P.barrier(lambda e: e.memset(BARS[:, 0:1], 0.0))
    if dbg:
        dbg_mixt = dout("dbg_mixt", [128, 16, TOK])
        P.dma("pool", dbg_mixt, MIXT[:], sem="dbg1")

    AR.off = markA
    X1 = AR.alloc([128, 9, D], F32)
    X1T = AR.alloc([128, 16, TOK], BF16)
    LNG = AR.alloc([128, D], F32)
    LNB = AR.alloc([128, D], F32)
    LNT = [dict(ST8=AR.alloc([128, 4, 6], F32), LNA=AR.alloc([128, 2], F32), LNR=AR.alloc([128, 1], F32), NMR=AR.alloc([128, 1], F32)) for _ in range(3)]
    markB = AR.off
    WO = [AR.alloc([128, 16, 512], BF16) for _ in range(2)]

    def bcast_row(row_ap):
        return bass.AP(row_ap.tensor, row_ap.offset, [[0, 128], [1, D]])

    for t2 in range(0, 8, 2):
        P.dma("sp", X1[:, t2:t2 + 2, :], x_own[t2 * 128:(t2 + 2) * 128, :].rearrange("(t p) d -> p t d", p=128), r=["XCH"],
              w=["X1_%d" % t2, "X1_%d" % (t2 + 1), "XCH"], sem="X1_%d" % t2, cost=7.0)
    P.dma("sp", X1[0:16, 8, :], x_smp, r=["XCH"], w=["X1_8", "XCH"], sem="X1_8")
    P.dma("sp", LNG[:], bcast_row(ln_d[0]), r=["XCH"], w=["LNG", "XCH"], sem="LNG", cost=5.0)
    P.dma("sp", LNB[:], bcast_row(ln_d[1]), r=["XCH"], w=["LNB", "XCH"], sem="LNB", cost=5.0)
    P.op("act", lambda e: e.mul(LNG[:], LNG[:], ALPHA), r=["LNG"], w=["LNG"], cost=1.9)
    P.op("act", lambda e: e.mul(LNB[:], LNB[:], ALPHA), r=["LNB"], w=["LNB"], cost=1.9)
    mix_keys = lambda tt: ["MIXT%d_%d" % (et, t0) for et in range(16) for t0 in ([tt * 128] if tt < 8 else [1024])]
    w_out_v = w_out.rearrange("(et p) c -> p et c", p=128)

    def dense_tm(ACT_T, akeys_fn, wv, nk, ring, rname, evac):
        for cb in range(4):
            s = cb % len(ring)
            P.dma("pool", ring[s][:, 0:nk, :], wv[:, :, cb * 512:(cb + 1) * 512], w=["%s%d" % (rname, s)], sem="%s%d" % (rname, s), cost=16.0)
            for tt in range(9):
                np_ = 128 if tt < 8 else 16
                ps, pk = bank()

                def mm(e, ps=ps, tt=tt, np_=np_, s=s):
                    for kt in range(nk):
                        ins = e.matmul(ps[0:np_, :], ACT_T[:, kt, tt * 128:tt * 128 + np_], ring[s][:, kt, :], start=(kt == 0), stop=(kt == nk - 1))
                    return ins
                P.op("pe", mm, r=["%s%d" % (rname, s)] + akeys_fn(tt), w=[pk], cost=0.3 + nk * 0.215)
                evac(ps, pk, tt, np_, cb)

    def ev_out(ps, pk, tt, np_, cb):
        dst = X1[0:np_, tt, cb * 512:(cb + 1) * 512]
        P.op("dve", lambda e: e.scalar_tensor_tensor(dst, dst, ALPHA, ps[0:np_, :], ALU.mult, ALU.add), r=[pk, "X1_%d" % tt], w=["X1_%d" % tt])

    dense_tm(MIXT, mix_keys, w_out_v, 16, WO, "WO", ev_out)

    def layer_norm(tt, np_, extra_r):
        xk = "X1_%d" % tt
        xv = X1[0:np_, tt, :]
        L = LNT[tt % 3]
        i = tt % 3
        ST8, LNA, LNR, NMR = L["ST8"], L["LNA"], L["LNR"], L["NMR"]
        k8, ka, kr, kn = "ST8_%d" % i, "LNA%d" % i, "LNR%d" % i, "NMR%d" % i

        def bn(e):
            for q in range(4):
                ins = e.bn_stats(ST8[0:np_, q, :], X1[0:np_, tt, q * 512:(q + 1) * 512])
            return ins
        P.op("dve", bn, r=[xk], w=[k8], cost=2.4)
        P.op("dve", lambda e: e.bn_aggr(LNA[0:np_, :], ST8[0:np_, :, :]), r=[k8], w=[ka])
        P.op("act", lambda e: e.activation(LNR[0:np_, :], LNA[0:np_, 1:2], AF.Ln, bias=EPSL[0:np_, :]), r=[ka], w=[kr])
        P.op("act", lambda e: e.activation(LNR[0:np_, :], LNR[0:np_, :], AF.Exp, scale=-0.5), r=[kr], w=[kr])
        P.op("dve", lambda e: e.tensor_scalar(NMR[0:np_, :], LNA[0:np_, 0:1], -1.0, LNR[0:np_, :], ALU.mult, ALU.mult), r=[ka, kr], w=[kn])
        P.op("act", lambda e: e.activation(xv, xv, AF.Identity, bias=NMR[0:np_, :], scale=LNR[0:np_, :]), r=[xk, kr, kn], w=[xk], cost=1.9)
        P.op("dve", lambda e: e.tensor_tensor(xv, xv, LNG[0:np_, :], ALU.mult), r=[xk, "LNG"] + extra_r, w=[xk], cost=2.3)
        P.op("pool", lambda e: e.tensor_tensor(xv, xv, LNB[0:np_, :], ALU.add), r=[xk, "LNB"], w=[xk], cost=4.0)

    for tt in range(9):
        np_ = 128 if tt < 8 else 16
        layer_norm(tt, np_, [])
        for g in range(4):
            ps, pk = bank()

            def tr(e, ps=ps, g=g, tt=tt, np_=np_):
                for q in range(4):
                    dt = g * 4 + q
                    ins = e.transpose(ps[:, q * 128:q * 128 + np_], X1[0:np_, tt, dt * 128:(dt + 1) * 128], IDF[0:np_, 0:np_])
                return ins
            P.op("pe", tr, r=["X1_%d" % tt, "C"], w=[pk])
            out = X1T[:, g * 4:(g + 1) * 4, tt * 128:tt * 128 + np_]
            inn = ps[:].rearrange("p (q t) -> p q t", q=4)[:, :, 0:np_]
            copy_op(ev_eng(), out, inn, r=[pk], w=["X1T%d_%d" % (tt, g)], scale=1.0 / ALPHA)

    P.barrier(lambda e: e.memset(BARS[:, 0:1], 0.0))
    if dbg:
        dbg_x1t = dout("dbg_x1t", [128, 16, TOK])
        P.dma("pool", dbg_x1t, X1T[:], sem="dbg2")

    AR.off = markB
    WB = [AR.alloc([128, 8, 512], BF16) for _ in range(3)]
    PT = AR.alloc([128, 2, TOK], BF16)
    SIG = AR.alloc([128, 512], F32)
    WPE = AR.alloc([128, 2, D], BF16)
    AR.off = mixt_off
    HT = AR.alloc([128, 8, TOK], BF16)
    W1 = [AR.alloc([128, 16, 128], BF16) for _ in range(4)]
    assert AR.off <= markA, (AR.off, markA)
    PIN = HT[:, 0:3, :].rearrange("p a t -> p (a t)")[:, 0:9 * 256].rearrange("p (t c) -> p t c", t=9)

    x1t_keys = lambda tt: ["X1T%d_%d" % (tt, g) for g in range(4)]
    P.dma("pool", PIN[:, 0:8, :], p_own.rearrange("(t p) c -> p t c", p=128), w=["PIN"], sem="PIN")
    P.dma("pool", PIN[0:16, 8, :], p_smp, w=["PIN8"], sem="PIN8")
    P.dma("pool", WPE[:], w_pe.rearrange("(pt p) c -> p pt c", p=128), w=["WPE"], sem="WPE")
    for tt in range(9):
        np_ = 128 if tt < 8 else 16
        pb, pkb = bbank()

        def trp(e, pb=pb, tt=tt, np_=np_):
            for j in range(2):
                ins = e.transpose(pb[:, j * 128:j * 128 + np_], PIN[0:np_, tt, j * 128:(j + 1) * 128], IDB[0:np_, 0:np_])
            return ins
        P.op("pe", trp, r=["PIN", "PIN8"], w=[pkb])
        copy_op(ev_eng(), PT[:, :, tt * 128:tt * 128 + np_], pb[:, 0:256].rearrange("p (j t) -> p j t", j=2)[:, :, 0:np_], r=[pkb], w=["PT%d" % tt])
    w_pg_v = w_pg.rearrange("(dt p) c -> p dt c", p=128)
    for cb in range(8):
        s = cb % 3
        P.dma("pool", WB[s][:].rearrange("p a b -> p (a b)").rearrange("p (dt c) -> p dt c", dt=16), w_pg_v[:, :, cb * 256:(cb + 1) * 256], w=["WB%d" % s], sem="WB%d" % s, cost=9.0)
        Wb = WB[s][:].rearrange("p a b -> p (a b)").rearrange("p (dt c) -> p dt c", dt=16)
        for tt in range(9):
            np_ = 128 if tt < 8 else 16
            ps, pk = bank()

            def mm(e, ps=ps, tt=tt, np_=np_, Wb=Wb, cb=cb):
                for dt in range(16):
                    e.matmul(ps[0:np_, 0:256], X1T[:, dt, tt * 128:tt * 128 + np_], Wb[:, dt, :], start=(dt == 0), stop=(dt == 15))
                for pt in range(2):
                    ins = e.matmul(ps[0:np_, 256:512], PT[:, pt, tt * 128:tt * 128 + np_], WPE[:, pt, cb * 256:(cb + 1) * 256], start=(pt == 0), stop=(pt == 1))
                return ins
            P.op("pe", mm, r=["WB%d" % s, "WPE", "PT%d" % tt] + x1t_keys(tt), w=[pk], cost=2.3)
            P.op("act", lambda e, ps=ps, np_=np_: e.activation(SIG[0:np_, 0:256], ps[0:np_, 0:256], AF.Sigmoid), r=[pk], w=["SIG"])
            P.op("dve", lambda e, ps=ps, np_=np_: e.tensor_tensor(SIG[0:np_, 256:512], ps[0:np_, 256:512], SIG[0:np_, 0:256], ALU.mult), r=[pk, "SIG"], w=["SIG2"])
            dst = X1[0:np_, tt, cb * 256:(cb + 1) * 256]
            P.op("pool", lambda e, dst=dst, np_=np_: e.tensor_tensor(dst, dst, SIG[0:np_, 256:512], ALU.add), r=["SIG2", "X1_%d" % tt], w=["X1_%d" % tt])
    P.dma("sp", LNG[:], bcast_row(ln_d[2]), r=["WPE", "PIN"], w=["LNG"], sem="LNG")
    P.dma("sp", LNB[:], bcast_row(ln_d[3]), r=["WPE", "PIN"], w=["LNB"], sem="LNB")
    w1_v = w_ff1.rearrange("(dt p) f -> p dt f", p=128)
    w2_v = w_ff2.rearrange("(ft p) c -> p ft c", p=128)
    w1i = 0
    for fs in range(8):
        for fl in range(8):
            ft = fs * 8 + fl
            s = w1i % 4; w1i += 1
            P.dma("pool", W1[s][:], w1_v[:, :, ft * 128:(ft + 1) * 128], w=["W1_%d" % s], sem="W1_%d" % s, cost=5.5)
            for t0, n in ((0, 512), (512, 512), (1024, 16)):
                ps, pk = bank()

                def mm(e, ps=ps, s=s, t0=t0, n=n):
                    for dt in range(16):
                        ins = e.matmul(ps[:, 0:n], W1[s][:, dt, :], X1T[:, dt, t0:t0 + n], start=(dt == 0), stop=(dt == 15))
                    return ins
                tiles = range(t0 // 128, (t0 + n - 1) // 128 + 1)
                P.op("pe", mm, r=["W1_%d" % s] + [k for tt in tiles for k in x1t_keys(tt)], w=[pk], cost=0.3 + 16 * n / 2400.0)
                P.op("act", lambda e, ps=ps, n=n: e.activation(SIG[:, 0:n], ps[:, 0:n], AF.Relu), r=[pk], w=["SIG", "SIG2"])
                P.op("dve", lambda e, fl=fl, t0=t0, n=n: e.tensor_tensor(HT[:, fl, t0:t0 + n], SIG[:, 0:n], SIG[:, 0:n], ALU.mult), r=["SIG", "SIG2"], w=["HT%d_%d" % (fl, t0)])
        groups = [list(range(9))] if fs < 7 else [[0, 1, 2, 3], [4, 5, 6], [7, 8]]
        for gi, grp in enumerate(groups):
            for cb in range(4):
                s = (fs * 4 + cb + 2 + gi) % 3
                P.dma("pool", WB[s][:], w2_v[:, fs * 8:(fs + 1) * 8, cb * 512:(cb + 1) * 512], w=["WB%d" % s], sem="WB%d" % s, cost=9.0)
                for tt in grp:
                    np_ = 128 if tt < 8 else 16
                    t0k = (tt // 4) * 512 if tt < 8 else 1024
                    ps, pk = bank()

                    def mm(e, ps=ps, tt=tt, np_=np_, s=s):
                        for fl in range(8):
                            ins = e.matmul(ps[0:np_, :], HT[:, fl, tt * 128:tt * 128 + np_], WB[s][:, fl, :], start=(fl == 0), stop=(fl == 7))
                        return ins
                    P.op("pe", mm, r=["WB%d" % s] + ["HT%d_%d" % (fl, t0k) for fl in range(8)], w=[pk], cost=2.0)
                    dst = X1[0:np_, tt, cb * 512:(cb + 1) * 512]
                    P.op("dve", lambda e, dst=dst, ps=ps, np_=np_: e.tensor_tensor(dst, dst, ps[0:np_, :], ALU.add), r=[pk, "X1_%d" % tt], w=["X1_%d" % tt])
    for tt in range(9):
        np_ = 128 if tt < 8 else 16
        layer_norm(tt, np_, [])
        if tt < 8:
            P.dma("sp", y_own[tt * 128:(tt + 1) * 128, :], X1[:, tt, :], r=["X1_%d" % tt], sem="st_y%d" % tt)
        else:
            P.dma("sp", y_smp, X1[0:16, 8, :], r=["X1_8"], sem="st_y8")
    P.emit()
    return nc, P, AR


_CACHE = {}
DBG = False


def _consts(hf):
    f32 = np.float32
    t = np.arange(128)
    sq = np.zeros((128, 6, 128), f32)
    sq[:, 0] = np.eye(128)
    sq[:, 1] = (t[:, None] <= t[None, :])
    sq[:, 2] = np.where(t[None, :] <= t[:, None], 0.0, -1e30)
    sq[127, 3, :] = 1.0
    sq[:, 4] = 1.0
    g = np.array(G, np.float64)
    rmask = np.zeros((128, 4, 128), f32)
    for h in range(4):
        rmask[:, h, :] = np.where(t[None, :] >= t[:, None], g[h] ** (-(t[:, None] + 1.0)) / 16.0, 0.0)
    small = np.zeros((128, 48), f32)
    for h in range(4):
        small[:, h] = g[h] ** (t + 1.0)
        small[:, 4 + h] = (g[h] ** (t + 1.0)) ** 2
        small[:, 8 + h] = g[h] ** (127.0 - t) / 16.0
        for c in range(8):
            small[:, 12 + c * 4 + h] = g[h] ** (1023.0 - (c * 128 + t)) / 16.0
    small[:, 44] = float(hf)
    small[:, 45] = GN_EPS
    small[:, 46] = LN_EPS
    id16 = np.tile(np.eye(16, dtype=f32).reshape(1, 256), (128, 1))
    inv = 10000.0 ** (-(np.arange(128, dtype=np.float64) * 2.0) / 256.0)
    pos_own = np.concatenate([hf * 1024 + np.arange(1024), np.full(16, 16384)]).astype(np.float64)
    pos_pre = np.arange(1024).astype(np.float64)

    def cs(pos):
        ang = pos[None, :] * inv[:, None]
        return np.ascontiguousarray(np.stack([np.cos(ang), np.sin(ang)], axis=1).astype(f32))
    return dict(c_sq=sq, c_rmask=rmask, c_small=small, c_id16=id16, cs_own=cs(pos_own), cs_pre=cs(pos_pre))


def kernel(x_prompt, x_sample, state_ret, state_mlstm_C, state_mlstm_n, state_mlstm_m,
           p_prompt, p_sample, w_in, b_gate, ret_gn_w, mlstm_gn_w, w_out, ln1_g, ln1_b,
           w_ff1, w_ff2, w_pe, w_pe_gate, ln2_g, ln2_b):
    if "nc" not in _CACHE:
        _CACHE["nc"] = build_program(dbg=DBG)[0]
    nc = _CACHE["nc"]
    A = lambda a: np.ascontiguousarray(np.asarray(a, dtype=np.float32))
    shared = dict(
        w_in=A(w_in[0]), w_out=A(w_out[0]), w_ff1=A(w_ff1[0]), w_ff2=A(w_ff2[0]), w_pe=A(w_pe[0]),
        w_pe_gate=A(w_pe_gate[0]), ln_all=A(np.stack([ln1_g[0], ln1_b[0], ln2_g[0], ln2_b[0]])),
        b_gate=A(b_gate), gnw=A(np.concatenate([ret_gn_w[0], mlstm_gn_w[0]]).reshape(16, 128).T),
    )
    in_maps = []
    for c in range(NCORES):
        s, hf = c // 2, c % 2
        m = dict(shared)
        m.update(_consts(hf))
        m["x_own"] = A(x_prompt[s, hf * 1024:(hf + 1) * 1024])
        m["x_pre"] = A(x_prompt[s, 0:1024])
        m["x_smp"] = A(x_sample[c * 16:(c + 1) * 16, 0])
        m["p_own"] = A(p_prompt[0, s, hf * 1024:(hf + 1) * 1024])
        m["p_smp"] = A(p_sample[0, c * 16:(c + 1) * 16, 0])
        m["st_ret"] = A(state_ret[0, c * 16:(c + 1) * 16])
        m["st_C"] = A(state_mlstm_C[0, c * 16:(c + 1) * 16])
        m["st_n"] = A(state_mlstm_n[0, c * 16:(c + 1) * 16].reshape(16, 1024))
        m["st_m"] = A(state_mlstm_m[0, c * 16:(c + 1) * 16])
        in_maps.append(m)
    res = run_bass_kernel_spmd(nc, in_maps, core_ids=list(range(NCORES)))
    R = res.results
    _CACHE["R"] = R
    yp = np.zeros((4, 2048, D), np.float32)
    for c in range(NCORES):
        yp[c // 2, (c % 2) * 1024:(c % 2 + 1) * 1024] = R[c]["y_own"]
    ys = np.concatenate([R[c]["y_smp"] for c in range(NCORES)], 0).reshape(128, 1, D)
    odd = [1, 3, 5, 7]
    rS_p = np.stack([R[c]["retS_p"] for c in odd])[None]
    C_p = np.stack([R[c]["C_p"] for c in odd])[None]
    n_p = np.stack([R[c]["n_p"] for c in odd])[None]
    m_p = np.stack([R[c]["m_p"][0] for c in odd])[None]
    rS_s = np.concatenate([R[c]["retS_s"] for c in range(NCORES)], 0)[None]
    C_s = np.concatenate([R[c]["C_s"] for c in range(NCORES)], 0)[None]
    n_s = np.concatenate([R[c]["n_s"] for c in range(NCORES)], 0).reshape(1, 128, 4, 256)
    m_s = np.concatenate([R[c]["m_s"] for c in range(NCORES)], 0)[None]
    f = lambda a: np.ascontiguousarray(a, dtype=np.float32)
    return (f(yp), f(ys), f(rS_p), f(C_p), f(n_p), f(m_p), f(rS_s), f(C_s), f(n_s), f(m_s))
```
